# Optimizing a Trainium2 kernel written in Bass

```python
import jax, jax.numpy as jnp
from jax import lax
import numpy as np

D_MODEL = 1024
BATCH = 8
SEQ = 2048
DEPTH = 1
DEC_BATCH = 32
DEC_SEQ = 4
PAST_LEN = 16384
PAGE_SIZE = 128

A_WIDTH = D_MODEL // 2
A_GROUPS = 4
A_GROUP_DIM = A_WIDTH // A_GROUPS
CHUNK = 128
B_HEADS = 8
HEAD_DIM = 64
B_WIDTH = B_HEADS * HEAD_DIM
KV_HEADS = 2
GQA = B_HEADS // KV_HEADS
BLOCK = 64
N_SELECT = 16
WINDOW = 512
CMP_HIDDEN = HEAD_DIM
N_KV_SLOTS = 6
N_PAGED_SLOTS = 4
QBLOCK = 128
D_FF = ((8 * D_MODEL // 3) + 127) // 128 * 128
CONV_W = 3

EPS = 1e-6
NEG = -1e30
FORCE_SCORE = 1e4
PAD_POS = -(2 ** 30)
KV_COLS = N_KV_SLOTS * KV_HEADS * HEAD_DIM
IN_COLS = 2 * A_WIDTH + B_WIDTH + KV_COLS + 3 * B_HEADS + 2 * D_MODEL
SPLIT_IDX = (2 * A_WIDTH, 2 * A_WIDTH + B_WIDTH, 2 * A_WIDTH + B_WIDTH + KV_COLS,
             2 * A_WIDTH + B_WIDTH + KV_COLS + 3 * B_HEADS)

kernel_name = "hybrid_gmlp_nsa_convffn_adaln_step"


def rmsnorm(x, g):
    xf = x.astype(jnp.float32)
    y = xf * lax.rsqrt(jnp.mean(xf * xf, -1, keepdims=True) + EPS)
    return (y * g.astype(jnp.float32)).astype(x.dtype)


def layernorm(x, g, b):
    xf = x.astype(jnp.float32)
    xc = xf - jnp.mean(xf, -1, keepdims=True)
    y = xc * lax.rsqrt(jnp.mean(xc * xc, -1, keepdims=True) + EPS)
    return (y * g.astype(jnp.float32) + b.astype(jnp.float32)).astype(x.dtype)


def masked_softmax(s, mask):
    s = jnp.where(mask, s.astype(jnp.float32), NEG)
    m = jnp.max(s, -1, keepdims=True)
    e = jnp.where(mask, jnp.exp(s - m), 0.0)
    return e / jnp.maximum(jnp.sum(e, -1, keepdims=True), 1e-30)


def chunk_spatial_gating(u, vn, w_s, b_s):
    B, T, _ = u.shape
    tc = CHUNK if T % CHUNK == 0 else T
    nc = T // tc
    w = jnp.tril(w_s[:, :tc, :tc])
    vr = vn.reshape(B, nc, tc, A_GROUPS, A_GROUP_DIM)
    s = jnp.einsum('hts,bcshd->bcthd', w, vr) + b_s[:, :tc].T[None, None, :, :, None]
    return u * s.reshape(B, T, A_WIDTH)


def compress_blocks(rows, pe, w1, b1, w2, b2):
    x = rows + pe[None, None, :, None, :]
    h = jax.nn.gelu(jnp.einsum('bnikd,idh->bnkh', x, w1) + b1)
    return jnp.einsum('bnkh,hd->bnkd', h, w2) + b2


def nsa_attention(q, kv_rows, win_rows, win_pos, q_pos, gate_logits, cmp_pe, cmp_w1, cmp_b1, cmp_w2, cmp_b2):
    B, T = q.shape[:2]
    L = kv_rows.shape[1]
    nb = -(-L // BLOCK)
    rows = jnp.pad(kv_rows, ((0, 0), (0, nb * BLOCK - L), (0, 0), (0, 0), (0, 0)))
    rows = rows.reshape(B, nb, BLOCK, N_PAGED_SLOTS, KV_HEADS, HEAD_DIM)
    qg = q.reshape(B, T, KV_HEADS, GQA, HEAD_DIM) * (HEAD_DIM ** -0.5)
    blk = jnp.arange(nb, dtype=jnp.int32)

    kc = compress_blocks(rows[:, :, :, 0], cmp_pe[0], cmp_w1[0], cmp_b1[0], cmp_w2[0], cmp_b2[0])
    vc = compress_blocks(rows[:, :, :, 1], cmp_pe[1], cmp_w1[1], cmp_b1[1], cmp_w2[1], cmp_b2[1])
    avail = (blk[None, :] + 1) * BLOCK <= q_pos[:, None] + 1
    p_cmp = masked_softmax(jnp.einsum('btkgd,bnkd->btkgn', qg, kc), avail[None, :, None, None, :])
    o_cmp = jnp.einsum('btkgn,bnkd->btkgd', p_cmp.astype(vc.dtype), vc)

    cur = (q_pos // BLOCK)[:, None]
    imp = jnp.sum(p_cmp, axis=3)
    forced = (blk[None, :] == 0) | (blk[None, :] == cur) | (blk[None, :] == cur - 1)
    future = blk[None, :] > cur
    imp = jnp.where(forced[None, :, None, :], FORCE_SCORE, jnp.where(future[None, :, None, :], -1.0, imp))
    n_sel = min(N_SELECT, nb)
    _, sel_idx = lax.top_k(imp, n_sel)
    ks = jnp.moveaxis(rows[:, :, :, 2], 3, 1)
    vs = jnp.moveaxis(rows[:, :, :, 3], 3, 1)
    qb = QBLOCK if T % QBLOCK == 0 else T
    nq = T // qb
    head_ix = jnp.arange(KV_HEADS)[None, :, None]
    offs = jnp.arange(BLOCK, dtype=jnp.int32)

    def sel_block(args):
        qi, ii, pi, bi = args
        kg = ks[bi][head_ix, ii]
        vg = vs[bi][head_ix, ii]
        kpos = ii[..., None] * BLOCK + offs
        mask = (kpos <= pi[:, None, None, None]).reshape(qb, KV_HEADS, 1, n_sel * BLOCK)
        s = jnp.einsum('qkgd,qksid->qkgsi', qi, kg).reshape(qb, KV_HEADS, GQA, n_sel * BLOCK)
        p = masked_softmax(s, mask).reshape(qb, KV_HEADS, GQA, n_sel, BLOCK)
        return jnp.einsum('qkgsi,qksid->qkgd', p.astype(vg.dtype), vg)

    o_sel = lax.map(sel_block, (qg.reshape(B * nq, qb, KV_HEADS, GQA, HEAD_DIM),
                                sel_idx.reshape(B * nq, qb, KV_HEADS, n_sel),
                                jnp.tile(q_pos.reshape(nq, qb), (B, 1)),
                                jnp.repeat(jnp.arange(B, dtype=jnp.int32), nq)))
    o_sel = o_sel.reshape(B, T, KV_HEADS, GQA, HEAD_DIM)

    lw = win_rows.shape[1]
    q_off = lw - T
    span = WINDOW + qb
    wpad = jnp.pad(win_rows, ((0, 0), (WINDOW, 0), (0, 0), (0, 0), (0, 0)))
    ppad = jnp.concatenate([jnp.full((WINDOW,), PAD_POS, jnp.int32), win_pos])
    kidx = q_off + qb * jnp.arange(nq)[:, None] + jnp.arange(span)[None, :]
    kw = wpad[:, kidx]
    kpos = ppad[kidx]
    diff = q_pos.reshape(nq, qb)[:, :, None] - kpos[:, None, :]
    wmask = (diff >= 0) & (diff < WINDOW)
    s = jnp.einsum('bnqkgd,bnjkd->bnqkgj', qg.reshape(B, nq, qb, KV_HEADS, GQA, HEAD_DIM), kw[:, :, :, 0])
    p = masked_softmax(s, wmask[None, :, :, None, None, :])
    o_win = jnp.einsum('bnqkgj,bnjkd->bnqkgd', p.astype(kw.dtype), kw[:, :, :, 1]).reshape(B, T, KV_HEADS, GQA, HEAD_DIM)

    g = jax.nn.sigmoid(gate_logits.astype(jnp.float32)).reshape(B, T, 3, KV_HEADS, GQA, 1).astype(q.dtype)
    o = g[:, :, 0] * o_cmp + g[:, :, 1] * o_sel + g[:, :, 2] * o_win
    return o.reshape(B, T, B_WIDTH)


def decoder_layer(x, c, q_pos, past_rows, win_prev, win_prev_pos, conv_prev, params):
    (w_ada, b_ada, g_norm1, w_in, ln_v_g, ln_v_b, w_spatial, b_spatial,
     cmp_pe, cmp_w1, cmp_b1, cmp_w2, cmp_b2, w_branch_a, w_branch_b, w_out,
     g_norm2, w_up, w_conv, b_conv, w_down) = params
    B, T, _ = x.shape
    mod = jax.nn.silu(c) @ w_ada + b_ada
    shift1, scale1, gate1, shift2, scale2, gate2 = jnp.split(mod[:, None, :], 6, axis=-1)

    h = rmsnorm(x, g_norm1) * (1 + scale1) + shift1
    za, zq, zkv, zg, zm = jnp.split(h @ w_in, SPLIT_IDX, axis=-1)

    u, v = jnp.split(jax.nn.gelu(za), 2, axis=-1)
    vn = layernorm(v, ln_v_g, ln_v_b)
    ya = chunk_spatial_gating(u, vn, w_spatial, b_spatial)

    q = zq.reshape(B, T, B_HEADS, HEAD_DIM)
    kv_new = zkv.reshape(B, T, N_KV_SLOTS, KV_HEADS, HEAD_DIM)
    paged_new = kv_new[:, :, :N_PAGED_SLOTS]
    win_new = kv_new[:, :, N_PAGED_SLOTS:]
    if past_rows is None:
        kv_rows, win_rows, win_pos = paged_new, win_new, q_pos
    else:
        kv_rows = jnp.concatenate([past_rows, paged_new], axis=1)
        win_rows = jnp.concatenate([win_prev, win_new], axis=1)
        win_pos = jnp.concatenate([win_prev_pos, q_pos])
    yb = nsa_attention(q, kv_rows, win_rows, win_pos, q_pos, zg, cmp_pe, cmp_w1, cmp_b1, cmp_w2, cmp_b2)

    gate_a, gate_b = jnp.split(jax.nn.sigmoid(zm), 2, axis=-1)
    mixed = (gate_a * (ya @ w_branch_a) + gate_b * (yb @ w_branch_b)) @ w_out
    x = x + gate1 * mixed

    h2 = rmsnorm(x, g_norm2) * (1 + scale2) + shift2
    up = h2 @ w_up
    prev = jnp.zeros((B, CONV_W - 1, 2 * D_FF), up.dtype) if conv_prev is None else conv_prev
    upc = jnp.concatenate([prev, up], axis=1)
    conv = b_conv
    for k in range(CONV_W):
        conv = conv + w_conv[k] * upc[:, k:k + T]
    a, gv = jnp.split(conv, 2, axis=-1)
    x = x + gate2 * ((jax.nn.gelu(a) * gv) @ w_down)

    new_win = win_rows[:, -min(WINDOW, win_rows.shape[1]):]
    new_conv = upc[:, -(CONV_W - 1):]
    return x, paged_new, new_win, vn, new_conv


def setup_inputs(seed: int = 0) -> dict:
    key = jax.random.key(seed)
    ks = jax.random.split(key, 32)
    n_pages = PAST_LEN // PAGE_SIZE
    n_pool = (5 * DEC_BATCH * n_pages + 3) // 4
    lbuf = min(WINDOW, PAST_LEN)

    def nrm(k, shape, s):
        return s * jax.random.normal(k, shape, jnp.float32)

    page_table = jax.random.permutation(ks[5], n_pool)[:DEC_BATCH * n_pages].reshape(DEC_BATCH, n_pages).astype(jnp.int32)
    return {
        'x_prompt': nrm(ks[0], (BATCH, SEQ, D_MODEL), 1.0),
        'x_sample': nrm(ks[1], (DEC_BATCH, DEC_SEQ, D_MODEL), 1.0),
        'cache_kv': nrm(ks[2], (DEPTH, n_pool, PAGE_SIZE, N_PAGED_SLOTS, KV_HEADS, HEAD_DIM), 1.0),
        'state_kv_win': nrm(ks[3], (DEPTH, DEC_BATCH, lbuf, 2, KV_HEADS, HEAD_DIM), 1.0),
        'state_ffn_conv': nrm(ks[4], (DEPTH, DEC_BATCH, CONV_W - 1, 2 * D_FF), 1.0),
        'page_table': page_table,
        'c_prompt': nrm(ks[6], (BATCH, D_MODEL), 1.0),
        'c_sample': nrm(ks[7], (DEC_BATCH, D_MODEL), 1.0),
        'w_ada': nrm(ks[8], (DEPTH, D_MODEL, 6 * D_MODEL), 0.5 * D_MODEL ** -0.5),
        'b_ada': nrm(ks[9], (DEPTH, 6 * D_MODEL), 0.01),
        'g_norm1': 1.0 + nrm(ks[10], (DEPTH, D_MODEL), 0.02),
        'w_in': nrm(ks[11], (DEPTH, D_MODEL, IN_COLS), D_MODEL ** -0.5),
        'ln_v_g': 1.0 + nrm(ks[12], (DEPTH, A_WIDTH), 0.02),
        'ln_v_b': nrm(ks[13], (DEPTH, A_WIDTH), 0.02),
        'w_spatial': nrm(ks[14], (DEPTH, A_GROUPS, CHUNK, CHUNK), 0.5 * CHUNK ** -0.5),
        'b_spatial': 1.0 + nrm(ks[15], (DEPTH, A_GROUPS, CHUNK), 0.1),
        'cmp_pe': nrm(ks[16], (DEPTH, 2, BLOCK, HEAD_DIM), 0.1),
        'cmp_w1': nrm(ks[17], (DEPTH, 2, BLOCK, HEAD_DIM, CMP_HIDDEN), (BLOCK * HEAD_DIM) ** -0.5),
        'cmp_b1': nrm(ks[18], (DEPTH, 2, CMP_HIDDEN), 0.01),
        'cmp_w2': nrm(ks[19], (DEPTH, 2, CMP_HIDDEN, HEAD_DIM), CMP_HIDDEN ** -0.5),
        'cmp_b2': nrm(ks[20], (DEPTH, 2, HEAD_DIM), 0.01),
        'w_branch_a': nrm(ks[21], (DEPTH, A_WIDTH, D_MODEL), A_WIDTH ** -0.5),
        'w_branch_b': nrm(ks[22], (DEPTH, B_WIDTH, D_MODEL), B_WIDTH ** -0.5),
        'w_out': nrm(ks[23], (DEPTH, D_MODEL, D_MODEL), D_MODEL ** -0.5),
        'g_norm2': 1.0 + nrm(ks[24], (DEPTH, D_MODEL), 0.02),
        'w_up': nrm(ks[25], (DEPTH, D_MODEL, 2 * D_FF), D_MODEL ** -0.5),
        'w_conv': nrm(ks[26], (DEPTH, CONV_W, 2 * D_FF), CONV_W ** -0.5),
        'b_conv': nrm(ks[27], (DEPTH, 2 * D_FF), 0.01),
        'w_down': nrm(ks[28], (DEPTH, D_FF, D_MODEL), D_FF ** -0.5),
        'g_final': 1.0 + nrm(ks[29], (D_MODEL,), 0.02),
    }


def reference(x_prompt, x_sample, cache_kv, state_kv_win, state_ffn_conv, page_table, c_prompt, c_sample,
              w_ada, b_ada, g_norm1, w_in, ln_v_g, ln_v_b, w_spatial, b_spatial,
              cmp_pe, cmp_w1, cmp_b1, cmp_w2, cmp_b2, w_branch_a, w_branch_b, w_out,
              g_norm2, w_up, w_conv, b_conv, w_down, g_final):
    t_p = x_prompt.shape[1]
    t_s = x_sample.shape[1]
    dec_b = x_sample.shape[0]
    past_len = page_table.shape[1] * cache_kv.shape[2]
    lbuf = state_kv_win.shape[2]
    pos_p = jnp.arange(t_p, dtype=jnp.int32)
    pos_s = past_len + jnp.arange(t_s, dtype=jnp.int32)
    win_prev_pos = past_len - lbuf + jnp.arange(lbuf, dtype=jnp.int32)

    xp, xs = x_prompt, x_sample
    kv_p, kv_s, win_p, win_s, v_s, conv_p, conv_s = [], [], [], [], [], [], []
    for l in range(DEPTH):
        params = (w_ada[l], b_ada[l], g_norm1[l], w_in[l], ln_v_g[l], ln_v_b[l], w_spatial[l], b_spatial[l],
                  cmp_pe[l], cmp_w1[l], cmp_b1[l], cmp_w2[l], cmp_b2[l], w_branch_a[l], w_branch_b[l], w_out[l],
                  g_norm2[l], w_up[l], w_conv[l], b_conv[l], w_down[l])
        past = cache_kv[l][page_table].reshape(dec_b, past_len, N_PAGED_SLOTS, KV_HEADS, HEAD_DIM)
        xp, kvp, wp, _, cp = decoder_layer(xp, c_prompt, pos_p, None, None, None, None, params)
        xs, kvs, ws, vs, cs = decoder_layer(xs, c_sample, pos_s, past, state_kv_win[l], win_prev_pos,
                                            state_ffn_conv[l], params)
        kv_p.append(kvp); kv_s.append(kvs); win_p.append(wp); win_s.append(ws)
        v_s.append(vs); conv_p.append(cp); conv_s.append(cs)

    y_prompt = rmsnorm(xp, g_final)
    y_sample = rmsnorm(xs, g_final)
    return (y_prompt, y_sample, jnp.stack(kv_p), jnp.stack(kv_s), jnp.stack(win_p), jnp.stack(win_s),
            jnp.stack(v_s), jnp.stack(conv_p), jnp.stack(conv_s))
```

```python
import numpy as np
from contextlib import ExitStack
import concourse.bass as bass
import concourse.mybir as mybir
from concourse.bass_utils import run_bass_kernel_spmd

F32 = mybir.dt.float32
BF16 = mybir.dt.bfloat16
I32 = mybir.dt.int32
AF = mybir.ActivationFunctionType
ALU = mybir.AluOpType
AX = mybir.AxisListType

NCORES = 8
D = 1024
KC = 8
TP = 2048
NS = 4
TS = 4
NT = TP + NS * TS
IN_COLS = 4376
DFF = 2816
NPG = 128
EPS = 1e-6
NEGB = -30000.0
TBS = [(0, 512), (512, 512), (1024, 512), (1536, 512), (2048, 16)]
TTS = [(i * 128, 128) for i in range(16)] + [(TP + s * TS, TS) for s in range(NS)]


class Tl:
    __slots__ = ("name", "w", "r", "ps")

    def __init__(self, name="", ps=False):
        self.name = name
        self.w = None
        self.r = []
        self.ps = ps


class Sched:
    ENG = ("pe", "act", "dve", "pool", "sp")

    def __init__(self, nc, n_dma_sems=(32, 4, 24)):
        self.nc = nc
        self.sems = {}
        self._stack = []
        for e in self.ENG:
            self.sems[e] = self._sem("s_" + e)
        self.seq = {e: 0 for e in self.ENG}
        self.epoch = {e: 0 for e in self.ENG}
        self.cur = {e: e for e in self.ENG}
        self.known = {e: {} for e in self.ENG}
        self.lists = {e: [] for e in self.ENG}
        self.dpool = {}
        for q, n in zip(("sp", "act", "pool"), n_dma_sems):
            self.dpool[q] = dict(keys=[], cnt=[], nxt=0)
            for i in range(n):
                k = "d_%s_%d" % (q, i)
                self.sems[k] = self._sem(k)
                self.dpool[q]["keys"].append(k)
                self.dpool[q]["cnt"].append(0)
        self.out_events = []

    def _sem(self, name):
        cm = self.nc.semaphore(name)
        s = cm.__enter__()
        self._stack.append(cm)
        return s

    def close(self):
        for cm in reversed(self._stack):
            cm.__exit__(None, None, None)

    def _deps(self, e, reads, writes):
        need = {}
        for t in reads:
            if t.w is not None:
                k, v = t.w
                if need.get(k, 0) < v:
                    need[k] = v
            if t.ps:
                for (k, v) in t.r:
                    if k.split("#")[0] != e and need.get(k, 0) < v:
                        need[k] = v
        for t in writes:
            if t.w is not None:
                k, v = t.w
                if need.get(k, 0) < v:
                    need[k] = v
            for (k, v) in t.r:
                if need.get(k, 0) < v:
                    need[k] = v
        waits = []
        kn = self.known[e]
        for k, v in need.items():
            if e == "pe" and k.split("#")[0] == "pe":
                continue
            if kn.get(k, 0) >= v:
                continue
            kn[k] = v
            waits.append((k, v))
        return waits

    def _mark(self, ev, reads, writes):
        for t in reads:
            t.r.append(ev)
            if len(t.r) > 64:
                mx = {}
                for k, v in t.r:
                    if mx.get(k, 0) < v:
                        mx[k] = v
                t.r = list(mx.items())
        for t in writes:
            t.w = ev
            t.r = []

    def op(self, e, fn, reads=(), writes=()):
        waits = self._deps(e, reads, writes)
        if self.seq[e] >= 6000:
            self.epoch[e] += 1
            self.cur[e] = "%s#%d" % (e, self.epoch[e])
            self.sems[self.cur[e]] = self._sem("s_%s_%d" % (e, self.epoch[e]))
            self.seq[e] = 0
        self.seq[e] += 1
        ev = (self.cur[e], self.seq[e])
        self.lists[e].append(("op", waits, fn, self.cur[e]))
        self._mark(ev, reads, writes)
        return ev

    def dma(self, q, fn, reads=(), writes=(), is_output=False):
        waits = self._deps(q, reads, writes)
        p = self.dpool[q]
        i = p["nxt"]
        p["nxt"] = (i + 1) % len(p["keys"])
        k = p["keys"][i]
        prev = p["cnt"][i]
        if prev > 0 and self.known[q].get(k, 0) < prev:
            waits.append((k, prev))
            self.known[q][k] = prev
        p["cnt"][i] = prev + 16
        ev = (k, prev + 16)
        self.lists[q].append(("dma", waits, fn, k))
        self._mark(ev, reads, writes)
        if is_output:
            self.out_events.append(ev)
        return ev

    def wait_all_outputs(self, e="sp"):
        need = {}
        for k, v in self.out_events:
            if need.get(k, 0) < v:
                need[k] = v
        waits = [(k, v) for k, v in need.items() if self.known[e].get(k, 0) < v]
        for k, v in waits:
            self.known[e][k] = v
        self.lists[e].append(("wait", waits))
        self.out_events = []

    def flush(self):
        dw = []
        for q, p in self.dpool.items():
            for kk, cnt in zip(p["keys"], p["cnt"]):
                if cnt > 0 and self.known["sp"].get(kk, 0) < cnt:
                    dw.append((kk, cnt))
                    self.known["sp"][kk] = cnt
        self.lists["sp"].append(("wait", dw))
        lists = self.lists
        self.lists = {e: [] for e in self.ENG}
        sems = self.sems
        with self.nc.Block() as block:
            def mk(e):
                def body(eng):
                    for item in lists[e]:
                        for (k, v) in item[1]:
                            eng.wait_ge(sems[k], v)
                        if item[0] == "op":
                            item[2](eng).then_inc(sems[item[3]], 1)
                        elif item[0] == "dma":
                            item[2](eng).then_inc(sems[item[3]], 16)
                return body
            block.tensor(mk("pe"))
            block.scalar(mk("act"))
            block.vector(mk("dve"))
            block.gpsimd(mk("pool"))
            block.sync(mk("sp"))


class K:
    def __init__(self, nc):
        self.nc = nc
        self.S = Sched(nc)
        self.dram = {}
        self._evac = 0

    def din(self, name, shape, dt=F32):
        t = self.nc.dram_tensor(name, list(shape), dt, kind="ExternalInput")
        self.dram[name] = t
        return t.ap()

    def dout(self, name, shape, dt=F32):
        t = self.nc.dram_tensor(name, list(shape), dt, kind="ExternalOutput")
        self.dram[name] = t
        return t.ap()

    def mm(self, out, lhsT, rhs, start, stop, R, W, sgc=False):
        self.S.op("pe", lambda e: e.matmul(out, lhsT=lhsT, rhs=rhs, start=start, stop=stop, skip_group_check=sgc), R, W)

    def tr(self, out, in_, ident, R, W):
        self.S.op("pe", lambda e: e.transpose(out=out, in_=in_, identity=ident), R, W)

    def act(self, out, in_, func, R, W, bias=None, scale=None, accum_out=None):
        kw = {}
        if bias is not None:
            kw["bias"] = bias
        if scale is not None:
            kw["scale"] = scale
        if accum_out is not None:
            kw["accum_out"] = accum_out
        self.S.op("act", lambda e: e.activation(out=out, in_=in_, func=func, **kw), R, W)

    def tt(self, eng, out, in0, in1, op, R, W):
        self.S.op(eng, lambda e: e.tensor_tensor(out=out, in0=in0, in1=in1, op=op), R, W)

    def ts(self, eng, out, in0, s1, s2, op0, op1, R, W):
        if op1 is None:
            self.S.op(eng, lambda e: e.tensor_scalar(out=out, in0=in0, scalar1=s1, scalar2=None, op0=op0), R, W)
        else:
            self.S.op(eng, lambda e: e.tensor_scalar(out=out, in0=in0, scalar1=s1, scalar2=s2, op0=op0, op1=op1), R, W)

    def stt(self, out, in0, scalar, in1, op0, op1, R, W):
        self.S.op("dve", lambda e: e.scalar_tensor_tensor(out=out, in0=in0, scalar=scalar, in1=in1, op0=op0, op1=op1), R, W)

    def cp(self, eng, out, in_, R, W):
        if eng == "act":
            self.S.op("act", lambda e: e.copy(out=out, in_=in_), R, W)
        else:
            self.S.op(eng, lambda e: e.tensor_copy(out=out, in_=in_), R, W)

    def evac(self, out, in_, R, W):
        self._evac ^= 1
        self.cp("act" if self._evac else "dve", out, in_, R, W)

    def memset(self, eng, ap, val, W):
        self.S.op(eng, lambda e: e.memset(ap, val), (), W)

    def ld(self, out, in_, W, R=(), q="sp"):
        self.S.dma(q, lambda e: e.dma_start(out=out, in_=in_), R, W)

    def ldc(self, out, in_, W, R=()):
        self.S.dma("pool", lambda e: e.dma_start(out=out, in_=in_), R, W)

    def st(self, out, in_, R, q="sp"):
        self.S.dma(q, lambda e: e.dma_start(out=out, in_=in_), R, (), is_output=True)


import os
STOP = os.environ.get('KSTOP', '')
DCUT = int(os.environ.get('DCUT', '9'))


def build_program():
    nc = bass.Bass("TRN2", target_bir_lowering=False)
    k = K(nc)
    S = k.S
    xT_d = k.din("xT", [128, KC, NT])
    cT_d = k.din("cT", [128, KC, 5])
    wada_d = k.din("w_ada", [D, 6 * D])
    bada_d = k.din("b_adaT", [128, 48])
    g1_d = k.din("g1T", [128, KC])
    g2_d = k.din("g2T", [128, KC])
    gf_d = k.din("gfT", [128, KC])
    win_d = k.din("w_in", [D, IN_COLS])
    lng_d = k.din("ln_g_bc", [128, 512])
    lnb_d = k.din("ln_b_bc", [128, 512])
    ident_d = k.din("ident", [128, 128])
    pe_d = k.din("pe_bc", [128, 2, 2, 64])
    swin_d = k.din("state_win", [NS, 512, 256])

    wsT_d = k.din("wsT", [128, 4, 128])
    wss_d = k.din("wssT", [4, 4, 4])
    bs_d = k.din("bs", [1, 4, 128])
    b1r_d = k.din("b1r", [128, 2])
    b2r_d = k.din("b2r", [128, 2])
    w1_d = k.din("cmp_w1", [2, 64, 64, 64])
    w2_d = k.din("cmp_w2", [2, 64, 64])
    msk_d = k.din("msk", [128, 4, 16, 32])
    E_d = k.din("Emat", [64, 2048])
    Cm_d = k.din("Cm", [128, 2, 128])
    wba_d = k.din("w_branch_a", [512, D])
    wbb_d = k.din("w_branch_b", [512, D])
    wout_d = k.din("w_out", [D, D])
    wup_d = k.din("w_up", [D, 2 * DFF])
    wdn_d = k.din("w_down", [DFF, D])
    wcv_d = k.din("w_convT", [128, 44, 3])
    bcv_d = k.din("b_convT", [128, 44])
    sprev_d = k.din("sprevT", [128, 44, 4, 2])
    cache_d = k.din("cache2d", [5120 * 128, 512]) if not STOP else None
    pt_d = k.din("pt_bc", [NS, 128, 128], I32)
    Gm_d = k.din("Gm", [64, 64]); SelT_d = k.din("SelT", [4, 64]); Hsel_d = k.din("Hsel", [64, 8])
    CN_d = k.din("CN", [64, 4]); CW_d = k.din("CW", [64, 4])
    b2k_d = k.din("b2k", [128, 1]); b2v_d = k.din("b2v", [128, 128])
    yT_o = k.dout("yT", [128, KC, NT])
    conv_o = k.dout("convT", [128, 440])
    kv_o = k.dout("kv_tok", [NT, 512])
    winp_o = k.dout("win_p", [512, 256])
    wins_o = k.dout("win_s", [NS, 512, 256])
    vch_o = k.dout("vchunk", [NS * TS, 512])

    with ExitStack() as top:
        def sb(name, shape, dt, es=top):
            return es.enter_context(nc.sbuf_tensor("sb_" + name, list(shape), dt))

        PS = [top.enter_context(nc.psum_tensor("ps%d" % i, [128, 512], F32)) for i in range(8)]
        PT = [Tl("ps%d" % i, ps=True) for i in range(8)]
        psi = [0]

        def nextps():
            i = psi[0]
            psi[0] = (i + 1) % 8
            return PS[i], PT[i]

        ident_f = sb("ident_f", [128, 128], F32); T_identf = Tl()
        ident_b = sb("ident_b", [128, 128], BF16); T_identb = Tl()
        ones_b = sb("ones_b", [128, 128], BF16); T_ones = Tl()
        modT = sb("modT", [128, 48, 5], F32); T_mod = Tl()
        A1 = sb("A1", [128, KC, 5], F32); T_A1 = Tl()
        A2 = sb("A2", [128, KC, 5], F32); T_A2 = Tl()
        hT = sb("hT", [128, KC, NT], BF16)
        T_h = [[Tl() for _ in TBS] for _ in range(KC)]
        x1raw = sb("x1raw", [128, KC * NT], F32)
        mid_scope = top.enter_context(ExitStack())
        uT = sb("uT", [128, 4, NT], BF16, mid_scope)
        T_u = [[Tl() for _ in TBS] for _ in range(4)]
        x1T = x1raw[:, :].rearrange("p (k t) -> p k t", k=KC)
        xbv = x1raw[:, :].bitcast(BF16)
        T_x1 = [[Tl() for _ in TBS] for _ in range(KC)]

        k.ld(ident_f[:], ident_d[:, :], [T_identf])
        k.cp("dve", ident_b[:], ident_f[:], [T_identf], [T_identb])
        k.memset("pool", ones_b[:], 1.0, [T_ones])

        with ExitStack() as pa:
            cs = sb("cs", [128, KC, 5], F32, pa); T_cs = Tl()
            bT = sb("bT", [128, 48], F32, pa); T_bT = Tl()
            g1 = sb("g1", [128, KC], F32, pa); T_g1 = Tl()
            g2 = sb("g2", [128, KC], F32, pa); T_g2 = Tl()
            wab = [sb("wab%d" % i, [128, KC, 512], F32, pa) for i in range(2)]
            T_wab = [Tl(), Tl()]
            xa = x1T; T_xa = [Tl() for _ in range(KC)]
            sq = [sb("sq%d" % i, [128, NT], BF16, pa) for i in range(2)]; T_sq = [Tl(), Tl()]
            rstd = sb("rstd", [128, NT], F32, pa); T_rstd = [Tl() for _ in TBS]
            tmp = [sb("tmpA%d" % i, [128, NT], F32, pa) for i in range(2)]; T_tmp = [Tl(), Tl()]

            k.ld(cs[:], cT_d[:, :, :], [T_cs])
            k.ld(bT[:], bada_d[:, :], [T_bT])
            k.ld(g1[:], g1_d[:, :], [T_g1])
            k.ld(g2[:], g2_d[:, :], [T_g2])
            k.act(cs[:], cs[:], AF.Silu, [T_cs], [T_cs])
            for kc in range(KC):
                k.ld(xa[:, kc, :], xT_d[:, kc, :], [T_xa[kc]])
            psm, T_psm = PS[7], PT[7]
            wv = wada_d.rearrange("(kc p) n -> p kc n", p=128)
            for jb in range(12):
                w = wab[jb % 2]; Tw = T_wab[jb % 2]
                for kc in range(KC):
                    k.ld(w[:, kc, :], wv[:, kc, jb * 512:(jb + 1) * 512], [Tw], q="sp")
                for j in range(4):
                    col = (jb * 4 + j) * 5
                    for kc in range(KC):
                        k.mm(psm[:, col:col + 5], w[:, kc, j * 128:(j + 1) * 128], cs[:, kc, :],
                             kc == 0, kc == KC - 1, [Tw, T_cs], [T_psm])
            k.tt("dve", modT[:], psm[:, 0:240].rearrange("p (j r) -> p j r", r=5),
                 bT[:, :].unsqueeze(2).to_broadcast([128, 48, 5]), ALU.add, [T_psm, T_bT], [T_mod])
            for (Ax, TAx, gx, Tgx, off) in ((A1, T_A1, g1, T_g1, 8), (A2, T_A2, g2, T_g2, 32)):
                k.ts("dve", Ax[:], modT[:, off:off + 8, :], 1.0, None, ALU.add, None, [T_mod], [TAx])
                k.tt("dve", Ax[:], Ax[:], gx[:, :].unsqueeze(2).to_broadcast([128, KC, 5]), ALU.mult, [TAx, Tgx], [TAx])
            for kc in range(KC):
                s_ = sq[kc % 2]; Ts = T_sq[kc % 2]
                k.act(s_[:], xa[:, kc, :], AF.Square, [T_xa[kc]], [Ts])
                for bi, (t0, n) in enumerate(TBS):
                    k.mm(PS[bi][:, 0:n], ones_b[:], s_[:, t0:t0 + n], kc == 0, kc == KC - 1, [T_ones, Ts], [PT[bi]])
            for bi, (t0, n) in enumerate(TBS):
                k.act(rstd[:, t0:t0 + n], PS[bi][:, 0:n], AF.Sqrt, [PT[bi]], [T_rstd[bi]], bias=EPS, scale=1.0 / D)
                k.S.op("dve", (lambda o: (lambda e: e.reciprocal(out=o, in_=o)))(rstd[:, t0:t0 + n]), [T_rstd[bi]], [T_rstd[bi]])
            for kc in range(KC):
                t_ = tmp[kc % 2]; Tt = T_tmp[kc % 2]
                k.tt("dve", t_[:], xa[:, kc, :], rstd[:], ALU.mult, [T_xa[kc]] + T_rstd, [Tt])
                k.act(hT[:, kc, 0:TP], t_[:, 0:TP], AF.Identity, [Tt, T_A1, T_mod], T_h[kc][0:4],
                      bias=modT[:, kc, 0:1], scale=A1[:, kc, 0:1])
                for s in range(NS):
                    c0 = TP + s * TS
                    k.act(hT[:, kc, c0:c0 + TS], t_[:, c0:c0 + TS], AF.Identity, [Tt, T_A1, T_mod], [T_h[kc][4]],
                          bias=modT[:, kc, 1 + s:2 + s], scale=A1[:, kc, 1 + s:2 + s])
            S.flush()

        with ExitStack() as pb_:
            att = pb_
            vn = xbv[:, 0:10240].rearrange("p (t c) -> p t c", t=20); T_vn = [Tl() for _ in TTS]
            gates = sb("gates", [128, 20, 24], F32, att); T_gates = [Tl() for _ in TTS]
            pe_bc = sb("pe_bc_sb", [128, 2, 2, 64], F32, att); T_pe = Tl()
            qTs = sb("qTs", [128, 4, 16], BF16, att); T_qTs = Tl()
            kTs = sb("kTs", [128, 4, 16], BF16, att); T_kTs = Tl()
            vnew = sb("vnew", [4, NS, 2, 130], BF16, att); T_vnew = Tl()
            pr = att.enter_context(ExitStack())
            qT = sb("qT", [128, 4, NT], BF16, pr); T_q = [[Tl() for _ in TBS] for _ in range(4)]
            kT = sb("kT", [128, 4, NT], BF16, pr); T_k = [[Tl() for _ in TBS] for _ in range(4)]
            vsel = sb("vsel", [128, 20, 2, 65], BF16, pr); T_vsel = [Tl() for _ in TTS]
            vwin = sb("vwin", [128, 20, 2, 65], BF16, pr); T_vwin = [Tl() for _ in TTS]
            Xc = xbv[:, 10240:14336].rearrange("p (t s c) -> p t s c", t=16, s=2); T_Xc = [Tl() for _ in range(16)]
            lng = sb("lng", [128, 512], F32, pr); T_lng = Tl()
            lnb = sb("lnb", [128, 512], F32, pr); T_lnb = Tl()
            k.ld(pe_bc[:], pe_d[:, :, :, :], [T_pe])
            k.ld(lng[:], lng_d[:, :], [T_lng])
            k.ld(lnb[:], lnb_d[:, :], [T_lnb])
            k.memset("pool", vsel[:], 1.0, T_vsel)
            k.memset("pool", vwin[:], 1.0, T_vwin)

            with ExitStack() as pb:
                wu = sb("wu", [128, KC, 512], BF16, pb); T_wu = Tl()
                wq = sb("wq", [128, KC, 512], BF16, pb); T_wq = Tl()
                wvv = wq; T_wv = T_wq
                wkd = wu[:, :, :].rearrange("p k (j u d) -> p k j u d", j=4, u=2); T_wkd = T_wu
                wkv = sb("wkv", [128, KC, 792], BF16, pb); T_wkv = Tl()
                vg = [sb("vg%d" % i, [128, 512], F32, pb) for i in range(2)]; T_vg = [Tl(), Tl()]
                vt = [sb("vt%d" % i, [128, 512], F32, pb) for i in range(2)]; T_vt = [Tl(), Tl()]
                st6 = [sb("st6%d" % i, [128, 8], F32, pb) for i in range(2)]; T_st6 = [Tl(), Tl()]
                mv = [sb("mv%d" % i, [128, 4], F32, pb) for i in range(2)]; T_mv = [Tl(), Tl()]
                kvo = [sb("kvo0", [128, 768], F32, pb)] * 2; T_kvo = [Tl()] * 2

                wi = win_d.rearrange("(kc p) n -> p kc n", p=128)
                k.ldc(wu[:], wi[:, :, 0:512], [T_wu])
                k.ldc(wq[:], wi[:, :, 1024:1536], [T_wq])
                k.ldc(wkv[:], wi[:, :, 1536:2328], [T_wkv])

                def fm_proj(wt, Tw, nch, wsl, evac):
                    for m in range(nch):
                        for bi, (t0, n) in enumerate(TBS):
                            ps, Tp = nextps()
                            for kc in range(KC):
                                k.mm(ps[:, 0:n], wsl(wt, kc, m), hT[:, kc, t0:t0 + n], kc == 0, kc == KC - 1,
                                     [Tw, T_h[kc][bi]], [Tp])
                            evac(m, bi, t0, n, ps, Tp)

                fm_proj(wu, T_wu, 4, lambda wt, kc, m: wt[:, kc, m * 128:(m + 1) * 128],
                        lambda m, bi, t0, n, ps, Tp: k.act(uT[:, m, t0:t0 + n], ps[:, 0:n], AF.Gelu_apprx_tanh, [Tp], [T_u[m][bi]]))
                fm_proj(wq, T_wq, 4, lambda wt, kc, m: wt[:, kc, m * 128:(m + 1) * 128],
                        lambda m, bi, t0, n, ps, Tp: k.act(qT[:, m, t0:t0 + n], ps[:, 0:n], AF.Copy, [Tp], [T_q[m][bi]], scale=0.125))
                for j, (slot, kvh) in enumerate(((2, 0), (2, 1), (4, 0), (4, 1))):
                    c0 = 1536 + slot * 128 + kvh * 64
                    for dup in range(2):
                        k.ldc(wkd[:, :, j, dup, :], wi[:, :, c0:c0 + 64], [T_wkd])
                fm_proj(wkd, T_wkd, 4, lambda wt, kc, m: wt[:, kc, m, :, :],
                        lambda m, bi, t0, n, ps, Tp: k.evac(kT[:, m, t0:t0 + n], ps[:, 0:n], [Tp], [T_k[m][bi]]))

                k.ldc(wvv[:], wi[:, :, 512:1024], [T_wv])
                def tb_of(t0):
                    return min(t0 // 512, 4)

                for ti, (t0, n) in enumerate(TTS):
                    bi = tb_of(t0)
                    ps, Tp = nextps()
                    for kc in range(KC):
                        k.mm(ps[0:n, :], hT[:, kc, t0:t0 + n], wvv[:, kc, :], kc == 0, kc == KC - 1, [T_wv, T_h[kc][bi]], [Tp])
                    g_ = vg[ti % 2]; Tg = T_vg[ti % 2]
                    t_ = vt[ti % 2]; Tt = T_vt[ti % 2]
                    s6 = st6[ti % 2]; Ts6 = T_st6[ti % 2]
                    m_ = mv[ti % 2]; Tm = T_mv[ti % 2]
                    k.act(g_[0:n, :], ps[0:n, :], AF.Gelu_apprx_tanh, [Tp], [Tg])
                    S.op("dve", (lambda o, i: (lambda e: e.bn_stats(out=o, in_=i)))(s6[0:n, 0:6], g_[0:n, :]), [Tg], [Ts6])
                    S.op("dve", (lambda o, i: (lambda e: e.bn_aggr(out=o, in_=i)))(m_[0:n, 0:2], s6[0:n, 0:6]), [Ts6], [Tm])
                    k.act(m_[0:n, 2:3], m_[0:n, 1:2], AF.Sqrt, [Tm], [Tm], bias=EPS, scale=1.0)
                    S.op("dve", (lambda o, i: (lambda e: e.reciprocal(out=o, in_=i)))(m_[0:n, 3:4], m_[0:n, 2:3]), [Tm], [Tm])
                    k.ts("dve", t_[0:n, :], g_[0:n, :], m_[0:n, 0:1], m_[0:n, 3:4], ALU.subtract, ALU.mult, [Tg, Tm], [Tt])
                    k.tt("dve", t_[0:n, :], t_[0:n, :], lng[0:n, :], ALU.mult, [Tt, T_lng], [Tt])
                    k.tt("dve", t_[0:n, :], t_[0:n, :], lnb[0:n, :], ALU.add, [Tt, T_lnb], [Tt])
                    k.cp("pool", vn[0:n, ti, :], t_[0:n, :], [Tt], [T_vn[ti]])
                    if ti >= 16:
                        s = ti - 16
                        k.st(vch_o[s * TS:(s + 1) * TS, :], t_[0:n, :], [Tt])
                    psa, Tpa = nextps()
                    psb, Tpb = nextps()
                    for kc in range(KC):
                        k.mm(psa[0:n, :], hT[:, kc, t0:t0 + n], wkv[:, kc, 0:512], kc == 0, kc == KC - 1, [T_wkv, T_h[kc][bi]], [Tpa])
                    for kc in range(KC):
                        k.mm(psb[0:n, 0:280], hT[:, kc, t0:t0 + n], wkv[:, kc, 512:792], kc == 0, kc == KC - 1, [T_wkv, T_h[kc][bi]], [Tpb])
                    o_ = kvo[ti % 2]; To = T_kvo[ti % 2]
                    k.cp("act", o_[0:n, 0:512], psa[0:n, :], [Tpa], [To])
                    k.cp("act", o_[0:n, 512:768], psb[0:n, 0:256], [Tpb], [To])
                    k.st(kv_o[t0:t0 + n, :], o_[0:n, 0:512], [To])
                    if ti < 16:
                        k.tt("dve", Xc[:, ti, :, :].rearrange("p s (k d) -> p s k d", k=2),
                             psa[:, 0:256].rearrange("p (s k d) -> p s k d", s=2, k=2), pe_bc[:], ALU.add, [Tpa, T_pe], [T_Xc[ti]])
                    k.cp("dve", vsel[0:n, ti, :, 0:64], psa[0:n, 384:512].rearrange("p (k d) -> p k d", k=2), [Tpa], [T_vsel[ti]])
                    k.cp("dve", vwin[0:n, ti, :, 0:64], psb[0:n, 128:256].rearrange("p (k d) -> p k d", k=2), [Tpb], [T_vwin[ti]])
                    k.act(gates[0:n, ti, :], psb[0:n, 256:280], AF.Sigmoid, [Tpb], [T_gates[ti]])
                    if 12 <= ti < 16:
                        r0 = (ti - 12) * 128
                        k.st(winp_o[r0:r0 + 128, :], o_[:, 512:768], [To])
                    if ti >= 16:
                        s = ti - 16
                        k.st(wins_o[s, 512 - TS:512, :], o_[0:n, 512:768], [To])
                for s in range(NS):
                    k.S.dma("sp", (lambda s: (lambda e: e.dma_start(out=wins_o[s, 0:512 - TS, :], in_=swin_d[s, TS:512, :])))(s), (), (), is_output=True)
                S.flush()

            kcT = sb("kcT", [128, 2, 16, 2], BF16, pr); T_kc = Tl()
            vcT = sb("vcT", [128, 2, 16, 2], BF16, pr); T_vcT = Tl()
            vc = sb("vc", [32, 2, 64], BF16, pr); T_vc = Tl()
            PSb = [PS[i][:, :].bitcast(BF16) for i in range(8)]

            def tb_of(t0):
                return min(t0 // 512, 4)

            with ExitStack() as pc:
                Wd = xbv[:, 14336:14336 + 16384].rearrange("p (s d m) -> p s d m", s=2, d=64); T_Wd = Tl()
                wsT_f = sb("wsT_f", [128, 4, 128], F32, pc); T_wsf = Tl()
                wsT = sb("wsT_b", [128, 4, 128], BF16, pc); T_ws = Tl()
                wss_f = sb("wss_f", [4, 4, 4], F32, pc); T_wssf = Tl()
                wss = sb("wss", [4, 4, 4], BF16, pc); T_wss = Tl()
                bs_f = sb("bs_f", [1, 4, 128], F32, pc); T_bs = Tl()
                ones_f = sb("ones_f", [1, 128], F32, pc); T_onesf = Tl()
                W2d = sb("W2d", [128, 2, 2, 128], BF16, pc); T_W2d = Tl()
                b1r = sb("b1r", [128, 2], F32, pc); b2r = sb("b2r", [128, 2], F32, pc); T_b12 = Tl()
                hidT = sb("hidT", [128, 2, 32], BF16, pc); T_hid = Tl()

                k.ld(wsT_f[:], wsT_d[:, :, :], [T_wsf])
                k.ld(wss_f[:], wss_d[:, :, :], [T_wssf])
                k.ld(bs_f[:], bs_d[:, :, :], [T_bs])
                k.ld(b1r[:], b1r_d[:, :], [T_b12])
                k.ld(b2r[:], b2r_d[:, :], [T_b12])
                k.memset("dve", ones_f[:], 1.0, [T_onesf])
                S.op("pool", lambda e: e.affine_select(out=wsT[:], in_=wsT_f[:], pattern=[[0, 4], [1, 128]], compare_op=ALU.is_ge,
                                                       fill=0.0, base=0, channel_multiplier=-1), [T_wsf], [T_ws])
                S.op("pool", lambda e: e.affine_select(out=wss[:], in_=wss_f[:], pattern=[[0, 4], [1, 4]], compare_op=ALU.is_ge,
                                                       fill=0.0, base=0, channel_multiplier=-1), [T_wssf], [T_wss])
                k.memset("pool", Wd, 0.0, [T_Wd])
                k.memset("pool", W2d[:], 0.0, [T_W2d])
                for sl in range(2):
                    for half in range(2):
                        k.ldc(Wd[half * 64:(half + 1) * 64, sl, :, half * 64:(half + 1) * 64], w1_d[sl, :, :, :], [T_Wd])
                    for parity in range(2):
                        for dup in range(2):
                            k.ldc(W2d[parity * 64:(parity + 1) * 64, sl, parity, dup * 64:(dup + 1) * 64], w2_d[sl, :, :], [T_W2d])

                for ti, (t0, n) in enumerate(TTS):
                    bi = tb_of(t0)
                    ps, Tp = nextps()
                    for g in range(4):
                        if ti < 16:
                            k.mm(ps[:, g * 128:(g + 1) * 128], vn[:, ti, g * 128:(g + 1) * 128], wsT[:, g, :], True, False, [T_vn[ti], T_ws], [Tp])
                            k.mm(ps[:, g * 128:(g + 1) * 128], ones_f[0:1, :], bs_f[0:1, g, :], False, True, [T_onesf, T_bs], [Tp])
                        else:
                            k.mm(ps[:, g * 128:g * 128 + 4], vn[0:4, ti, g * 128:(g + 1) * 128], wss[0:4, g, :], True, False, [T_vn[ti], T_wss], [Tp])
                            k.mm(ps[:, g * 128:g * 128 + 4], ones_f[0:1, :], bs_f[0:1, g, 0:4], False, True, [T_onesf, T_bs], [Tp])
                    uv = uT[:, :, t0:t0 + n]
                    Tus = [T_u[g][bi] for g in range(4)]
                    k.tt("dve", uv, uv, ps[:, :].rearrange("p (g t) -> p g t", g=4)[:, :, 0:n], ALU.mult, [Tp] + Tus, Tus)

                Xc5 = Xc.rearrange("p t s (k d) -> p t s k d", k=2)
                for sl in range(2):
                    ps, Tp = nextps()
                    for d in range(64):
                        k.mm(ps[:, 0:32].rearrange("p (t k) -> p t k", k=2), Wd[:, sl, d, :], Xc5[:, :, sl, :, d], d == 0, d == 63,
                             [T_Wd] + T_Xc, [Tp])
                    k.act(hidT[:, sl, :], ps[:, 0:32], AF.Gelu_apprx_tanh, [Tp, T_b12], [T_hid], bias=b1r[:, sl:sl + 1])
                    for parity in range(2):
                        ps2, Tp2 = nextps()
                        k.mm(ps2[:, 0:32], W2d[:, sl, parity, :], hidT[:, sl, :], True, True, [T_W2d, T_hid], [Tp2])
                        dst = (kcT if sl == 0 else vcT)
                        Td = T_kc if sl == 0 else T_vcT
                        k.act(dst[:, :, :, parity], ps2[:, 0:32].rearrange("p (t k) -> p k t", k=2), AF.Identity, [Tp2, T_b12], [Td],
                              bias=b2r[:, sl:sl + 1])
                for kvh in range(2):
                    pi_ = psi[0]
                    ps, Tp = nextps()
                    k.tr(PSb[pi_][0:32, 0:64], vcT[0:64, kvh, :, :].rearrange("p t q -> p (t q)"), ident_b[0:64, 0:64], [T_vcT, T_identb], [Tp])
                    k.cp("dve", vc[:, kvh, :], PSb[pi_][0:32, 0:64], [Tp], [T_vc])
                S.flush()
                if STOP == 'C':
                    S.wait_all_outputs("sp"); S.flush(); S.close(); return nc

            with ExitStack() as pd:
                ybT = xbv[:, 0:4 * NT].rearrange("p (c t) -> p c t", c=4); T_yb = [Tl() for _ in TBS]
                ybacc = sb("ybacc", [128, 4, 512], F32, pd); T_ya = [Tl() for _ in range(4)]
                msk = sb("msk", [128, 4, 16, 32], F32, pd); T_msk = Tl()
                Eb = sb("Eb", [64, 2048], BF16, pd); T_E = Tl()
                Cm = sb("Cm", [128, 2, 128], BF16, pd); T_C = Tl()
                penT = sb("penT", [64, 512], BF16, pd); T_penT = Tl()
                PTb = [sb("PTb%d" % i, [128, 512], BF16, pd) for i in range(4)]; T_PT = [Tl() for _ in range(4)]
                sm = sb("sm", [128, 8, 32], F32, pd); T_sm = Tl()
                ee = sb("ee", [128, 8, 32], F32, pd); T_ee = Tl()
                pp = sb("pp", [128, 8, 32], F32, pd); T_pp = Tl()
                pbf = sb("pbf", [128, 8, 32], BF16, pd); T_pbf = Tl()
                pT = sb("pT", [32, 8, 128], BF16, pd); T_pT = Tl()
                imp = sb("imp", [128, 2, 32], F32, pd); T_imp = Tl()
                t8 = sb("t8", [128, 16], F32, pd); T_t8 = Tl()
                wk8 = sb("wk8", [128, 32], F32, pd); T_wk8 = Tl()
                penf = sb("penf", [128, 2, 32], F32, pd); T_penf = Tl()
                pen = sb("pen", [128, 2, 32], BF16, pd); T_pen = Tl()
                mx = sb("mx", [128, 8], F32, pd); T_mx = Tl()
                rs = sb("rs", [128, 8], F32, pd); T_rs = Tl()
                rs4 = sb("rs4", [128, 4], F32, pd); T_rs4 = Tl()
                tmpo = sb("tmpo", [128, 4, 64], F32, pd); T_tmpo = Tl()
                ybb = sb("ybb", [128, 512], BF16, pd); T_ybb = Tl()

                k.ld(msk[:], msk_d[:, :, :, :], [T_msk])
                k.ldc(Eb[:], E_d[:, :], [T_E])
                k.ldc(Cm[:], Cm_d[:, :, :], [T_C])

                rot = [4]

                def rps():
                    i = rot[0]
                    rot[0] = 4 + (i - 3) % 4
                    return i

                def B3(ap, sh):
                    return ap.to_broadcast(sh)

                for qb in range(4):
                    for j in range(4):
                        qt = 4 * qb + j
                        if DCUT < 1:
                            continue
                        piA = rps(); piB = rps()
                        for h in range(8):
                            base = (h % 2) * 64; ch = h // 2; kvh = h // 4
                            bk = piA if h % 2 == 0 else piB
                            k.mm(PS[bk][:, (h // 2) * 32:(h // 2 + 1) * 32], qT[base:base + 64, ch, qt * 128:(qt + 1) * 128],
                                 kcT[base:base + 64, kvh, :, :].rearrange("p t q -> p (t q)"), True, True, [T_q[ch][qb], T_kc], [PT[bk]])
                        for par, bk in ((0, piA), (1, piB)):
                            k.tt("dve", sm[:, par::2, :], PS[bk][:, 0:128].rearrange("p (h n) -> p h n", h=4),
                                 B3(msk[:, 0, qt, :].unsqueeze(1), [128, 4, 32]), ALU.add, [PT[bk], T_msk], [T_sm])
                        S.op("dve", lambda e: e.tensor_reduce(out=mx[:, 0:8], in_=sm[:], axis=AX.X, op=ALU.max), [T_sm], [T_mx])
                        k.tt("dve", sm[:], sm[:], B3(mx[:, 0:8].unsqueeze(2), [128, 8, 32]), ALU.subtract, [T_sm, T_mx], [T_sm])
                        k.act(ee[:], sm[:], AF.Exp, [T_sm], [T_ee])
                        k.tt("dve", ee[:], ee[:], B3(msk[:, 1, qt, :].unsqueeze(1), [128, 8, 32]), ALU.mult, [T_ee, T_msk], [T_ee])
                        S.op("dve", lambda e: e.tensor_reduce(out=rs[:, 0:8], in_=ee[:], axis=AX.X, op=ALU.add), [T_ee], [T_rs])
                        k.ts("dve", rs[:], rs[:], 1e-30, None, ALU.max, None, [T_rs], [T_rs])
                        S.op("dve", lambda e: e.reciprocal(out=rs[:], in_=rs[:]), [T_rs], [T_rs])
                        k.tt("dve", pp[:], ee[:], B3(rs[:, 0:8].unsqueeze(2), [128, 8, 32]), ALU.mult, [T_ee, T_rs], [T_pp])
                        k.cp("pool", pbf[:], pp[:], [T_pp], [T_pbf])
                        if qb >= 2 and 'NOTOPK' not in STOP:
                            S.op("dve", lambda e: e.tensor_reduce(out=imp[:], in_=pp[:].rearrange("p (k g) n -> p k n g", k=2), axis=AX.X, op=ALU.add),
                                 [T_pp], [T_imp])
                            k.tt("dve", imp[:], imp[:], B3(msk[:, 2, qt, :].unsqueeze(1), [128, 2, 32]), ALU.mult, [T_imp, T_msk], [T_imp])
                            k.tt("dve", imp[:], imp[:], B3(msk[:, 3, qt, :].unsqueeze(1), [128, 2, 32]), ALU.add, [T_imp, T_msk], [T_imp])
                            for kvh in range(2):
                                S.op("dve", (lambda kvh: lambda e: e.max(out=t8[:, 0:8], in_=imp[:, kvh, :]))(kvh), [T_imp], [T_t8])
                                S.op("dve", (lambda kvh: lambda e: e.match_replace(out=wk8[:], in_to_replace=t8[:, 0:8], in_values=imp[:, kvh, :], imm_value=-1e30))(kvh),
                                     [T_imp, T_t8], [T_wk8])
                                S.op("dve", lambda e: e.max(out=t8[:, 8:16], in_=wk8[:]), [T_wk8], [T_t8])
                                k.ts("dve", penf[:, kvh, :], imp[:, kvh, :], t8[:, 15:16], -NEGB, ALU.is_ge, ALU.mult, [T_imp, T_t8], [T_penf])
                            k.ts("dve", pen[:], penf[:], NEGB, None, ALU.add, None, [T_penf], [T_pen])
                            pi2 = rps()
                            k.tr(PSb[pi2][0:64, 0:128], pen[:].rearrange("p k n -> p (k n)"), ident_b[:], [T_pen, T_identb], [PT[pi2]])
                            k.cp("act", penT[:, j * 128:(j + 1) * 128], PSb[pi2][0:64, 0:128], [PT[pi2]], [T_penT])
                        if DCUT < 2:
                            continue
                        pi3 = rps()
                        for h in range(8):
                            k.tr(PSb[pi3][0:32, h * 128:(h + 1) * 128], pbf[:, h, :], ident_b[:], [T_pbf, T_identb], [PT[pi3]])
                        k.cp("act", pT[:].rearrange("p h q -> p (h q)"), PSb[pi3][0:32, 0:1024], [PT[pi3]], [T_pT])
                        if DCUT < 3:
                            continue
                        pi4 = rps(); pso, Tpo = PS[pi4], PT[pi4]
                        for h in range(8):
                            k.mm(pso[:, h * 64:(h + 1) * 64], pT[:, h, :], vc[:, h // 4, :], True, True, [T_pT, T_vc], [Tpo])
                        k.tt("dve", ybacc[:, j, :].rearrange("p (h d) -> p h d", h=8), pso[:, :].rearrange("p (h d) -> p h d", h=8),
                             B3(gates[:, qt, 0:8].unsqueeze(2), [128, 8, 64]), ALU.mult, [Tpo, T_gates[qt]], [T_ya[j]])

                    for kvh in range(2 if 'NOSEL' not in STOP else 0):
                        for bri, (vaug, T_va) in ((1, (vsel, T_vsel)), (2, (vwin, T_vwin))):
                            first = [True] * 4
                            kt_lo = 0 if bri == 1 else max(0, 4 * qb - 4)
                            for kt in range(kt_lo, 4 * qb + 4):
                                jlo = max(0, kt - 4 * qb)
                                jhi = 3 if bri == 1 else min(3, kt + 4 - 4 * qb)
                                c0, c1 = jlo * 128, (jhi + 1) * 128
                                q0 = qb * 512
                                for hh in range(4):
                                    h = 4 * kvh + hh
                                    base = (h % 2) * 64; ch = h // 2
                                    pi_ = rps(); ps, Tp = PS[pi_], PT[pi_]
                                    km = (0 if bri == 1 else 2) + kvh
                                    grp = [(ps[:, c0:c1], kT[base:base + 64, km, kt * 128:(kt + 1) * 128], qT[base:base + 64, ch, q0 + c0:q0 + c1],
                                            [T_k[km][kt // 4], T_q[ch][qb]])]
                                    if bri == 1 and qb >= 2:
                                        grp.append((ps[:, c0:c1], Eb[kvh * 32:(kvh + 1) * 32, kt * 128:(kt + 1) * 128], penT[kvh * 32:(kvh + 1) * 32, c0:c1],
                                                    [T_E, T_penT]))
                                    if kt >= 4 * qb:
                                        jd = kt - 4 * qb
                                        grp.append((ps[:, jd * 128:(jd + 1) * 128], ident_b[:], Cm[:, 0, :], [T_identb, T_C]))
                                    if bri == 2 and 0 <= kt + 4 - 4 * qb <= 3:
                                        j4 = kt + 4 - 4 * qb
                                        grp.append((ps[:, j4 * 128:(j4 + 1) * 128], ident_b[:], Cm[:, 1, :], [T_identb, T_C]))
                                    for gi, (o_, l_, r_, R_) in enumerate(grp):
                                        k.mm(o_, l_, r_, gi == 0, gi == len(grp) - 1, R_, [Tp])
                                    pb_i = (kt * 4 + hh) % 4
                                    k.act(PTb[pb_i][:, c0:c1], ps[:, c0:c1], AF.Exp, [Tp], [T_PT[pb_i]])
                                    for j in range(jlo, jhi + 1):
                                        k.mm(PS[j][:, hh * 65:(hh + 1) * 65], PTb[pb_i][:, j * 128:(j + 1) * 128], vaug[:, kt, kvh, :],
                                             first[j], kt == 4 * qb + j, [T_PT[pb_i], T_va[kt]], [PT[j]], sgc=True)
                                        first[j] = False
                            for j in range(4):
                                qt = 4 * qb + j
                                po3 = PS[j][:, 0:260].rearrange("p (h c) -> p h c", c=65)
                                S.op("dve", (lambda po3: lambda e: e.reciprocal(out=rs4[:], in_=po3[:, :, 64]))(po3), [PT[j]], [T_rs4])
                                k.tt("dve", rs4[:], rs4[:], gates[:, qt, bri * 8 + 4 * kvh:bri * 8 + 4 * kvh + 4], ALU.mult, [T_rs4, T_gates[qt]], [T_rs4])
                                k.tt("dve", tmpo[:], po3[:, :, 0:64], B3(rs4[:, 0:4].unsqueeze(2), [128, 4, 64]), ALU.mult, [PT[j], T_rs4], [T_tmpo])
                                ya = ybacc[:, j, kvh * 256:(kvh + 1) * 256].rearrange("p (h d) -> p h d", h=4)
                                k.tt("dve", ya, ya, tmpo[:], ALU.add, [T_ya[j], T_tmpo], [T_ya[j]])
                    for j in range(4 if DCUT >= 4 else 0):
                        qt = 4 * qb + j
                        k.cp("act", ybb[:], ybacc[:, j, :], [T_ya[j]], [T_ybb])
                        pi_ = rps()
                        for c in range(4):
                            k.tr(PSb[pi_][:, c * 128:(c + 1) * 128], ybb[:, c * 128:(c + 1) * 128], ident_b[:], [T_ybb, T_identb], [PT[pi_]])
                        k.cp("dve", ybT[:, :, qt * 128:(qt + 1) * 128], PSb[pi_][:, 0:512].rearrange("p (c q) -> p c q", c=4), [PT[pi_]], [T_yb[qb]])
                k.cp("dve", qTs[:], qT[:, :, TP:NT], [T_q[m][4] for m in range(4)], [T_qTs])
                k.cp("dve", kTs[:], kT[:, :, TP:NT], [T_k[m][4] for m in range(4)], [T_kTs])
                for s in range(NS):
                    k.cp("pool", vnew[0:4, s, 0, :], vsel[0:4, 16 + s, :, :].rearrange("p k c -> p (k c)"), [T_vsel[16 + s]], [T_vnew])
                    k.cp("pool", vnew[0:4, s, 1, :], vwin[0:4, 16 + s, :, :].rearrange("p k c -> p (k c)"), [T_vwin[16 + s]], [T_vnew])
                S.flush()
                if STOP.startswith('D'):
                    S.wait_all_outputs("sp"); S.flush(); S.close(); return nc
            pr.close()
            with ExitStack() as ps_:
                kselT = sb("kselT", [128, NPG * 128], BF16, ps_); T_kst = [Tl() for _ in range(32)]
                vsl = sb("vsl", [128, NPG, 130], BF16, ps_); T_vsl = [Tl() for _ in range(32)]
                pg = [sb("pg%d" % i, [128, 512], F32, ps_) for i in range(2)]; T_pg = [Tl(), Tl()]
                Xcs = xbv[:, 8256:8256 + 4096].rearrange("p (t s c) -> p t s c", t=16, s=2); T_Xcs = Tl()
                Xcs5 = Xcs.rearrange("p t s (k d) -> p t s k d", k=2)
                Wd = xbv[:, 14336:14336 + 16384].rearrange("p (s d m) -> p s d m", s=2, d=64); T_Wd2 = Tl()
                ptb = sb("ptb", [128, 128], I32, ps_); T_ptb = Tl()
                idxf = sb("idxf", [128, 128], F32, ps_); T_idxf = Tl()
                idxi = ptb; T_idxi = T_ptb
                iot_i = sb("iot_i", [128, 1], I32, ps_); iot_f = sb("iot_f", [128, 1], F32, ps_); T_iot = Tl()
                hidTs = sb("hidTs", [128, 2, 256], BF16, ps_); T_hid = Tl()
                kcTs = sb("kcTs", [128, 256], BF16, ps_); T_kcs = Tl()
                vcs = sb("vcs", [128, 2, 128], BF16, ps_); T_vcs = Tl()
                W2p = sb("W2p", [128, 2, 2, 128], BF16, ps_); T_W2p = Tl()
                w2v = sb("w2v", [128, 64], BF16, ps_); T_w2v = Tl()
                b1s = sb("b1s", [128, 2], F32, ps_); b2k = sb("b2k", [128, 1], F32, ps_); b2v = sb("b2v", [128, 128], F32, ps_); T_bs2 = Tl()
                qbd = sb("qbd", [128, 64], BF16, ps_); T_qbd = Tl()
                knew = sb("knew", [128, 2, 4], BF16, ps_); T_knew = Tl()
                ssm = [sb("ssm0", [64, 512], F32, ps_)] * 2; T_ssm = [Tl()] * 2
                pex = [sb("pex0", [64, 512], BF16, ps_)] * 2; T_pex = [Tl()] * 2
                PTs = [sb("PTs0", [128, 256], BF16, ps_)] * 2; T_PTs = [Tl()] * 2
                PTn = sb("PTn", [4, 64], BF16, ps_); T_PTn = Tl()
                pc = sb("pc", [64, 256], F32, ps_); T_pc = Tl()
                ec = pc; T_ec = T_pc
                pcb = sb("pcb", [64, 256], BF16, ps_); T_pcb = Tl()
                impf = sb("impf", [64, 256], F32, ps_); T_impf = Tl()
                t8s = sb("t8s", [64, 16], F32, ps_); T_t8s = Tl()
                wks = sb("wks", [64, 256], F32, ps_); T_wks = Tl()
                pens = sb("pens", [64, 256], F32, ps_); T_pens = Tl()
                mxs = sb("mxs", [64, 2], F32, ps_); T_mxs = Tl()
                pTs = sb("pTs", [128, 2, 64], BF16, ps_); T_pTs = Tl()
                kwinT = sb("kwinT", [128, 512], BF16, ps_); T_kwT = Tl()
                vwn = sb("vwn", [128, 4, 130], BF16, ps_); T_vwn = Tl()
                swt = [sb("swt0", [128, 256], F32, ps_)] * 2; T_swt = [Tl()] * 2
                Gm = sb("Gm", [64, 64], F32, ps_); SelT = sb("SelT", [4, 64], F32, ps_); Hsel = sb("Hsel", [64, 8], F32, ps_)
                CN = sb("CN", [64, 4], F32, ps_); CW = sb("CW", [64, 4], F32, ps_); T_cst = Tl()
                gr3 = sb("gr3", [64, 3, 8], F32, ps_); grow = sb("grow", [64, 3], F32, ps_); T_grow = Tl()
                ob = sb("ob", [64, 64], F32, ps_); T_ob = Tl()
                obb = sb("obb", [64, 64], BF16, ps_); T_obb = Tl()
                rs2 = sb("rs2", [64, 1], F32, ps_); T_rs2 = Tl()
                tmo = ssm[0][:, 448:512]; T_tmo = T_ssm[0]
                cache2d = cache_d

                for (t_, d_) in ((Gm, Gm_d), (SelT, SelT_d), (Hsel, Hsel_d), (CN, CN_d), (CW, CW_d)):
                    k.ld(t_[:], d_[:, :], [T_cst])
                k.ld(b1s[:], b1r_d[:, :], [T_bs2]); k.ld(b2k[:], b2k_d[:, :], [T_bs2]); k.ld(b2v[:], b2v_d[:, :], [T_bs2])
                k.memset("pool", W2p[:], 0.0, [T_W2p])
                for parity in range(2):
                    for kvh in range(2):
                        k.ldc(W2p[parity * 64:(parity + 1) * 64, parity, kvh, kvh * 64:(kvh + 1) * 64], w2_d[0, :, :], [T_W2p])
                    k.ldc(w2v[parity * 64:(parity + 1) * 64, :], w2_d[1, :, :], [T_w2v])
                k.memset("pool", vsl[:], 1.0, T_vsl)
                k.memset("pool", vwn[:], 1.0, [T_vwn])
                S.op("pool", lambda e: e.iota(iot_i[:], pattern=[[0, 1]], base=0, channel_multiplier=1), (), [T_iot])
                k.cp("dve", iot_f[:], iot_i[:], [T_iot], [T_iot])

                rot = [1]

                def rp():
                    i = rot[0]
                    rot[0] = 1 + (i % 7)
                    return i

                PO, T_PO = PS[0], PT[0]
                cvt = [0]
                for s in range(NS):
                    tg = 16 + s
                    k.ld(ptb[:], pt_d[s, :, :], [T_ptb])
                    k.cp("dve", idxf[:], ptb[:], [T_ptb], [T_idxf])
                    k.ts("dve", idxf[:], idxf[:], 128.0, iot_f[:, 0:1], ALU.mult, ALU.add, [T_idxf, T_iot], [T_idxf])
                    k.cp("dve", idxi[:], idxf[:], [T_idxf], [T_idxi])
                    k.memset("pool", qbd[:], 0.0, [T_qbd])
                    for h in range(8):
                        kvh = h // 4; g = h % 4; sb_ = (h % 2) * 64
                        k.cp("dve", qbd[kvh * 64:(kvh + 1) * 64, kvh * 32 + g * 4:kvh * 32 + g * 4 + 4], qTs[sb_:sb_ + 64, h // 2, s * 4:(s + 1) * 4],
                             [T_qTs], [T_qbd])
                    for br in range(2):
                        for kvh in range(2):
                            k.cp("dve", knew[kvh * 64:(kvh + 1) * 64, br, :], kTs[kvh * 64:(kvh + 1) * 64, br * 2 + kvh, s * 4:(s + 1) * 4], [T_kTs], [T_knew])
                    pi_ = rp()
                    k.mm(PS[pi_][0:64, 0:24], SelT[:, :], gates[0:4, tg, :], True, True, [T_cst, T_gates[tg]], [PT[pi_]])
                    k.tt("dve", gr3[:], PS[pi_][0:64, 0:24].rearrange("p (b h) -> p b h", b=3), Hsel[:, :].unsqueeze(1).to_broadcast([64, 3, 8]), ALU.mult,
                         [PT[pi_], T_cst], [T_grow])
                    S.op("dve", lambda e: e.tensor_reduce(out=grow[:], in_=gr3[:], axis=AX.X, op=ALU.add), [T_grow], [T_grow])

                    for j in range(NPG):
                        p_ = pg[j % 2]; Tp_ = T_pg[j % 2]
                        S.dma("pool", (lambda p_, j: lambda e: e.indirect_dma_start(out=p_[:, :], out_offset=None, in_=cache2d[:, :],
                                                                                    in_offset=bass.IndirectOffsetOnAxis(ap=idxi[:, j:j + 1], axis=0)))(p_, j),
                              [T_idxi], [Tp_])
                        jj = j % 16
                        k.tt("pool", Xcs[:, jj, :, :], p_[:, 0:256].rearrange("p (s c) -> p s c", s=2), pe_bc[:].rearrange("p s k d -> p s (k d)"), ALU.add,
                             [Tp_, T_pe], [T_Xcs])
                        k.cp("pool", vsl[:, j, :].rearrange("p (k c) -> p k c", k=2)[:, :, 0:64], p_[:, 384:512].rearrange("p (k d) -> p k d", k=2), [Tp_], [T_vsl[j // 4]])
                        if j % 4 == 0:
                            pit = rp()
                        k.tr(PS[pit][:, (j % 4) * 128:(j % 4 + 1) * 128], p_[:, 256:384], ident_f[:], [Tp_, T_identf], [PT[pit]])
                        if j % 4 == 3:
                            k.evac(kselT[:, (j - 3) * 128:(j + 1) * 128], PS[pit][:, :], [PT[pit]], [T_kst[j // 4]])
                        if jj == 15:
                            ch = j // 16
                            for sl in range(2):
                                pic = rp()
                                for d in range(64):
                                    k.mm(PS[pic][:, 0:32].rearrange("p (t k) -> p t k", k=2), Wd[:, sl, d, :], Xcs5[:, :, sl, :, d], d == 0, d == 63,
                                         [T_Wd2, T_Xcs], [PT[pic]])
                                k.act(hidTs[:, sl, ch * 32:(ch + 1) * 32], PS[pic][:, 0:32], AF.Gelu_apprx_tanh, [PT[pic], T_bs2], [T_hid], bias=b1s[:, sl:sl + 1])
                    hv = hidTs[:, :, :].rearrange("p s (g k) -> p s g k", k=2)
                    for parity in range(2):
                        pi_ = rp()
                        for kvh in range(2):
                            k.mm(PS[pi_][:, 0:128], W2p[:, parity, kvh, :], hv[:, 0, :, kvh], kvh == 0, kvh == 1, [T_W2p, T_hid], [PT[pi_]])
                        k.act(kcTs[:, parity * 128:(parity + 1) * 128], PS[pi_][:, 0:128], AF.Identity, [PT[pi_], T_bs2], [T_kcs], bias=b2k[:, 0:1])
                    for parity in range(2):
                        pi_ = rp()
                        for kvh in range(2):
                            k.mm(PS[pi_][:, kvh * 64:(kvh + 1) * 64], hv[parity * 64:(parity + 1) * 64, 1, :, kvh], w2v[parity * 64:(parity + 1) * 64, :], True, True,
                                 [T_hid, T_w2v], [PT[pi_]])
                        k.tt("dve", vcs[:, parity, :], PS[pi_][:, 0:128], b2v[:, :], ALU.add, [PT[pi_], T_bs2], [T_vcs])
                    pi_ = rp()
                    k.mm(PS[pi_][0:64, 0:256], qbd[:, :], kcTs[:, :], True, True, [T_qbd, T_kcs], [PT[pi_]])
                    S.op("dve", (lambda pi_: lambda e: e.tensor_reduce(out=mxs[:, 0:1], in_=PS[pi_][0:64, 0:256], axis=AX.X, op=ALU.max))(pi_), [PT[pi_]], [T_mxs])
                    k.ts("dve", mxs[:, 0:1], mxs[:, 0:1], -1.0, None, ALU.mult, None, [T_mxs], [T_mxs])
                    k.act(ec[:], PS[pi_][0:64, 0:256], AF.Exp, [PT[pi_], T_mxs], [T_ec, T_mxs], bias=mxs[:, 0:1], accum_out=mxs[:, 1:2])
                    S.op("dve", lambda e: e.reciprocal(out=mxs[:, 1:2], in_=mxs[:, 1:2]), [T_mxs], [T_mxs])
                    k.ts("dve", pc[:], ec[:], mxs[:, 1:2], None, ALU.mult, None, [T_mxs, T_pc], [T_pc])
                    k.cp("pool", pcb[:], pc[:], [T_pc], [T_pcb])
                    pi2 = rp()
                    k.mm(PS[pi2][0:64, 0:256], Gm[:, :], pc[:, :], True, True, [T_cst, T_pc], [PT[pi2]])
                    k.cp("act", impf[:], PS[pi2][0:64, 0:256], [PT[pi2]], [T_impf])
                    k.memset("dve", impf[:, 0:1], 1e4, [T_impf])
                    k.memset("dve", impf[:, 255:256], 1e4, [T_impf])
                    S.op("dve", lambda e: e.max(out=t8s[:, 0:8], in_=impf[:]), [T_impf], [T_t8s])
                    S.op("dve", lambda e: e.match_replace(out=wks[:], in_to_replace=t8s[:, 0:8], in_values=impf[:], imm_value=-1e30), [T_impf, T_t8s], [T_wks])
                    S.op("dve", lambda e: e.max(out=t8s[:, 8:16], in_=wks[:]), [T_wks], [T_t8s])
                    k.ts("dve", pens[:], impf[:], t8s[:, 14:15], -NEGB, ALU.is_ge, ALU.mult, [T_impf, T_t8s], [T_pens])
                    k.ts("dve", pens[:], pens[:], NEGB, None, ALU.add, None, [T_pens], [T_pens])
                    pi3 = rp()
                    for parity in range(2):
                        k.tr(PSb[pi3][:, parity * 64:(parity + 1) * 64], pcb[:, parity * 128:(parity + 1) * 128], ident_b[0:64, 0:64], [T_pcb, T_identb], [PT[pi3]])
                    k.cp("act", pTs[:].rearrange("p a r -> p (a r)"), PSb[pi3][:, 0:128], [PT[pi3]], [T_pTs])
                    pi4 = rp()
                    for parity in range(2):
                        k.mm(PS[pi4][0:64, 0:128], pTs[:, parity, :], vcs[:, parity, :], parity == 0, parity == 1, [T_pTs, T_vcs], [PT[pi4]])
                    for half in range(2):
                        rsl = slice(32 * half, 32 * half + 32)
                        k.ts("dve", ob[rsl, :], PS[pi4][rsl, half * 64:(half + 1) * 64], grow[rsl, 0:1], None, ALU.mult, None, [PT[pi4], T_grow], [T_ob])

                    for t in range(4):
                        w_ = swt[t % 2]; Tw_ = T_swt[t % 2]
                        k.ld(w_[:], swin_d[s, t * 128:(t + 1) * 128, :], [Tw_])
                        if t == 0:
                            piw = rp()
                        k.tr(PS[piw][:, t * 128:(t + 1) * 128], w_[:, 0:128], ident_f[:], [Tw_, T_identf], [PT[piw]])
                        k.cp("pool", vwn[:, t, :].rearrange("p (k c) -> p k c", k=2)[:, :, 0:64], w_[:, 128:256].rearrange("p (k d) -> p k d", k=2), [Tw_], [T_vwn])
                    k.evac(kwinT[:], PS[piw][:, :], [PT[piw]], [T_kwT])

                    pen3 = pens[:].rearrange("r (a j) -> r j a", a=2)
                    for br in range(2):
                        ngrp = 32 if br == 0 else 1
                        first = True
                        for gq in range(ngrp):
                            b_ = cvt[0] % 2; cvt[0] += 1
                            pi_ = rp()
                            if br == 0:
                                k.mm(PS[pi_][0:64, :], qbd[:, :], kselT[:, gq * 512:(gq + 1) * 512], True, True, [T_qbd, T_kst[gq]], [PT[pi_]])
                                k.tt("dve", ssm[b_][:].rearrange("r (j a i) -> r j a i", j=4, a=2), PS[pi_][0:64, :].rearrange("r (j a i) -> r j a i", j=4, a=2),
                                     pen3[:, gq * 4:(gq + 1) * 4, :].unsqueeze(3).to_broadcast([64, 4, 2, 64]), ALU.add, [PT[pi_], T_pens], [T_ssm[b_]])
                            else:
                                k.mm(PS[pi_][0:64, :], qbd[:, :], kwinT[:, :], True, True, [T_qbd, T_kwT], [PT[pi_]])
                                k.tt("dve", ssm[b_][:, 0:4], PS[pi_][0:64, 0:4], CW[:, :], ALU.add, [PT[pi_], T_cst], [T_ssm[b_]])
                            if br == 0:
                                k.act(pex[b_][:], ssm[b_][:], AF.Exp, [T_ssm[b_]], [T_pex[b_]])
                            else:
                                k.act(pex[b_][:, 0:4], ssm[b_][:, 0:4], AF.Exp, [T_ssm[b_]], [T_pex[b_]])
                                k.act(pex[b_][:, 4:512], PS[pi_][0:64, 4:512], AF.Exp, [PT[pi_]], [T_pex[b_]])
                            pit2 = rp()
                            for jj in range(4):
                                k.tr(PSb[pit2][:, jj * 64:(jj + 1) * 64], pex[b_][:, jj * 128:(jj + 1) * 128], ident_b[0:64, 0:64], [T_pex[b_], T_identb], [PT[pit2]])
                            k.evac(PTs[b_][:], PSb[pit2][:, 0:256], [PT[pit2]], [T_PTs[b_]])
                            for jj in range(4):
                                if br == 0:
                                    rhs_ = vsl[:, gq * 4 + jj, :]; Tr_ = T_vsl[gq]
                                else:
                                    rhs_ = vwn[:, jj, :]; Tr_ = T_vwn
                                k.mm(PO[0:64, 0:130], PTs[b_][:, jj * 64:(jj + 1) * 64], rhs_, first, False, [T_PTs[b_], Tr_], [T_PO])
                                first = False
                        pi_ = rp()
                        k.mm(PS[pi_][0:64, 0:4], qbd[:, :], knew[:, br, :], True, True, [T_qbd, T_knew], [PT[pi_]])
                        b_ = cvt[0] % 2; cvt[0] += 1
                        k.tt("dve", ssm[b_][:, 0:4], PS[pi_][0:64, 0:4], CN[:, :], ALU.add, [PT[pi_], T_cst], [T_ssm[b_]])
                        k.act(pex[b_][:, 0:4], ssm[b_][:, 0:4], AF.Exp, [T_ssm[b_]], [T_pex[b_]])
                        pit2 = rp()
                        k.tr(PSb[pit2][0:4, 0:64], pex[b_][:, 0:4], ident_b[0:64, 0:64], [T_pex[b_], T_identb], [PT[pit2]])
                        k.cp("dve", PTn[:], PSb[pit2][0:4, 0:64], [PT[pit2]], [T_PTn])
                        k.mm(PO[0:64, 0:130], PTn[:, :], vnew[0:4, s, br, :], False, True, [T_PTn, T_vnew], [T_PO])
                        for half in range(2):
                            rsl = slice(32 * half, 32 * half + 32)
                            c0 = half * 65
                            S.op("dve", (lambda rsl, c0: lambda e: e.reciprocal(out=rs2[rsl, :], in_=PO[rsl, c0 + 64:c0 + 65]))(rsl, c0), [T_PO], [T_rs2])
                            k.tt("dve", rs2[rsl, :], rs2[rsl, :], grow[rsl, 1 + br:2 + br], ALU.mult, [T_rs2, T_grow], [T_rs2])
                            k.ts("dve", tmo[rsl, :], PO[rsl, c0:c0 + 64], rs2[rsl, 0:1], None, ALU.mult, None, [T_PO, T_rs2], [T_tmo])
                            k.tt("dve", ob[rsl, :], ob[rsl, :], tmo[rsl, :], ALU.add, [T_ob, T_tmo], [T_ob])
                    k.cp("act", obb[:], ob[:], [T_ob], [T_obb])
                    pi_ = rp()
                    k.tr(PSb[pi_][0:64, 0:64], obb[:, :], ident_b[0:64, 0:64], [T_obb, T_identb], [PT[pi_]])
                    for h in range(8):
                        kvh = h // 4; g = h % 4; db = (h % 2) * 64
                        k.cp("dve", ybT[db:db + 64, h // 2, TP + s * 4:TP + (s + 1) * 4], PSb[pi_][0:64, kvh * 32 + g * 4:kvh * 32 + g * 4 + 4], [PT[pi_]], [T_yb[4]])
                S.flush()

        with ExitStack() as pe_:
            mixT = sb("mixT", [128, KC, NT], BF16, pe_); T_mix = [[Tl() for _ in TBS] for _ in range(KC)]
            with ExitStack() as pe1:
                wba = sb("wba", [128, 4, D], BF16, pe1); T_wba = Tl()
                wbb = sb("wbb", [128, 4, D], BF16, pe1); T_wbb = Tl()
                wzm = [sb("wzm%d" % i, [128, KC, 2, 128], BF16, pe1) for i in range(2)]; T_wzm = [Tl(), Tl()]
                gsa = [sb("gsa%d" % i, [128, 512], F32, pe1) for i in range(2)]; T_gsa = [Tl(), Tl()]
                gsb = [sb("gsb%d" % i, [128, 512], F32, pe1) for i in range(2)]; T_gsb = [Tl(), Tl()]
                k.ldc(wba[:], wba_d.rearrange("(kc p) n -> p kc n", p=128), [T_wba])
                k.ldc(wbb[:], wbb_d.rearrange("(kc p) n -> p kc n", p=128), [T_wbb])
                wi = win_d.rearrange("(kc p) n -> p kc n", p=128)
                it = 0
                for c in range(KC):
                    wz = wzm[c % 2]; Twz = T_wzm[c % 2]
                    k.ldc(wz[:, :, 0, :], wi[:, :, 2328 + c * 128:2328 + (c + 1) * 128], [Twz])
                    k.ldc(wz[:, :, 1, :], wi[:, :, 3352 + c * 128:3352 + (c + 1) * 128], [Twz])
                    for bi, (t0, n) in enumerate(TBS):
                        ga = gsa[it % 2]; Tga = T_gsa[it % 2]; gb = gsb[it % 2]; Tgb = T_gsb[it % 2]; it += 1
                        psA, TA = nextps()
                        for k4 in range(4):
                            k.mm(psA[:, 0:n], wba[:, k4, c * 128:(c + 1) * 128], uT[:, k4, t0:t0 + n], k4 == 0, k4 == 3, [T_wba, T_u[k4][bi]], [TA])
                        psG, TG = nextps()
                        for kc in range(KC):
                            k.mm(psG[:, 0:n], wz[:, kc, 0, :], hT[:, kc, t0:t0 + n], kc == 0, kc == KC - 1, [Twz, T_h[kc][bi]], [TG])
                        k.act(ga[:, 0:n], psG[:, 0:n], AF.Sigmoid, [TG], [Tga])
                        k.tt("dve", ga[:, 0:n], ga[:, 0:n], psA[:, 0:n], ALU.mult, [Tga, TA], [Tga])
                        psB, TB = nextps()
                        for k4 in range(4):
                            k.mm(psB[:, 0:n], wbb[:, k4, c * 128:(c + 1) * 128], ybT[:, k4, t0:t0 + n], k4 == 0, k4 == 3, [T_wbb, T_yb[bi]], [TB])
                        psH, TH = nextps()
                        for kc in range(KC):
                            k.mm(psH[:, 0:n], wz[:, kc, 1, :], hT[:, kc, t0:t0 + n], kc == 0, kc == KC - 1, [Twz, T_h[kc][bi]], [TH])
                        k.act(gb[:, 0:n], psH[:, 0:n], AF.Sigmoid, [TH], [Tgb])
                        k.tt("dve", gb[:, 0:n], gb[:, 0:n], psB[:, 0:n], ALU.mult, [Tgb, TB], [Tgb])
                        k.tt("pool", mixT[:, c, t0:t0 + n], ga[:, 0:n], gb[:, 0:n], ALU.add, [Tga, Tgb], [T_mix[c][bi]])
                S.flush()
                if STOP == 'E1':
                    S.wait_all_outputs("sp"); S.flush(); S.close(); return nc
            with ExitStack() as pe2:
                wout = sb("wout", [128, KC, D], BF16, pe2); T_wout = Tl()
                xr = [sb("xr%d" % i, [128, 512], F32, pe2) for i in range(2)]; T_xr = [Tl(), Tl()]
                k.ldc(wout[:], wout_d.rearrange("(kc p) n -> p kc n", p=128), [T_wout])
                it = 0
                for c in range(KC):
                    for bi, (t0, n) in enumerate(TBS):
                        x_ = xr[it % 2]; Tx = T_xr[it % 2]; it += 1
                        k.ld(x_[:, 0:n], xT_d[:, c, t0:t0 + n], [Tx])
                        ps, Tp = nextps()
                        for kc in range(KC):
                            k.mm(ps[:, 0:n], wout[:, kc, c * 128:(c + 1) * 128], mixT[:, kc, t0:t0 + n], kc == 0, kc == KC - 1, [T_wout, T_mix[kc][bi]], [Tp])
                        if bi < 4:
                            k.stt(x1T[:, c, t0:t0 + n], ps[:, 0:n], modT[:, 16 + c, 0:1], x_[:, 0:n], ALU.mult, ALU.add, [Tp, Tx, T_mod], [T_x1[c][bi]])
                        else:
                            for s in range(NS):
                                k.stt(x1T[:, c, t0 + s * TS:t0 + (s + 1) * TS], ps[:, s * TS:(s + 1) * TS], modT[:, 16 + c, 1 + s:2 + s],
                                      x_[:, s * TS:(s + 1) * TS], ALU.mult, ALU.add, [Tp, Tx, T_mod], [T_x1[c][bi]])
                S.flush()

        mid_scope.close()
        with ExitStack() as pf:
            sq = [sb("sqF%d" % i, [128, 1040], BF16, pf) for i in range(2)]; T_sq = [Tl(), Tl()]
            rstd = sb("rstdF", [128, NT], F32, pf); T_rstd = [Tl() for _ in TBS]
            tmp = [sb("tmpF0", [128, 1040], F32, pf)] * 2; T_tmp = [Tl()] * 2
            actT = sb("actT", [128, 22, 1040], BF16, pf); T_act = [[Tl() for _ in range(3)] for _ in range(22)]
            wup = [sb("wup%d" % i, [128, KC, 2, 128], BF16, pf) for i in range(2)]; T_wup = [Tl(), Tl()]
            wdn = [sb("wdn%d" % i, [128, 22, 128], BF16, pf) for i in range(2)]; T_wdn = [Tl(), Tl()]
            U = [sb("U%d" % i, [128, 514], F32, pf) for i in range(4)]; T_U = [Tl() for _ in range(4)]
            cv = [sb("cv%d" % i, [128, 512], F32, pf) for i in range(4)]; T_cv = [Tl() for _ in range(4)]
            gl = [sb("gl%d" % i, [128, 512], F32, pf) for i in range(2)]; T_gl = [Tl(), Tl()]
            halo = sb("halo", [128, 44, 2], F32, pf); T_halo = [Tl() for _ in range(44)]
            convo = sb("convo", [128, 44, 10], F32, pf); T_convo = Tl()
            wcv = sb("wcv", [128, 44, 3], F32, pf); bcv = sb("bcv", [128, 44], F32, pf); T_wcv = Tl()
            sprev = sb("sprev", [128, 44, 4, 2], F32, pf); T_sprev = Tl()
            ups = [sb("ups%d" % i, [128, 4, 6], F32, pf) for i in range(2)]; T_ups = [Tl(), Tl()]
            cvs = [sb("cvs%d" % i, [128, 4, 4], F32, pf) for i in range(2)]; T_cvs = [Tl(), Tl()]
            gf = sb("gf", [128, KC], F32, pf); T_gf = Tl()
            yo = [sb("yo0", [128, 1040], F32, pf)] * 2; T_yo = [Tl()] * 2
            k.ld(wcv[:], wcv_d[:, :, :], [T_wcv]); k.ld(bcv[:], bcv_d[:, :], [T_wcv])
            k.ld(sprev[:], sprev_d[:, :, :, :], [T_sprev])
            k.ld(gf[:], gf_d[:, :], [T_gf])
            wupv = wup_d.rearrange("(kc p) n -> p kc n", p=128)
            wdnv = wdn_d.rearrange("(c p) n -> p c n", p=128)
            HALVES = [(0, [(0, 512), (512, 512)]), (1024, [(1024, 512), (1536, 512), (2048, 16)])]

            def rms_stats(src, Tsrc, hs, blocks, nb0):
                ntk = sum(n for _, n in blocks)
                for kc in range(KC):
                    s_ = sq[kc % 2]; Ts = T_sq[kc % 2]
                    k.act(s_[:, 0:ntk], src[:, kc, hs:hs + ntk], AF.Square, Tsrc(kc), [Ts])
                    for bi, (t0, n) in enumerate(blocks):
                        k.mm(PS[bi][:, 0:n], ones_b[:], s_[:, t0 - hs:t0 - hs + n], kc == 0, kc == KC - 1, [T_ones, Ts], [PT[bi]])
                for bi, (t0, n) in enumerate(blocks):
                    k.act(rstd[:, t0:t0 + n], PS[bi][:, 0:n], AF.Sqrt, [PT[bi]], [T_rstd[nb0 + bi]], bias=EPS, scale=1.0 / D)
                    S.op("dve", (lambda o: (lambda e: e.reciprocal(out=o, in_=o)))(rstd[:, t0:t0 + n]), [T_rstd[nb0 + bi]], [T_rstd[nb0 + bi]])

            uidx = [0]
            for hi, (hs, blocks) in enumerate(HALVES):
                nb0 = 0 if hi == 0 else 2
                ntk = sum(n for _, n in blocks)
                rms_stats(x1T, lambda kc: [T_x1[kc][nb0 + b] for b in range(len(blocks))], hs, blocks, nb0)
                for kc in range(KC):
                    t_ = tmp[kc % 2]; Tt = T_tmp[kc % 2]
                    Tx = [T_x1[kc][nb0 + b] for b in range(len(blocks))]
                    Th = [T_h[kc][nb0 + b] for b in range(len(blocks))]
                    k.tt("dve", t_[:, 0:ntk], x1T[:, kc, hs:hs + ntk], rstd[:, hs:hs + ntk], ALU.mult, Tx + T_rstd[nb0:nb0 + len(blocks)], [Tt])
                    npr = ntk if hi == 0 else 1024
                    k.act(hT[:, kc, hs:hs + npr], t_[:, 0:npr], AF.Identity, [Tt, T_A2, T_mod], Th, bias=modT[:, 24 + kc, 0:1], scale=A2[:, kc, 0:1])
                    if hi == 1:
                        for s in range(NS):
                            c0 = 1024 + s * TS
                            k.act(hT[:, kc, TP + s * TS:TP + (s + 1) * TS], t_[:, c0:c0 + TS], AF.Identity, [Tt, T_A2, T_mod], Th,
                                  bias=modT[:, 24 + kc, 1 + s:2 + s], scale=A2[:, kc, 1 + s:2 + s])
                for cp_ in range(22):
                    wu_ = wup[cp_ % 2]; Twu = T_wup[cp_ % 2]
                    k.ldc(wu_[:, :, 0, :], wupv[:, :, cp_ * 128:(cp_ + 1) * 128], [Twu])
                    k.ldc(wu_[:, :, 1, :], wupv[:, :, DFF + cp_ * 128:DFF + (cp_ + 1) * 128], [Twu])
                    for cc in range(1):
                        c = cp_
                        for bi, (t0, n) in enumerate(blocks):
                            gbi = nb0 + bi
                            res = []
                            for ag in range(2):
                                idx = c + 22 * ag
                                ps, Tp = nextps()
                                for kc in range(KC):
                                    k.mm(ps[:, 0:n], wu_[:, kc, ag, cc * 128:(cc + 1) * 128], hT[:, kc, t0:t0 + n], kc == 0, kc == KC - 1,
                                         [Twu, T_h[kc][gbi]], [Tp])
                                w0 = wcv[:, idx, 0:1]; w1 = wcv[:, idx, 1:2]; w2 = wcv[:, idx, 2:3]; bb = bcv[:, idx:idx + 1]
                                if n == 512:
                                    ui = uidx[0] % 4; uidx[0] += 1
                                    U_ = U[ui]; TU = T_U[ui]; cv_ = cv[ui]; Tcv = T_cv[ui]
                                    if t0 == 0:
                                        k.memset("pool", U_[:, 0:2], 0.0, [TU])
                                    else:
                                        k.cp("pool", U_[:, 0:2], halo[:, idx, :], [T_halo[idx]], [TU])
                                    k.cp("act", U_[:, 2:514], ps[:, 0:512], [Tp], [TU])
                                    k.cp("pool", halo[:, idx, :], U_[:, 512:514], [TU], [T_halo[idx]])
                                    if t0 == 1536:
                                        k.cp("pool", convo[:, idx, 0:2], U_[:, 512:514], [TU], [T_convo])
                                    k.ts("dve", cv_[:], U_[:, 2:514], w2, bb, ALU.mult, ALU.add, [TU, T_wcv], [Tcv])
                                    k.stt(cv_[:], U_[:, 1:513], w1, cv_[:], ALU.mult, ALU.add, [TU, T_wcv, Tcv], [Tcv])
                                    k.stt(cv_[:], U_[:, 0:512], w0, cv_[:], ALU.mult, ALU.add, [TU, T_wcv, Tcv], [Tcv])
                                    res.append((cv_[:], Tcv))
                                else:
                                    u_ = ups[ag]; Tu_ = T_ups[ag]; c_ = cvs[ag]; Tc_ = T_cvs[ag]
                                    k.cp("pool", u_[:, :, 0:2], sprev[:, idx, :, :], [T_sprev], [Tu_])
                                    k.cp("act", u_[:, :, 2:6], ps[:, 0:16].rearrange("p (s t) -> p s t", s=4), [Tp], [Tu_])
                                    k.cp("pool", convo[:, idx, 2:10].rearrange("p (s r) -> p s r", s=4), u_[:, :, 4:6], [Tu_], [T_convo])
                                    k.ts("dve", c_[:], u_[:, :, 2:6], w2, bb, ALU.mult, ALU.add, [Tu_, T_wcv], [Tc_])
                                    k.stt(c_[:], u_[:, :, 1:5], w1, c_[:], ALU.mult, ALU.add, [Tu_, T_wcv, Tc_], [Tc_])
                                    k.stt(c_[:], u_[:, :, 0:4], w0, c_[:], ALU.mult, ALU.add, [Tu_, T_wcv, Tc_], [Tc_])
                                    res.append((c_[:].rearrange("p s t -> p (s t)"), Tc_))
                            (ca, Tca), (cg, Tcg) = res
                            g_ = gl[(c + bi) % 2]; Tg_ = T_gl[(c + bi) % 2]
                            k.act(g_[:, 0:n], ca, AF.Gelu_apprx_tanh, [Tca], [Tg_])
                            k.tt("dve", actT[:, c, t0 - hs:t0 - hs + n], g_[:, 0:n], cg, ALU.mult, [Tg_, Tcg], [T_act[c][bi]])
                for m in range(KC):
                    wd_ = wdn[m % 2]; Twd = T_wdn[m % 2]
                    k.ldc(wd_[:], wdnv[:, :, m * 128:(m + 1) * 128], [Twd])
                    for bi, (t0, n) in enumerate(blocks):
                        gbi = nb0 + bi
                        ps, Tp = nextps()
                        for c in range(22):
                            k.mm(ps[:, 0:n], wd_[:, c, :], actT[:, c, t0 - hs:t0 - hs + n], c == 0, c == 21, [Twd, T_act[c][bi]], [Tp])
                        if n == 512:
                            k.stt(x1T[:, m, t0:t0 + n], ps[:, 0:n], modT[:, 40 + m, 0:1], x1T[:, m, t0:t0 + n], ALU.mult, ALU.add,
                                  [Tp, T_mod, T_x1[m][gbi]], [T_x1[m][gbi]])
                        else:
                            for s in range(NS):
                                xs_ = x1T[:, m, t0 + s * TS:t0 + (s + 1) * TS]
                                k.stt(xs_, ps[:, s * TS:(s + 1) * TS], modT[:, 40 + m, 1 + s:2 + s], xs_, ALU.mult, ALU.add,
                                      [Tp, T_mod, T_x1[m][gbi]], [T_x1[m][gbi]])
                rms_stats(x1T, lambda kc: [T_x1[kc][nb0 + b] for b in range(len(blocks))], hs, blocks, nb0)
                for m in range(KC):
                    t_ = tmp[m % 2]; Tt = T_tmp[m % 2]
                    y_ = yo[m % 2]; Ty = T_yo[m % 2]
                    Tx = [T_x1[m][nb0 + b] for b in range(len(blocks))]
                    k.tt("dve", t_[:, 0:ntk], x1T[:, m, hs:hs + ntk], rstd[:, hs:hs + ntk], ALU.mult, Tx + T_rstd[nb0:nb0 + len(blocks)], [Tt])
                    k.act(y_[:, 0:ntk], t_[:, 0:ntk], AF.Copy, [Tt, T_gf], [Ty], scale=gf[:, m:m + 1])
                    k.st(yT_o[:, m, hs:hs + ntk], y_[:, 0:ntk], [Ty])
            k.st(conv_o[:, :], convo[:].rearrange("p i r -> p (i r)"), [T_convo])
            S.wait_all_outputs("sp")
            S.flush()
    S.close()
    return nc


_NC_CACHE = {}


def _prep_inputs(inp):
    f = lambda a: np.ascontiguousarray(a, dtype=np.float32)
    xp = np.asarray(inp["x_prompt"]); xs = np.asarray(inp["x_sample"])
    cp_ = np.asarray(inp["c_prompt"]); cs_ = np.asarray(inp["c_sample"])

    def fm(vec):
        return f(np.asarray(vec).reshape(KC, 128).T)

    shared = {
        "w_ada": f(np.asarray(inp["w_ada"])[0]),
        "b_adaT": f(np.asarray(inp["b_ada"])[0].reshape(48, 128).T),
        "g1T": fm(inp["g_norm1"][0]), "g2T": fm(inp["g_norm2"][0]), "gfT": fm(inp["g_final"]),
        "w_in": f(np.asarray(inp["w_in"])[0]),
        "ln_g_bc": f(np.broadcast_to(np.asarray(inp["ln_v_g"])[0][None, :], (128, 512))),
        "ln_b_bc": f(np.broadcast_to(np.asarray(inp["ln_v_b"])[0][None, :], (128, 512))),
        "ident": np.eye(128, dtype=np.float32),
    }
    ws = np.asarray(inp["w_spatial"])[0]
    shared["wsT"] = f(ws.transpose(2, 0, 1))
    shared["wssT"] = f(ws[:, :4, :4].transpose(2, 0, 1))
    shared["bs"] = f(np.asarray(inp["b_spatial"])[0][None])
    b1 = np.asarray(inp["cmp_b1"])[0]; b2 = np.asarray(inp["cmp_b2"])[0]
    shared["b1r"] = f(np.concatenate([b1.T, b1.T], axis=0))
    shared["b2r"] = f(np.concatenate([b2.T, b2.T], axis=0))
    shared["cmp_w1"] = f(np.asarray(inp["cmp_w1"])[0])
    shared["cmp_w2"] = f(np.asarray(inp["cmp_w2"])[0])
    shared["w_branch_a"] = f(np.asarray(inp["w_branch_a"])[0])
    shared["w_branch_b"] = f(np.asarray(inp["w_branch_b"])[0])
    shared["w_out"] = f(np.asarray(inp["w_out"])[0])
    shared["w_up"] = f(np.asarray(inp["w_up"])[0])
    shared["w_down"] = f(np.asarray(inp["w_down"])[0])
    shared["w_convT"] = f(np.asarray(inp["w_conv"])[0].reshape(3, 44, 128).transpose(2, 1, 0))
    shared["b_convT"] = f(np.asarray(inp["b_conv"])[0].reshape(44, 128).T)
    t = np.arange(2048)[:, None]; n = np.arange(32)[None, :]
    avail = (n + 1) * 64 <= t + 1
    cur = t // 64
    forced = (n == 0) | (n == cur) | (n == cur - 1)
    future = n > cur
    mk = np.stack([np.where(avail, 0.0, -1e30), avail.astype(np.float32), (~(forced | future)).astype(np.float32),
                   np.where(forced, 1e4, np.where(future, -1.0, 0.0))], axis=0)
    shared["msk"] = f(mk.reshape(4, 16, 128, 32).transpose(2, 0, 1, 3))
    key = np.arange(2048)[None, :]; r = np.arange(64)[:, None]
    shared["Emat"] = f((key // 64 == (r % 32)).astype(np.float32))
    b_ = np.arange(128)[:, None]; a_ = np.arange(128)[None, :]
    shared["Cm"] = f(np.stack([np.where(a_ >= b_, 0.0, NEGB), np.where(a_ < b_, 0.0, NEGB)], axis=1))
    sconv = np.asarray(inp["state_ffn_conv"])[0]
    if not STOP:
        shared["cache2d"] = np.asarray(inp["cache_kv"], dtype=np.float32).reshape(5120 * 128, 512)
    shared["b2k"] = f(np.concatenate([b2[0], b2[0]])[:, None])
    shared["b2v"] = f(np.broadcast_to(np.concatenate([b2[1], b2[1]])[None, :], (128, 128)))
    rr = np.arange(64); kvh_r = rr // 32; sl_r = rr % 32; g_r = sl_r // 4; tok_r = sl_r % 4; used = sl_r < 16
    shared["Gm"] = f(((kvh_r[:, None] == kvh_r[None, :]) & (tok_r[:, None] == tok_r[None, :]) & used[:, None]).astype(np.float32))
    shared["SelT"] = f((np.arange(4)[:, None] == tok_r[None, :]).astype(np.float32))
    shared["Hsel"] = f((np.arange(8)[None, :] == (4 * kvh_r + np.minimum(g_r, 3))[:, None]).astype(np.float32))
    shared["CN"] = f(np.where(np.arange(4)[None, :] <= tok_r[:, None], 0.0, NEGB))
    shared["CW"] = f(np.where(np.arange(4)[None, :] > tok_r[:, None], 0.0, NEGB))
    ptab = np.asarray(inp["page_table"]).astype(np.int32)
    pe = np.asarray(inp["cmp_pe"])[0]
    pe_bc = np.broadcast_to(pe.transpose(1, 0, 2)[None, :, :, None, :], (2, 64, 2, 2, 64)).reshape(128, 2, 2, 64)
    shared["pe_bc"] = f(pe_bc)
    maps = []
    swin = np.asarray(inp["state_kv_win"])[0].reshape(32, 512, 256)
    for c in range(NCORES):
        xall = np.concatenate([xp[c], xs[4 * c:4 * c + 4].reshape(16, D)], axis=0)
        xT = f(xall.T.reshape(KC, 128, NT).transpose(1, 0, 2))
        call = np.concatenate([cp_[c:c + 1], cs_[4 * c:4 * c + 4]], axis=0)
        cT = f(call.T.reshape(KC, 128, 5).transpose(1, 0, 2))
        m = dict(shared)
        m["xT"] = xT
        m["cT"] = cT
        m["state_win"] = f(swin[4 * c:4 * c + 4])
        m["pt_bc"] = np.ascontiguousarray(np.broadcast_to(ptab[4 * c:4 * c + 4][:, None, :], (4, 128, 128)), dtype=np.int32)
        m["sprevT"] = f(sconv[4 * c:4 * c + 4].reshape(4, 2, 44, 128).transpose(3, 2, 0, 1))
        maps.append(m)
    return maps


def kernel(**inp):
    if "nc" not in _NC_CACHE:
        _NC_CACHE["nc"] = build_program()
    nc = _NC_CACHE["nc"]
    maps = _prep_inputs(inp)
    res = run_bass_kernel_spmd(nc, maps, core_ids=list(range(NCORES)))
    R = res.results
    y_prompt = np.zeros((8, 2048, 1024), np.float32)
    y_sample = np.zeros((32, 4, 1024), np.float32)
    kv_prompt = np.zeros((1, 8, 2048, 4, 2, 64), np.float32)
    kv_sample = np.zeros((1, 32, 4, 4, 2, 64), np.float32)
    win_prompt = np.zeros((1, 8, 512, 2, 2, 64), np.float32)
    win_sample = np.zeros((1, 32, 512, 2, 2, 64), np.float32)
    v_chunk = np.zeros((1, 32, 4, 512), np.float32)
    conv_prompt = np.zeros((1, 8, 2, 5632), np.float32)
    conv_sample = np.zeros((1, 32, 2, 5632), np.float32)
    for c in range(NCORES):
        r = R[c]
        kv = r["kv_tok"]
        kv_prompt[0, c] = kv[:TP].reshape(2048, 4, 2, 64)
        kv_sample[0, 4 * c:4 * c + 4] = kv[TP:].reshape(4, 4, 4, 2, 64)
        win_prompt[0, c] = r["win_p"].reshape(512, 2, 2, 64)
        win_sample[0, 4 * c:4 * c + 4] = r["win_s"].reshape(4, 512, 2, 2, 64)
        v_chunk[0, 4 * c:4 * c + 4] = r["vchunk"].reshape(4, 4, 512)
        yT = r["yT"].transpose(2, 1, 0).reshape(NT, D)
        y_prompt[c] = yT[:TP]
        y_sample[4 * c:4 * c + 4] = yT[TP:].reshape(4, 4, D)
        cv = r["convT"].reshape(128, 44, 10)
        conv_prompt[0, c] = cv[:, :, 0:2].transpose(2, 1, 0).reshape(2, 5632)
        conv_sample[0, 4 * c:4 * c + 4] = cv[:, :, 2:10].reshape(128, 44, 4, 2).transpose(2, 3, 1, 0).reshape(4, 2, 5632)
    return (y_prompt, y_sample, kv_prompt, kv_sample, win_prompt, win_sample, v_chunk, conv_prompt, conv_sample)
```

```python
import numpy as np
from contextlib import ExitStack
import concourse.bass as bass
import concourse.mybir as mybir
from concourse.bass_utils import run_bass_kernel_spmd

F32 = mybir.dt.float32
BF16 = mybir.dt.bfloat16
I32 = mybir.dt.int32
AF = mybir.ActivationFunctionType
ALU = mybir.AluOpType
AX = mybir.AxisListType

NCORES = 8
D = 1024
KC = 8
TP = 2048
NS = 4
TS = 4
NT = TP + NS * TS
IN_COLS = 4376
DFF = 2816
NPG = 128
EPS = 1e-6
NEGB = -30000.0
TBS = [(0, 512), (512, 512), (1024, 512), (1536, 512), (2048, 16)]
TTS = [(i * 128, 128) for i in range(16)] + [(TP + s * TS, TS) for s in range(NS)]


class Tl:
    __slots__ = ("name", "w", "r", "ps")

    def __init__(self, name="", ps=False):
        self.name = name
        self.w = None
        self.r = []
        self.ps = ps


class Sched:
    ENG = ("pe", "act", "dve", "pool", "sp")

    def __init__(self, nc, n_dma_sems=(32, 4, 24)):
        self.nc = nc
        self.sems = {}
        self._stack = []
        for e in self.ENG:
            self.sems[e] = self._sem("s_" + e)
        self.seq = {e: 0 for e in self.ENG}
        self.epoch = {e: 0 for e in self.ENG}
        self.cur = {e: e for e in self.ENG}
        self.known = {e: {} for e in self.ENG}
        self.lists = {e: [] for e in self.ENG}
        self.dpool = {}
        for q, n in zip(("sp", "act", "pool"), n_dma_sems):
            self.dpool[q] = dict(keys=[], cnt=[], nxt=0)
            for i in range(n):
                k = "d_%s_%d" % (q, i)
                self.sems[k] = self._sem(k)
                self.dpool[q]["keys"].append(k)
                self.dpool[q]["cnt"].append(0)
        self.out_events = []

    def _sem(self, name):
        cm = self.nc.semaphore(name)
        s = cm.__enter__()
        self._stack.append(cm)
        return s

    def close(self):
        for cm in reversed(self._stack):
            cm.__exit__(None, None, None)

    def _deps(self, e, reads, writes):
        need = {}
        for t in reads:
            if t.w is not None:
                k, v = t.w
                if need.get(k, 0) < v:
                    need[k] = v
            if t.ps:
                for (k, v) in t.r:
                    if k.split("#")[0] != e and need.get(k, 0) < v:
                        need[k] = v
        for t in writes:
            if t.w is not None:
                k, v = t.w
                if need.get(k, 0) < v:
                    need[k] = v
            for (k, v) in t.r:
                if need.get(k, 0) < v:
                    need[k] = v
        waits = []
        kn = self.known[e]
        for k, v in need.items():
            if e == "pe" and k.split("#")[0] == "pe":
                continue
            if kn.get(k, 0) >= v:
                continue
            kn[k] = v
            waits.append((k, v))
        return waits

    def _mark(self, ev, reads, writes):
        for t in reads:
            t.r.append(ev)
            if len(t.r) > 64:
                mx = {}
                for k, v in t.r:
                    if mx.get(k, 0) < v:
                        mx[k] = v
                t.r = list(mx.items())
        for t in writes:
            t.w = ev
            t.r = []

    def op(self, e, fn, reads=(), writes=()):
        waits = self._deps(e, reads, writes)
        if self.seq[e] >= 6000:
            self.epoch[e] += 1
            self.cur[e] = "%s#%d" % (e, self.epoch[e])
            self.sems[self.cur[e]] = self._sem("s_%s_%d" % (e, self.epoch[e]))
            self.seq[e] = 0
        self.seq[e] += 1
        ev = (self.cur[e], self.seq[e])
        self.lists[e].append(("op", waits, fn, self.cur[e]))
        self._mark(ev, reads, writes)
        return ev

    def dma(self, q, fn, reads=(), writes=(), is_output=False):
        waits = self._deps(q, reads, writes)
        p = self.dpool[q]
        i = p["nxt"]
        p["nxt"] = (i + 1) % len(p["keys"])
        k = p["keys"][i]
        prev = p["cnt"][i]
        if prev > 0 and self.known[q].get(k, 0) < prev:
            waits.append((k, prev))
            self.known[q][k] = prev
        p["cnt"][i] = prev + 16
        ev = (k, prev + 16)
        self.lists[q].append(("dma", waits, fn, k))
        self._mark(ev, reads, writes)
        if is_output:
            self.out_events.append(ev)
        return ev

    def wait_all_outputs(self, e="sp"):
        need = {}
        for k, v in self.out_events:
            if need.get(k, 0) < v:
                need[k] = v
        waits = [(k, v) for k, v in need.items() if self.known[e].get(k, 0) < v]
        for k, v in waits:
            self.known[e][k] = v
        self.lists[e].append(("wait", waits))
        self.out_events = []

    def flush(self):
        dw = []
        for q, p in self.dpool.items():
            for kk, cnt in zip(p["keys"], p["cnt"]):
                if cnt > 0 and self.known["sp"].get(kk, 0) < cnt:
                    dw.append((kk, cnt))
                    self.known["sp"][kk] = cnt
        self.lists["sp"].append(("wait", dw))
        lists = self.lists
        self.lists = {e: [] for e in self.ENG}
        sems = self.sems
        with self.nc.Block() as block:
            def mk(e):
                def body(eng):
                    for item in lists[e]:
                        for (k, v) in item[1]:
                            eng.wait_ge(sems[k], v)
                        if item[0] == "op":
                            item[2](eng).then_inc(sems[item[3]], 1)
                        elif item[0] == "dma":
                            item[2](eng).then_inc(sems[item[3]], 16)
                return body
            block.tensor(mk("pe"))
            block.scalar(mk("act"))
            block.vector(mk("dve"))
            block.gpsimd(mk("pool"))
            block.sync(mk("sp"))


class K:
    def __init__(self, nc):
        self.nc = nc
        self.S = Sched(nc)
        self.dram = {}
        self._evac = 0

    def din(self, name, shape, dt=F32):
        t = self.nc.dram_tensor(name, list(shape), dt, kind="ExternalInput")
        self.dram[name] = t
        return t.ap()

    def dout(self, name, shape, dt=F32):
        t = self.nc.dram_tensor(name, list(shape), dt, kind="ExternalOutput")
        self.dram[name] = t
        return t.ap()

    def mm(self, out, lhsT, rhs, start, stop, R, W, sgc=False):
        self.S.op("pe", lambda e: e.matmul(out, lhsT=lhsT, rhs=rhs, start=start, stop=stop, skip_group_check=sgc), R, W)

    def tr(self, out, in_, ident, R, W):
        self.S.op("pe", lambda e: e.transpose(out=out, in_=in_, identity=ident), R, W)

    def act(self, out, in_, func, R, W, bias=None, scale=None, accum_out=None):
        kw = {}
        if bias is not None:
            kw["bias"] = bias
        if scale is not None:
            kw["scale"] = scale
        if accum_out is not None:
            kw["accum_out"] = accum_out
        self.S.op("act", lambda e: e.activation(out=out, in_=in_, func=func, **kw), R, W)

    def tt(self, eng, out, in0, in1, op, R, W):
        self.S.op(eng, lambda e: e.tensor_tensor(out=out, in0=in0, in1=in1, op=op), R, W)

    def ts(self, eng, out, in0, s1, s2, op0, op1, R, W):
        if op1 is None:
            self.S.op(eng, lambda e: e.tensor_scalar(out=out, in0=in0, scalar1=s1, scalar2=None, op0=op0), R, W)
        else:
            self.S.op(eng, lambda e: e.tensor_scalar(out=out, in0=in0, scalar1=s1, scalar2=s2, op0=op0, op1=op1), R, W)

    def stt(self, out, in0, scalar, in1, op0, op1, R, W):
        self.S.op("dve", lambda e: e.scalar_tensor_tensor(out=out, in0=in0, scalar=scalar, in1=in1, op0=op0, op1=op1), R, W)

    def cp(self, eng, out, in_, R, W):
        if eng == "act":
            self.S.op("act", lambda e: e.copy(out=out, in_=in_), R, W)
        else:
            self.S.op(eng, lambda e: e.tensor_copy(out=out, in_=in_), R, W)

    def evac(self, out, in_, R, W):
        self._evac ^= 1
        self.cp("act" if self._evac else "dve", out, in_, R, W)

    def memset(self, eng, ap, val, W):
        self.S.op(eng, lambda e: e.memset(ap, val), (), W)

    def ld(self, out, in_, W, R=(), q="sp"):
        self.S.dma(q, lambda e: e.dma_start(out=out, in_=in_), R, W)

    def ldc(self, out, in_, W, R=()):
        self.S.dma("pool", lambda e: e.dma_start(out=out, in_=in_), R, W)

    def st(self, out, in_, R, q="sp"):
        self.S.dma(q, lambda e: e.dma_start(out=out, in_=in_), R, (), is_output=True)


import os
STOP = os.environ.get('KSTOP', '')
DCUT = int(os.environ.get('DCUT', '9'))


def build_program():
    nc = bass.Bass("TRN2", target_bir_lowering=False)
    k = K(nc)
    S = k.S
    xT_d = k.din("xT", [128, KC, NT])
    cT_d = k.din("cT", [128, KC, 5])
    wada_d = k.din("w_ada", [D, 6 * D])
    bada_d = k.din("b_adaT", [128, 48])
    g1_d = k.din("g1T", [128, KC])
    g2_d = k.din("g2T", [128, KC])
    gf_d = k.din("gfT", [128, KC])
    win_d = k.din("w_in", [D, IN_COLS])
    lng_d = k.din("ln_g_bc", [128, 512])
    lnb_d = k.din("ln_b_bc", [128, 512])
    ident_d = k.din("ident", [128, 128])
    pe_d = k.din("pe_bc", [128, 2, 2, 64])
    swin_d = k.din("state_win", [NS, 512, 256])

    wsT_d = k.din("wsT", [128, 4, 128])
    wss_d = k.din("wssT", [4, 4, 4])
    bs_d = k.din("bs", [1, 4, 128])
    b1r_d = k.din("b1r", [128, 2])
    b2r_d = k.din("b2r", [128, 2])
    w1_d = k.din("cmp_w1", [2, 64, 64, 64])
    w2_d = k.din("cmp_w2", [2, 64, 64])
    msk_d = k.din("msk", [128, 4, 16, 32])
    E_d = k.din("Emat", [64, 2048])
    Cm_d = k.din("Cm", [128, 2, 128])
    wba_d = k.din("w_branch_a", [512, D])
    wbb_d = k.din("w_branch_b", [512, D])
    wout_d = k.din("w_out", [D, D])
    wup_d = k.din("w_up", [D, 2 * DFF])
    wdn_d = k.din("w_down", [DFF, D])
    wcv_d = k.din("w_convT", [128, 44, 3])
    bcv_d = k.din("b_convT", [128, 44])
    sprev_d = k.din("sprevT", [128, 44, 4, 2])
    cache_d = k.din("cache2d", [5120 * 128, 512]) if not STOP else None
    pt_d = k.din("pt_bc", [NS, 128, 128], I32)
    Gm_d = k.din("Gm", [64, 64]); SelT_d = k.din("SelT", [4, 64]); Hsel_d = k.din("Hsel", [64, 8])
    CN_d = k.din("CN", [64, 4]); CW_d = k.din("CW", [64, 4])
    b2k_d = k.din("b2k", [128, 1]); b2v_d = k.din("b2v", [128, 128])
    yT_o = k.dout("yT", [128, KC, NT])
    conv_o = k.dout("convT", [128, 440])
    kv_o = k.dout("kv_tok", [NT, 512])
    winp_o = k.dout("win_p", [512, 256])
    wins_o = k.dout("win_s", [NS, 512, 256])
    vch_o = k.dout("vchunk", [NS * TS, 512])

    with ExitStack() as top:
        def sb(name, shape, dt, es=top):
            return es.enter_context(nc.sbuf_tensor("sb_" + name, list(shape), dt))

        PS = [top.enter_context(nc.psum_tensor("ps%d" % i, [128, 512], F32)) for i in range(8)]
        PT = [Tl("ps%d" % i, ps=True) for i in range(8)]
        psi = [0]

        def nextps():
            i = psi[0]
            psi[0] = (i + 1) % 8
            return PS[i], PT[i]

        ident_f = sb("ident_f", [128, 128], F32); T_identf = Tl()
        ident_b = sb("ident_b", [128, 128], BF16); T_identb = Tl()
        ones_b = sb("ones_b", [128, 128], BF16); T_ones = Tl()
        modT = sb("modT", [128, 48, 5], F32); T_mod = Tl()
        A1 = sb("A1", [128, KC, 5], F32); T_A1 = Tl()
        A2 = sb("A2", [128, KC, 5], F32); T_A2 = Tl()
        hT = sb("hT", [128, KC, NT], BF16)
        T_h = [[Tl() for _ in TBS] for _ in range(KC)]
        x1raw = sb("x1raw", [128, KC * NT], F32)
        mid_scope = top.enter_context(ExitStack())
        uT = sb("uT", [128, 4, NT], BF16, mid_scope)
        T_u = [[Tl() for _ in TBS] for _ in range(4)]
        x1T = x1raw[:, :].rearrange("p (k t) -> p k t", k=KC)
        xbv = x1raw[:, :].bitcast(BF16)
        T_x1 = [[Tl() for _ in TBS] for _ in range(KC)]

        k.ld(ident_f[:], ident_d[:, :], [T_identf])
        k.cp("dve", ident_b[:], ident_f[:], [T_identf], [T_identb])
        k.memset("pool", ones_b[:], 1.0, [T_ones])

        with ExitStack() as pa:
            cs = sb("cs", [128, KC, 5], F32, pa); T_cs = Tl()
            bT = sb("bT", [128, 48], F32, pa); T_bT = Tl()
            g1 = sb("g1", [128, KC], F32, pa); T_g1 = Tl()
            g2 = sb("g2", [128, KC], F32, pa); T_g2 = Tl()
            wab = [sb("wab%d" % i, [128, KC, 512], F32, pa) for i in range(2)]
            T_wab = [Tl(), Tl()]
            xa = x1T; T_xa = [Tl() for _ in range(KC)]
            sq = [sb("sq%d" % i, [128, NT], BF16, pa) for i in range(2)]; T_sq = [Tl(), Tl()]
            rstd = sb("rstd", [128, NT], F32, pa); T_rstd = [Tl() for _ in TBS]
            tmp = [sb("tmpA%d" % i, [128, NT], F32, pa) for i in range(2)]; T_tmp = [Tl(), Tl()]

            k.ld(cs[:], cT_d[:, :, :], [T_cs])
            k.ld(bT[:], bada_d[:, :], [T_bT])
            k.ld(g1[:], g1_d[:, :], [T_g1])
            k.ld(g2[:], g2_d[:, :], [T_g2])
            k.act(cs[:], cs[:], AF.Silu, [T_cs], [T_cs])
            for kc in range(KC):
                k.ld(xa[:, kc, :], xT_d[:, kc, :], [T_xa[kc]])
            psm, T_psm = PS[7], PT[7]
            wv = wada_d.rearrange("(kc p) n -> p kc n", p=128)
            for jb in range(12):
                w = wab[jb % 2]; Tw = T_wab[jb % 2]
                for kc in range(KC):
                    k.ld(w[:, kc, :], wv[:, kc, jb * 512:(jb + 1) * 512], [Tw], q="sp")
                for j in range(4):
                    col = (jb * 4 + j) * 5
                    for kc in range(KC):
                        k.mm(psm[:, col:col + 5], w[:, kc, j * 128:(j + 1) * 128], cs[:, kc, :],
                             kc == 0, kc == KC - 1, [Tw, T_cs], [T_psm])
            k.tt("dve", modT[:], psm[:, 0:240].rearrange("p (j r) -> p j r", r=5),
                 bT[:, :].unsqueeze(2).to_broadcast([128, 48, 5]), ALU.add, [T_psm, T_bT], [T_mod])
            for (Ax, TAx, gx, Tgx, off) in ((A1, T_A1, g1, T_g1, 8), (A2, T_A2, g2, T_g2, 32)):
                k.ts("dve", Ax[:], modT[:, off:off + 8, :], 1.0, None, ALU.add, None, [T_mod], [TAx])
                k.tt("dve", Ax[:], Ax[:], gx[:, :].unsqueeze(2).to_broadcast([128, KC, 5]), ALU.mult, [TAx, Tgx], [TAx])
            for kc in range(KC):
                s_ = sq[kc % 2]; Ts = T_sq[kc % 2]
                k.act(s_[:], xa[:, kc, :], AF.Square, [T_xa[kc]], [Ts])
                for bi, (t0, n) in enumerate(TBS):
                    k.mm(PS[bi][:, 0:n], ones_b[:], s_[:, t0:t0 + n], kc == 0, kc == KC - 1, [T_ones, Ts], [PT[bi]])
            for bi, (t0, n) in enumerate(TBS):
                k.act(rstd[:, t0:t0 + n], PS[bi][:, 0:n], AF.Sqrt, [PT[bi]], [T_rstd[bi]], bias=EPS, scale=1.0 / D)
                k.S.op("dve", (lambda o: (lambda e: e.reciprocal(out=o, in_=o)))(rstd[:, t0:t0 + n]), [T_rstd[bi]], [T_rstd[bi]])
            for kc in range(KC):
                t_ = tmp[kc % 2]; Tt = T_tmp[kc % 2]
                k.tt("dve", t_[:], xa[:, kc, :], rstd[:], ALU.mult, [T_xa[kc]] + T_rstd, [Tt])
                k.act(hT[:, kc, 0:TP], t_[:, 0:TP], AF.Identity, [Tt, T_A1, T_mod], T_h[kc][0:4],
                      bias=modT[:, kc, 0:1], scale=A1[:, kc, 0:1])
                for s in range(NS):
                    c0 = TP + s * TS
                    k.act(hT[:, kc, c0:c0 + TS], t_[:, c0:c0 + TS], AF.Identity, [Tt, T_A1, T_mod], [T_h[kc][4]],
                          bias=modT[:, kc, 1 + s:2 + s], scale=A1[:, kc, 1 + s:2 + s])
            S.flush()

        with ExitStack() as pb_:
            att = pb_
            vn = xbv[:, 0:10240].rearrange("p (t c) -> p t c", t=20); T_vn = [Tl() for _ in TTS]
            gates = sb("gates", [128, 20, 24], F32, att); T_gates = [Tl() for _ in TTS]
            pe_bc = sb("pe_bc_sb", [128, 2, 2, 64], F32, att); T_pe = Tl()
            qTs = sb("qTs", [128, 4, 16], BF16, att); T_qTs = Tl()
            kTs = sb("kTs", [128, 4, 16], BF16, att); T_kTs = Tl()
            vnew = sb("vnew", [4, NS, 2, 130], BF16, att); T_vnew = Tl()
            pr = att.enter_context(ExitStack())
            qT = sb("qT", [128, 4, NT], BF16, pr); T_q = [[Tl() for _ in TBS] for _ in range(4)]
            kT = sb("kT", [128, 4, NT], BF16, pr); T_k = [[Tl() for _ in TBS] for _ in range(4)]
            vsel = sb("vsel", [128, 20, 2, 65], BF16, pr); T_vsel = [Tl() for _ in TTS]
            vwin = sb("vwin", [128, 20, 2, 65], BF16, pr); T_vwin = [Tl() for _ in TTS]
            Xc = xbv[:, 10240:14336].rearrange("p (t s c) -> p t s c", t=16, s=2); T_Xc = [Tl() for _ in range(16)]
            lng = sb("lng", [128, 512], F32, pr); T_lng = Tl()
            lnb = sb("lnb", [128, 512], F32, pr); T_lnb = Tl()
            k.ld(pe_bc[:], pe_d[:, :, :, :], [T_pe])
            k.ld(lng[:], lng_d[:, :], [T_lng])
            k.ld(lnb[:], lnb_d[:, :], [T_lnb])
            k.memset("pool", vsel[:], 1.0, T_vsel)
            k.memset("pool", vwin[:], 1.0, T_vwin)

            with ExitStack() as pb:
                wu = sb("wu", [128, KC, 512], BF16, pb); T_wu = Tl()
                wq = sb("wq", [128, KC, 512], BF16, pb); T_wq = Tl()
                wvv = wq; T_wv = T_wq
                wkd = wu[:, :, :].rearrange("p k (j u d) -> p k j u d", j=4, u=2); T_wkd = T_wu
                wkv = sb("wkv", [128, KC, 792], BF16, pb); T_wkv = Tl()
                vg = [sb("vg%d" % i, [128, 512], F32, pb) for i in range(2)]; T_vg = [Tl(), Tl()]
                vt = [sb("vt%d" % i, [128, 512], F32, pb) for i in range(2)]; T_vt = [Tl(), Tl()]
                st6 = [sb("st6%d" % i, [128, 8], F32, pb) for i in range(2)]; T_st6 = [Tl(), Tl()]
                mv = [sb("mv%d" % i, [128, 4], F32, pb) for i in range(2)]; T_mv = [Tl(), Tl()]
                kvo = [sb("kvo0", [128, 768], F32, pb)] * 2; T_kvo = [Tl()] * 2

                wi = win_d.rearrange("(kc p) n -> p kc n", p=128)
                k.ldc(wu[:], wi[:, :, 0:512], [T_wu])
                k.ldc(wq[:], wi[:, :, 1024:1536], [T_wq])
                k.ldc(wkv[:], wi[:, :, 1536:2328], [T_wkv])

                def fm_proj(wt, Tw, nch, wsl, evac):
                    for m in range(nch):
                        for bi, (t0, n) in enumerate(TBS):
                            ps, Tp = nextps()
                            for kc in range(KC):
                                k.mm(ps[:, 0:n], wsl(wt, kc, m), hT[:, kc, t0:t0 + n], kc == 0, kc == KC - 1,
                                     [Tw, T_h[kc][bi]], [Tp])
                            evac(m, bi, t0, n, ps, Tp)

                fm_proj(wu, T_wu, 4, lambda wt, kc, m: wt[:, kc, m * 128:(m + 1) * 128],
                        lambda m, bi, t0, n, ps, Tp: k.act(uT[:, m, t0:t0 + n], ps[:, 0:n], AF.Gelu_apprx_tanh, [Tp], [T_u[m][bi]]))
                fm_proj(wq, T_wq, 4, lambda wt, kc, m: wt[:, kc, m * 128:(m + 1) * 128],
                        lambda m, bi, t0, n, ps, Tp: k.act(qT[:, m, t0:t0 + n], ps[:, 0:n], AF.Copy, [Tp], [T_q[m][bi]], scale=0.125))
                for j, (slot, kvh) in enumerate(((2, 0), (2, 1), (4, 0), (4, 1))):
                    c0 = 1536 + slot * 128 + kvh * 64
                    for dup in range(2):
                        k.ldc(wkd[:, :, j, dup, :], wi[:, :, c0:c0 + 64], [T_wkd])
                fm_proj(wkd, T_wkd, 4, lambda wt, kc, m: wt[:, kc, m, :, :],
                        lambda m, bi, t0, n, ps, Tp: k.evac(kT[:, m, t0:t0 + n], ps[:, 0:n], [Tp], [T_k[m][bi]]))

                k.ldc(wvv[:], wi[:, :, 512:1024], [T_wv])
                def tb_of(t0):
                    return min(t0 // 512, 4)

                for ti, (t0, n) in enumerate(TTS):
                    bi = tb_of(t0)
                    ps, Tp = nextps()
                    for kc in range(KC):
                        k.mm(ps[0:n, :], hT[:, kc, t0:t0 + n], wvv[:, kc, :], kc == 0, kc == KC - 1, [T_wv, T_h[kc][bi]], [Tp])
                    g_ = vg[ti % 2]; Tg = T_vg[ti % 2]
                    t_ = vt[ti % 2]; Tt = T_vt[ti % 2]
                    s6 = st6[ti % 2]; Ts6 = T_st6[ti % 2]
                    m_ = mv[ti % 2]; Tm = T_mv[ti % 2]
                    k.act(g_[0:n, :], ps[0:n, :], AF.Gelu_apprx_tanh, [Tp], [Tg])
                    S.op("dve", (lambda o, i: (lambda e: e.bn_stats(out=o, in_=i)))(s6[0:n, 0:6], g_[0:n, :]), [Tg], [Ts6])
                    S.op("dve", (lambda o, i: (lambda e: e.bn_aggr(out=o, in_=i)))(m_[0:n, 0:2], s6[0:n, 0:6]), [Ts6], [Tm])
                    k.act(m_[0:n, 2:3], m_[0:n, 1:2], AF.Sqrt, [Tm], [Tm], bias=EPS, scale=1.0)
                    S.op("dve", (lambda o, i: (lambda e: e.reciprocal(out=o, in_=i)))(m_[0:n, 3:4], m_[0:n, 2:3]), [Tm], [Tm])
                    k.ts("dve", t_[0:n, :], g_[0:n, :], m_[0:n, 0:1], m_[0:n, 3:4], ALU.subtract, ALU.mult, [Tg, Tm], [Tt])
                    k.tt("dve", t_[0:n, :], t_[0:n, :], lng[0:n, :], ALU.mult, [Tt, T_lng], [Tt])
                    k.tt("dve", t_[0:n, :], t_[0:n, :], lnb[0:n, :], ALU.add, [Tt, T_lnb], [Tt])
                    k.cp("pool", vn[0:n, ti, :], t_[0:n, :], [Tt], [T_vn[ti]])
                    if ti >= 16:
                        s = ti - 16
                        k.st(vch_o[s * TS:(s + 1) * TS, :], t_[0:n, :], [Tt])
                    psa, Tpa = nextps()
                    psb, Tpb = nextps()
                    for kc in range(KC):
                        k.mm(psa[0:n, :], hT[:, kc, t0:t0 + n], wkv[:, kc, 0:512], kc == 0, kc == KC - 1, [T_wkv, T_h[kc][bi]], [Tpa])
                    for kc in range(KC):
                        k.mm(psb[0:n, 0:280], hT[:, kc, t0:t0 + n], wkv[:, kc, 512:792], kc == 0, kc == KC - 1, [T_wkv, T_h[kc][bi]], [Tpb])
                    o_ = kvo[ti % 2]; To = T_kvo[ti % 2]
                    k.cp("act", o_[0:n, 0:512], psa[0:n, :], [Tpa], [To])
                    k.cp("act", o_[0:n, 512:768], psb[0:n, 0:256], [Tpb], [To])
                    k.st(kv_o[t0:t0 + n, :], o_[0:n, 0:512], [To])
                    if ti < 16:
                        k.tt("dve", Xc[:, ti, :, :].rearrange("p s (k d) -> p s k d", k=2),
                             psa[:, 0:256].rearrange("p (s k d) -> p s k d", s=2, k=2), pe_bc[:], ALU.add, [Tpa, T_pe], [T_Xc[ti]])
                    k.cp("dve", vsel[0:n, ti, :, 0:64], psa[0:n, 384:512].rearrange("p (k d) -> p k d", k=2), [Tpa], [T_vsel[ti]])
                    k.cp("dve", vwin[0:n, ti, :, 0:64], psb[0:n, 128:256].rearrange("p (k d) -> p k d", k=2), [Tpb], [T_vwin[ti]])
                    k.act(gates[0:n, ti, :], psb[0:n, 256:280], AF.Sigmoid, [Tpb], [T_gates[ti]])
                    if 12 <= ti < 16:
                        r0 = (ti - 12) * 128
                        k.st(winp_o[r0:r0 + 128, :], o_[:, 512:768], [To])
                    if ti >= 16:
                        s = ti - 16
                        k.st(wins_o[s, 512 - TS:512, :], o_[0:n, 512:768], [To])
                for s in range(NS):
                    k.S.dma("sp", (lambda s: (lambda e: e.dma_start(out=wins_o[s, 0:512 - TS, :], in_=swin_d[s, TS:512, :])))(s), (), (), is_output=True)
                S.flush()

            kcT = sb("kcT", [128, 2, 16, 2], BF16, pr); T_kc = Tl()
            vcT = sb("vcT", [128, 2, 16, 2], BF16, pr); T_vcT = Tl()
            vc = sb("vc", [32, 2, 64], BF16, pr); T_vc = Tl()
            PSb = [PS[i][:, :].bitcast(BF16) for i in range(8)]

            def tb_of(t0):
                return min(t0 // 512, 4)

            with ExitStack() as pc:
                Wd = xbv[:, 14336:14336 + 16384].rearrange("p (s d m) -> p s d m", s=2, d=64); T_Wd = Tl()
                wsT_f = sb("wsT_f", [128, 4, 128], F32, pc); T_wsf = Tl()
                wsT = sb("wsT_b", [128, 4, 128], BF16, pc); T_ws = Tl()
                wss_f = sb("wss_f", [4, 4, 4], F32, pc); T_wssf = Tl()
                wss = sb("wss", [4, 4, 4], BF16, pc); T_wss = Tl()
                bs_f = sb("bs_f", [1, 4, 128], F32, pc); T_bs = Tl()
                ones_f = sb("ones_f", [1, 128], F32, pc); T_onesf = Tl()
                W2d = sb("W2d", [128, 2, 2, 128], BF16, pc); T_W2d = Tl()
                b1r = sb("b1r", [128, 2], F32, pc); b2r = sb("b2r", [128, 2], F32, pc); T_b12 = Tl()
                hidT = sb("hidT", [128, 2, 32], BF16, pc); T_hid = Tl()

                k.ld(wsT_f[:], wsT_d[:, :, :], [T_wsf])
                k.ld(wss_f[:], wss_d[:, :, :], [T_wssf])
                k.ld(bs_f[:], bs_d[:, :, :], [T_bs])
                k.ld(b1r[:], b1r_d[:, :], [T_b12])
                k.ld(b2r[:], b2r_d[:, :], [T_b12])
                k.memset("dve", ones_f[:], 1.0, [T_onesf])
                S.op("pool", lambda e: e.affine_select(out=wsT[:], in_=wsT_f[:], pattern=[[0, 4], [1, 128]], compare_op=ALU.is_ge,
                                                       fill=0.0, base=0, channel_multiplier=-1), [T_wsf], [T_ws])
                S.op("pool", lambda e: e.affine_select(out=wss[:], in_=wss_f[:], pattern=[[0, 4], [1, 4]], compare_op=ALU.is_ge,
                                                       fill=0.0, base=0, channel_multiplier=-1), [T_wssf], [T_wss])
                k.memset("pool", Wd, 0.0, [T_Wd])
                k.memset("pool", W2d[:], 0.0, [T_W2d])
                for sl in range(2):
                    for half in range(2):
                        k.ldc(Wd[half * 64:(half + 1) * 64, sl, :, half * 64:(half + 1) * 64], w1_d[sl, :, :, :], [T_Wd])
                    for parity in range(2):
                        for dup in range(2):
                            k.ldc(W2d[parity * 64:(parity + 1) * 64, sl, parity, dup * 64:(dup + 1) * 64], w2_d[sl, :, :], [T_W2d])

                for ti, (t0, n) in enumerate(TTS):
                    bi = tb_of(t0)
                    ps, Tp = nextps()
                    for g in range(4):
                        if ti < 16:
                            k.mm(ps[:, g * 128:(g + 1) * 128], vn[:, ti, g * 128:(g + 1) * 128], wsT[:, g, :], True, False, [T_vn[ti], T_ws], [Tp])
                            k.mm(ps[:, g * 128:(g + 1) * 128], ones_f[0:1, :], bs_f[0:1, g, :], False, True, [T_onesf, T_bs], [Tp])
                        else:
                            k.mm(ps[:, g * 128:g * 128 + 4], vn[0:4, ti, g * 128:(g + 1) * 128], wss[0:4, g, :], True, False, [T_vn[ti], T_wss], [Tp])
                            k.mm(ps[:, g * 128:g * 128 + 4], ones_f[0:1, :], bs_f[0:1, g, 0:4], False, True, [T_onesf, T_bs], [Tp])
                    uv = uT[:, :, t0:t0 + n]
                    Tus = [T_u[g][bi] for g in range(4)]
                    k.tt("dve", uv, uv, ps[:, :].rearrange("p (g t) -> p g t", g=4)[:, :, 0:n], ALU.mult, [Tp] + Tus, Tus)

                Xc5 = Xc.rearrange("p t s (k d) -> p t s k d", k=2)
                for sl in range(2):
                    ps, Tp = nextps()
                    for d in range(64):
                        k.mm(ps[:, 0:32].rearrange("p (t k) -> p t k", k=2), Wd[:, sl, d, :], Xc5[:, :, sl, :, d], d == 0, d == 63,
                             [T_Wd] + T_Xc, [Tp])
                    k.act(hidT[:, sl, :], ps[:, 0:32], AF.Gelu_apprx_tanh, [Tp, T_b12], [T_hid], bias=b1r[:, sl:sl + 1])
                    for parity in range(2):
                        ps2, Tp2 = nextps()
                        k.mm(ps2[:, 0:32], W2d[:, sl, parity, :], hidT[:, sl, :], True, True, [T_W2d, T_hid], [Tp2])
                        dst = (kcT if sl == 0 else vcT)
                        Td = T_kc if sl == 0 else T_vcT
                        k.act(dst[:, :, :, parity], ps2[:, 0:32].rearrange("p (t k) -> p k t", k=2), AF.Identity, [Tp2, T_b12], [Td],
                              bias=b2r[:, sl:sl + 1])
                for kvh in range(2):
                    pi_ = psi[0]
                    ps, Tp = nextps()
                    k.tr(PSb[pi_][0:32, 0:64], vcT[0:64, kvh, :, :].rearrange("p t q -> p (t q)"), ident_b[0:64, 0:64], [T_vcT, T_identb], [Tp])
                    k.cp("dve", vc[:, kvh, :], PSb[pi_][0:32, 0:64], [Tp], [T_vc])
                S.flush()
                if STOP == 'C':
                    S.wait_all_outputs("sp"); S.flush(); S.close(); return nc

            with ExitStack() as pd:
                ybT = xbv[:, 0:4 * NT].rearrange("p (c t) -> p c t", c=4); T_yb = [Tl() for _ in TBS]
                ybacc = sb("ybacc", [128, 4, 512], F32, pd); T_ya = [Tl() for _ in range(4)]
                msk = sb("msk", [128, 4, 16, 32], F32, pd); T_msk = Tl()
                Eb = sb("Eb", [64, 2048], BF16, pd); T_E = Tl()
                Cm = sb("Cm", [128, 2, 128], BF16, pd); T_C = Tl()
                penT = sb("penT", [64, 512], BF16, pd); T_penT = Tl()
                PTb = [sb("PTb%d" % i, [128, 512], BF16, pd) for i in range(4)]; T_PT = [Tl() for _ in range(4)]
                sm = sb("sm", [128, 8, 32], F32, pd); T_sm = Tl()
                ee = sb("ee", [128, 8, 32], F32, pd); T_ee = Tl()
                pp = sb("pp", [128, 8, 32], F32, pd); T_pp = Tl()
                pbf = sb("pbf", [128, 8, 32], BF16, pd); T_pbf = Tl()
                pT = sb("pT", [32, 8, 128], BF16, pd); T_pT = Tl()
                imp = sb("imp", [128, 2, 32], F32, pd); T_imp = Tl()
                t8 = sb("t8", [128, 16], F32, pd); T_t8 = Tl()
                wk8 = sb("wk8", [128, 32], F32, pd); T_wk8 = Tl()
                penf = sb("penf", [128, 2, 32], F32, pd); T_penf = Tl()
                pen = sb("pen", [128, 2, 32], BF16, pd); T_pen = Tl()
                mx = sb("mx", [128, 8], F32, pd); T_mx = Tl()
                rs = sb("rs", [128, 8], F32, pd); T_rs = Tl()
                rs4 = sb("rs4", [128, 4], F32, pd); T_rs4 = Tl()
                tmpo = sb("tmpo", [128, 4, 64], F32, pd); T_tmpo = Tl()
                ybb = sb("ybb", [128, 512], BF16, pd); T_ybb = Tl()

                k.ld(msk[:], msk_d[:, :, :, :], [T_msk])
                k.ldc(Eb[:], E_d[:, :], [T_E])
                k.ldc(Cm[:], Cm_d[:, :, :], [T_C])
                if 'NOS' in STOP:
                    k.memset("pool", ybT[:, :, TP:NT], 0.0, [T_yb[4]])

                rot = [4]

                def rps():
                    i = rot[0]
                    rot[0] = 4 + (i - 3) % 4
                    return i

                def B3(ap, sh):
                    return ap.to_broadcast(sh)

                for qb in range(4):
                    for j in range(4):
                        qt = 4 * qb + j
                        if DCUT < 1:
                            continue
                        piA = rps(); piB = rps()
                        for h in range(8):
                            base = (h % 2) * 64; ch = h // 2; kvh = h // 4
                            bk = piA if h % 2 == 0 else piB
                            k.mm(PS[bk][:, (h // 2) * 32:(h // 2 + 1) * 32], qT[base:base + 64, ch, qt * 128:(qt + 1) * 128],
                                 kcT[base:base + 64, kvh, :, :].rearrange("p t q -> p (t q)"), True, True, [T_q[ch][qb], T_kc], [PT[bk]])
                        for par, bk in ((0, piA), (1, piB)):
                            k.tt("dve", sm[:, par::2, :], PS[bk][:, 0:128].rearrange("p (h n) -> p h n", h=4),
                                 B3(msk[:, 0, qt, :].unsqueeze(1), [128, 4, 32]), ALU.add, [PT[bk], T_msk], [T_sm])
                        S.op("dve", lambda e: e.tensor_reduce(out=mx[:, 0:8], in_=sm[:], axis=AX.X, op=ALU.max), [T_sm], [T_mx])
                        k.tt("dve", sm[:], sm[:], B3(mx[:, 0:8].unsqueeze(2), [128, 8, 32]), ALU.subtract, [T_sm, T_mx], [T_sm])
                        k.act(ee[:], sm[:], AF.Exp, [T_sm], [T_ee])
                        k.tt("dve", ee[:], ee[:], B3(msk[:, 1, qt, :].unsqueeze(1), [128, 8, 32]), ALU.mult, [T_ee, T_msk], [T_ee])
                        S.op("dve", lambda e: e.tensor_reduce(out=rs[:, 0:8], in_=ee[:], axis=AX.X, op=ALU.add), [T_ee], [T_rs])
                        k.ts("dve", rs[:], rs[:], 1e-30, None, ALU.max, None, [T_rs], [T_rs])
                        S.op("dve", lambda e: e.reciprocal(out=rs[:], in_=rs[:]), [T_rs], [T_rs])
                        k.tt("dve", pp[:], ee[:], B3(rs[:, 0:8].unsqueeze(2), [128, 8, 32]), ALU.mult, [T_ee, T_rs], [T_pp])
                        k.cp("pool", pbf[:], pp[:], [T_pp], [T_pbf])
                        if qb >= 2 and 'NOTOPK' not in STOP:
                            S.op("dve", lambda e: e.tensor_reduce(out=imp[:], in_=pp[:].rearrange("p (k g) n -> p k n g", k=2), axis=AX.X, op=ALU.add),
                                 [T_pp], [T_imp])
                            k.tt("dve", imp[:], imp[:], B3(msk[:, 2, qt, :].unsqueeze(1), [128, 2, 32]), ALU.mult, [T_imp, T_msk], [T_imp])
                            k.tt("dve", imp[:], imp[:], B3(msk[:, 3, qt, :].unsqueeze(1), [128, 2, 32]), ALU.add, [T_imp, T_msk], [T_imp])
                            for kvh in range(2):
                                S.op("dve", (lambda kvh: lambda e: e.max(out=t8[:, 0:8], in_=imp[:, kvh, :]))(kvh), [T_imp], [T_t8])
                                S.op("dve", (lambda kvh: lambda e: e.match_replace(out=wk8[:], in_to_replace=t8[:, 0:8], in_values=imp[:, kvh, :], imm_value=-1e30))(kvh),
                                     [T_imp, T_t8], [T_wk8])
                                S.op("dve", lambda e: e.max(out=t8[:, 8:16], in_=wk8[:]), [T_wk8], [T_t8])
                                k.ts("dve", penf[:, kvh, :], imp[:, kvh, :], t8[:, 15:16], -NEGB, ALU.is_ge, ALU.mult, [T_imp, T_t8], [T_penf])
                            k.ts("dve", pen[:], penf[:], NEGB, None, ALU.add, None, [T_penf], [T_pen])
                            pi2 = rps()
                            k.tr(PSb[pi2][0:64, 0:128], pen[:].rearrange("p k n -> p (k n)"), ident_b[:], [T_pen, T_identb], [PT[pi2]])
                            k.cp("act", penT[:, j * 128:(j + 1) * 128], PSb[pi2][0:64, 0:128], [PT[pi2]], [T_penT])
                        if DCUT < 2:
                            continue
                        pi3 = rps()
                        for h in range(8):
                            k.tr(PSb[pi3][0:32, h * 128:(h + 1) * 128], pbf[:, h, :], ident_b[:], [T_pbf, T_identb], [PT[pi3]])
                        k.cp("act", pT[:].rearrange("p h q -> p (h q)"), PSb[pi3][0:32, 0:1024], [PT[pi3]], [T_pT])
                        if DCUT < 3:
                            continue
                        pi4 = rps(); pso, Tpo = PS[pi4], PT[pi4]
                        for h in range(8):
                            k.mm(pso[:, h * 64:(h + 1) * 64], pT[:, h, :], vc[:, h // 4, :], True, True, [T_pT, T_vc], [Tpo])
                        k.tt("dve", ybacc[:, j, :].rearrange("p (h d) -> p h d", h=8), pso[:, :].rearrange("p (h d) -> p h d", h=8),
                             B3(gates[:, qt, 0:8].unsqueeze(2), [128, 8, 64]), ALU.mult, [Tpo, T_gates[qt]], [T_ya[j]])

                    for kvh in range(2 if 'NOSEL' not in STOP else 0):
                        for bri, (vaug, T_va) in ((1, (vsel, T_vsel)), (2, (vwin, T_vwin))):
                            first = [True] * 4
                            kt_lo = 0 if bri == 1 else max(0, 4 * qb - 4)
                            q0 = qb * 512
                            km = (0 if bri == 1 else 2) + kvh
                            units = [(kt, hh) for kt in range(kt_lo, 4 * qb + 4) for hh in range(4)]

                            def scores(ui, kt, hh):
                                jlo = max(0, kt - 4 * qb)
                                jhi = 3 if bri == 1 else min(3, kt + 4 - 4 * qb)
                                c0, c1 = jlo * 128, (jhi + 1) * 128
                                h = 4 * kvh + hh
                                base = (h % 2) * 64; ch = h // 2
                                pi_ = rps(); ps, Tp = PS[pi_], PT[pi_]
                                grp = [(ps[:, c0:c1], kT[base:base + 64, km, kt * 128:(kt + 1) * 128], qT[base:base + 64, ch, q0 + c0:q0 + c1],
                                        [T_k[km][kt // 4], T_q[ch][qb]])]
                                if bri == 1 and qb >= 2:
                                    grp.append((ps[:, c0:c1], Eb[kvh * 32:(kvh + 1) * 32, kt * 128:(kt + 1) * 128], penT[kvh * 32:(kvh + 1) * 32, c0:c1],
                                                [T_E, T_penT]))
                                if kt >= 4 * qb:
                                    jd = kt - 4 * qb
                                    grp.append((ps[:, jd * 128:(jd + 1) * 128], ident_b[:], Cm[:, 0, :], [T_identb, T_C]))
                                if bri == 2 and 0 <= kt + 4 - 4 * qb <= 3:
                                    j4 = kt + 4 - 4 * qb
                                    grp.append((ps[:, j4 * 128:(j4 + 1) * 128], ident_b[:], Cm[:, 1, :], [T_identb, T_C]))
                                for gi, (o_, l_, r_, R_) in enumerate(grp):
                                    k.mm(o_, l_, r_, gi == 0, gi == len(grp) - 1, R_, [Tp])
                                pb_i = ui % 4
                                k.act(PTb[pb_i][:, c0:c1], ps[:, c0:c1], AF.Exp, [Tp], [T_PT[pb_i]])
                                return pb_i, jlo, jhi

                            def pvs(kt, hh, pb_i, jlo, jhi):
                                for j in range(jlo, jhi + 1):
                                    k.mm(PS[j][:, hh * 65:(hh + 1) * 65], PTb[pb_i][:, j * 128:(j + 1) * 128], vaug[:, kt, kvh, :],
                                         first[j], kt == 4 * qb + j, [T_PT[pb_i], T_va[kt]], [PT[j]], sgc=True)
                                    first[j] = False

                            LAG = 2
                            pend = []
                            for ui in range(len(units) + LAG):
                                if ui < len(units):
                                    pend.append(scores(ui, *units[ui]))
                                if ui >= LAG:
                                    pvs(*units[ui - LAG], *pend[ui - LAG])
                            for j in range(4):
                                qt = 4 * qb + j
                                po3 = PS[j][:, 0:260].rearrange("p (h c) -> p h c", c=65)
                                S.op("dve", (lambda po3: lambda e: e.reciprocal(out=rs4[:], in_=po3[:, :, 64]))(po3), [PT[j]], [T_rs4])
                                k.tt("dve", rs4[:], rs4[:], gates[:, qt, bri * 8 + 4 * kvh:bri * 8 + 4 * kvh + 4], ALU.mult, [T_rs4, T_gates[qt]], [T_rs4])
                                k.tt("dve", tmpo[:], po3[:, :, 0:64], B3(rs4[:, 0:4].unsqueeze(2), [128, 4, 64]), ALU.mult, [PT[j], T_rs4], [T_tmpo])
                                ya = ybacc[:, j, kvh * 256:(kvh + 1) * 256].rearrange("p (h d) -> p h d", h=4)
                                k.tt("dve", ya, ya, tmpo[:], ALU.add, [T_ya[j], T_tmpo], [T_ya[j]])
                    for j in range(4 if DCUT >= 4 else 0):
                        qt = 4 * qb + j
                        k.cp("act", ybb[:], ybacc[:, j, :], [T_ya[j]], [T_ybb])
                        pi_ = rps()
                        for c in range(4):
                            k.tr(PSb[pi_][:, c * 128:(c + 1) * 128], ybb[:, c * 128:(c + 1) * 128], ident_b[:], [T_ybb, T_identb], [PT[pi_]])
                        k.cp("dve", ybT[:, :, qt * 128:(qt + 1) * 128], PSb[pi_][:, 0:512].rearrange("p (c q) -> p c q", c=4), [PT[pi_]], [T_yb[qb]])
                k.cp("dve", qTs[:], qT[:, :, TP:NT], [T_q[m][4] for m in range(4)], [T_qTs])
                k.cp("dve", kTs[:], kT[:, :, TP:NT], [T_k[m][4] for m in range(4)], [T_kTs])
                for s in range(NS):
                    k.cp("pool", vnew[0:4, s, 0, :], vsel[0:4, 16 + s, :, :].rearrange("p k c -> p (k c)"), [T_vsel[16 + s]], [T_vnew])
                    k.cp("pool", vnew[0:4, s, 1, :], vwin[0:4, 16 + s, :, :].rearrange("p k c -> p (k c)"), [T_vwin[16 + s]], [T_vnew])
                S.flush()
                if STOP.startswith('D'):
                    S.wait_all_outputs("sp"); S.flush(); S.close(); return nc
            pr.close()
            with ExitStack() as ps_:
                kselT = sb("kselT", [128, NPG * 128], BF16, ps_); T_kst = [Tl() for _ in range(32)]
                vsl = sb("vsl", [128, NPG, 130], BF16, ps_); T_vsl = [Tl() for _ in range(32)]
                pg = [sb("pg%d" % i, [128, 512], F32, ps_) for i in range(2)]; T_pg = [Tl(), Tl()]
                XcsB = [xbv[:, 8256 + b * 2048:8256 + (b + 1) * 2048].rearrange("p (t s c) -> p t s c", t=8, s=2) for b in range(2)]
                T_XcsB = [Tl(), Tl()]
                Wd = xbv[:, 14336:14336 + 16384].rearrange("p (s d m) -> p s d m", s=2, d=64); T_Wd2 = Tl()
                ptb = sb("ptb", [128, 128], I32, ps_); T_ptb = Tl()
                idxf = sb("idxf", [128, 128], F32, ps_); T_idxf = Tl()
                idxi = ptb; T_idxi = T_ptb
                iot_i = sb("iot_i", [128, 1], I32, ps_); iot_f = sb("iot_f", [128, 1], F32, ps_); T_iot = Tl()
                hidTs = sb("hidTs", [128, 2, 256], BF16, ps_); T_hid = Tl()
                kcTs = sb("kcTs", [128, 256], BF16, ps_); T_kcs = Tl()
                vcs = sb("vcs", [128, 2, 128], BF16, ps_); T_vcs = Tl()
                W2p = sb("W2p", [128, 2, 2, 128], BF16, ps_); T_W2p = Tl()
                w2v = sb("w2v", [128, 64], BF16, ps_); T_w2v = Tl()
                b1s = sb("b1s", [128, 2], F32, ps_); b2k = sb("b2k", [128, 1], F32, ps_); b2v = sb("b2v", [128, 128], F32, ps_); T_bs2 = Tl()
                qbd = sb("qbd", [128, 64], BF16, ps_); T_qbd = Tl()
                knew = sb("knew", [128, 2, 4], BF16, ps_); T_knew = Tl()
                ssm = [sb("ssm0", [64, 512], F32, ps_)] * 2; T_ssm = [Tl()] * 2
                pex = [sb("pex0", [64, 512], BF16, ps_)] * 2; T_pex = [Tl()] * 2
                PTs = [sb("PTs0", [128, 256], BF16, ps_)] * 2; T_PTs = [Tl()] * 2
                PTn = sb("PTn", [4, 64], BF16, ps_); T_PTn = Tl()
                pc = sb("pc", [64, 256], F32, ps_); T_pc = Tl()
                ec = pc; T_ec = T_pc
                pcb = sb("pcb", [64, 256], BF16, ps_); T_pcb = Tl()
                impf = sb("impf", [64, 256], F32, ps_); T_impf = Tl()
                t8s = sb("t8s", [64, 16], F32, ps_); T_t8s = Tl()
                wks = sb("wks", [64, 256], F32, ps_); T_wks = Tl()
                pens = sb("pens", [64, 256], F32, ps_); T_pens = Tl()
                mxs = sb("mxs", [64, 2], F32, ps_); T_mxs = Tl()
                pTs = sb("pTs", [128, 2, 64], BF16, ps_); T_pTs = Tl()
                kwinT = sb("kwinT", [128, 512], BF16, ps_); T_kwT = Tl()
                vwn = sb("vwn", [128, 4, 130], BF16, ps_); T_vwn = Tl()
                swt = [sb("swt0", [128, 256], F32, ps_)] * 2; T_swt = [Tl()] * 2
                Gm = sb("Gm", [64, 64], F32, ps_); SelT = sb("SelT", [4, 64], F32, ps_); Hsel = sb("Hsel", [64, 8], F32, ps_)
                CN = sb("CN", [64, 4], F32, ps_); CW = sb("CW", [64, 4], F32, ps_); T_cst = Tl()
                gr3 = sb("gr3", [64, 3, 8], F32, ps_); grow = sb("grow", [64, 3], F32, ps_); T_grow = Tl()
                ob = sb("ob", [64, 64], F32, ps_); T_ob = Tl()
                obb = sb("obb", [64, 64], BF16, ps_); T_obb = Tl()
                rs2 = sb("rs2", [64, 1], F32, ps_); T_rs2 = Tl()
                tmo = ssm[0][:, 448:512]; T_tmo = T_ssm[0]
                cache2d = cache_d

                for (t_, d_) in ((Gm, Gm_d), (SelT, SelT_d), (Hsel, Hsel_d), (CN, CN_d), (CW, CW_d)):
                    k.ld(t_[:], d_[:, :], [T_cst])
                k.ld(b1s[:], b1r_d[:, :], [T_bs2]); k.ld(b2k[:], b2k_d[:, :], [T_bs2]); k.ld(b2v[:], b2v_d[:, :], [T_bs2])
                k.memset("pool", W2p[:], 0.0, [T_W2p])
                for parity in range(2):
                    for kvh in range(2):
                        k.ldc(W2p[parity * 64:(parity + 1) * 64, parity, kvh, kvh * 64:(kvh + 1) * 64], w2_d[0, :, :], [T_W2p])
                    k.ldc(w2v[parity * 64:(parity + 1) * 64, :], w2_d[1, :, :], [T_w2v])
                k.memset("pool", vsl[:], 1.0, T_vsl)
                k.memset("pool", vwn[:], 1.0, [T_vwn])
                S.op("pool", lambda e: e.iota(iot_i[:], pattern=[[0, 1]], base=0, channel_multiplier=1), (), [T_iot])
                k.cp("dve", iot_f[:], iot_i[:], [T_iot], [T_iot])

                rot = [1]

                def rp():
                    i = rot[0]
                    rot[0] = 1 + (i % 7)
                    return i

                PO, T_PO = PS[0], PT[0]
                cvt = [0]
                for s in range(NS if 'NOS' not in STOP else 0):
                    tg = 16 + s
                    k.ld(ptb[:], pt_d[s, :, :], [T_ptb])
                    k.cp("dve", idxf[:], ptb[:], [T_ptb], [T_idxf])
                    k.ts("dve", idxf[:], idxf[:], 128.0, iot_f[:, 0:1], ALU.mult, ALU.add, [T_idxf, T_iot], [T_idxf])
                    k.cp("dve", idxi[:], idxf[:], [T_idxf], [T_idxi])
                    k.memset("pool", qbd[:], 0.0, [T_qbd])
                    for h in range(8):
                        kvh = h // 4; g = h % 4; sb_ = (h % 2) * 64
                        k.cp("dve", qbd[kvh * 64:(kvh + 1) * 64, kvh * 32 + g * 4:kvh * 32 + g * 4 + 4], qTs[sb_:sb_ + 64, h // 2, s * 4:(s + 1) * 4],
                             [T_qTs], [T_qbd])
                    for br in range(2):
                        for kvh in range(2):
                            k.cp("dve", knew[kvh * 64:(kvh + 1) * 64, br, :], kTs[kvh * 64:(kvh + 1) * 64, br * 2 + kvh, s * 4:(s + 1) * 4], [T_kTs], [T_knew])
                    pi_ = rp()
                    k.mm(PS[pi_][0:64, 0:24], SelT[:, :], gates[0:4, tg, :], True, True, [T_cst, T_gates[tg]], [PT[pi_]])
                    k.tt("dve", gr3[:], PS[pi_][0:64, 0:24].rearrange("p (b h) -> p b h", b=3), Hsel[:, :].unsqueeze(1).to_broadcast([64, 3, 8]), ALU.mult,
                         [PT[pi_], T_cst], [T_grow])
                    S.op("dve", lambda e: e.tensor_reduce(out=grow[:], in_=gr3[:], axis=AX.X, op=ALU.add), [T_grow], [T_grow])

                    for j in range(NPG):
                        p_ = pg[j % 2]; Tp_ = T_pg[j % 2]
                        S.dma("pool", (lambda p_, j: lambda e: e.indirect_dma_start(out=p_[:, :], out_offset=None, in_=cache2d[:, :],
                                                                                    in_offset=bass.IndirectOffsetOnAxis(ap=idxi[:, j:j + 1], axis=0)))(p_, j),
                              [T_idxi], [Tp_])
                        jj = j % 8
                        Xcs = XcsB[(j // 8) % 2]; T_Xcs = T_XcsB[(j // 8) % 2]
                        Xcs5 = Xcs.rearrange("p t s (k d) -> p t s k d", k=2)
                        k.tt("dve", Xcs[:, jj, :, :], p_[:, 0:256].rearrange("p (s c) -> p s c", s=2), pe_bc[:].rearrange("p s k d -> p s (k d)"), ALU.add,
                             [Tp_, T_pe], [T_Xcs])
                        k.cp("act", vsl[:, j, :].rearrange("p (k c) -> p k c", k=2)[:, :, 0:64], p_[:, 384:512].rearrange("p (k d) -> p k d", k=2), [Tp_], [T_vsl[j // 4]])
                        if j % 4 == 0:
                            pit = rp()
                        k.tr(PS[pit][:, (j % 4) * 128:(j % 4 + 1) * 128], p_[:, 256:384], ident_f[:], [Tp_, T_identf], [PT[pit]])
                        if j % 4 == 3:
                            k.evac(kselT[:, (j - 3) * 128:(j + 1) * 128], PS[pit][:, :], [PT[pit]], [T_kst[j // 4]])
                        if jj == 7:
                            ch = j // 8
                            for sl in range(2):
                                pic = rp()
                                for d in range(64):
                                    k.mm(PS[pic][:, 0:16].rearrange("p (t k) -> p t k", k=2), Wd[:, sl, d, :], Xcs5[:, :, sl, :, d], d == 0, d == 63,
                                         [T_Wd2, T_Xcs], [PT[pic]])
                                k.act(hidTs[:, sl, ch * 16:(ch + 1) * 16], PS[pic][:, 0:16], AF.Gelu_apprx_tanh, [PT[pic], T_bs2], [T_hid], bias=b1s[:, sl:sl + 1])
                    hv = hidTs[:, :, :].rearrange("p s (g k) -> p s g k", k=2)
                    for parity in range(2):
                        pi_ = rp()
                        for kvh in range(2):
                            k.mm(PS[pi_][:, 0:128], W2p[:, parity, kvh, :], hv[:, 0, :, kvh], kvh == 0, kvh == 1, [T_W2p, T_hid], [PT[pi_]])
                        k.act(kcTs[:, parity * 128:(parity + 1) * 128], PS[pi_][:, 0:128], AF.Identity, [PT[pi_], T_bs2], [T_kcs], bias=b2k[:, 0:1])
                    for parity in range(2):
                        pi_ = rp()
                        for kvh in range(2):
                            k.mm(PS[pi_][:, kvh * 64:(kvh + 1) * 64], hv[parity * 64:(parity + 1) * 64, 1, :, kvh], w2v[parity * 64:(parity + 1) * 64, :], True, True,
                                 [T_hid, T_w2v], [PT[pi_]])
                        k.tt("dve", vcs[:, parity, :], PS[pi_][:, 0:128], b2v[:, :], ALU.add, [PT[pi_], T_bs2], [T_vcs])
                    pi_ = rp()
                    k.mm(PS[pi_][0:64, 0:256], qbd[:, :], kcTs[:, :], True, True, [T_qbd, T_kcs], [PT[pi_]])
                    S.op("dve", (lambda pi_: lambda e: e.tensor_reduce(out=mxs[:, 0:1], in_=PS[pi_][0:64, 0:256], axis=AX.X, op=ALU.max))(pi_), [PT[pi_]], [T_mxs])
                    k.ts("dve", mxs[:, 0:1], mxs[:, 0:1], -1.0, None, ALU.mult, None, [T_mxs], [T_mxs])
                    k.act(ec[:], PS[pi_][0:64, 0:256], AF.Exp, [PT[pi_], T_mxs], [T_ec, T_mxs], bias=mxs[:, 0:1], accum_out=mxs[:, 1:2])
                    S.op("dve", lambda e: e.reciprocal(out=mxs[:, 1:2], in_=mxs[:, 1:2]), [T_mxs], [T_mxs])
                    k.ts("dve", pc[:], ec[:], mxs[:, 1:2], None, ALU.mult, None, [T_mxs, T_pc], [T_pc])
                    k.cp("pool", pcb[:], pc[:], [T_pc], [T_pcb])
                    pi2 = rp()
                    k.mm(PS[pi2][0:64, 0:256], Gm[:, :], pc[:, :], True, True, [T_cst, T_pc], [PT[pi2]])
                    k.cp("act", impf[:], PS[pi2][0:64, 0:256], [PT[pi2]], [T_impf])
                    k.memset("dve", impf[:, 0:1], 1e4, [T_impf])
                    k.memset("dve", impf[:, 255:256], 1e4, [T_impf])
                    S.op("dve", lambda e: e.max(out=t8s[:, 0:8], in_=impf[:]), [T_impf], [T_t8s])
                    S.op("dve", lambda e: e.match_replace(out=wks[:], in_to_replace=t8s[:, 0:8], in_values=impf[:], imm_value=-1e30), [T_impf, T_t8s], [T_wks])
                    S.op("dve", lambda e: e.max(out=t8s[:, 8:16], in_=wks[:]), [T_wks], [T_t8s])
                    k.ts("dve", pens[:], impf[:], t8s[:, 14:15], -NEGB, ALU.is_ge, ALU.mult, [T_impf, T_t8s], [T_pens])
                    k.ts("dve", pens[:], pens[:], NEGB, None, ALU.add, None, [T_pens], [T_pens])
                    pi3 = rp()
                    for parity in range(2):
                        k.tr(PSb[pi3][:, parity * 64:(parity + 1) * 64], pcb[:, parity * 128:(parity + 1) * 128], ident_b[0:64, 0:64], [T_pcb, T_identb], [PT[pi3]])
                    k.cp("act", pTs[:].rearrange("p a r -> p (a r)"), PSb[pi3][:, 0:128], [PT[pi3]], [T_pTs])
                    pi4 = rp()
                    for parity in range(2):
                        k.mm(PS[pi4][0:64, 0:128], pTs[:, parity, :], vcs[:, parity, :], parity == 0, parity == 1, [T_pTs, T_vcs], [PT[pi4]])
                    for half in range(2):
                        rsl = slice(32 * half, 32 * half + 32)
                        k.ts("dve", ob[rsl, :], PS[pi4][rsl, half * 64:(half + 1) * 64], grow[rsl, 0:1], None, ALU.mult, None, [PT[pi4], T_grow], [T_ob])

                    for t in range(4):
                        w_ = swt[t % 2]; Tw_ = T_swt[t % 2]
                        k.ld(w_[:], swin_d[s, t * 128:(t + 1) * 128, :], [Tw_])
                        if t == 0:
                            piw = rp()
                        k.tr(PS[piw][:, t * 128:(t + 1) * 128], w_[:, 0:128], ident_f[:], [Tw_, T_identf], [PT[piw]])
                        k.cp("pool", vwn[:, t, :].rearrange("p (k c) -> p k c", k=2)[:, :, 0:64], w_[:, 128:256].rearrange("p (k d) -> p k d", k=2), [Tw_], [T_vwn])
                    k.evac(kwinT[:], PS[piw][:, :], [PT[piw]], [T_kwT])

                    pen3 = pens[:].rearrange("r (a j) -> r j a", a=2)
                    for br in range(2):
                        ngrp = 32 if br == 0 else 1
                        first = True
                        for gq in range(ngrp):
                            b_ = cvt[0] % 2; cvt[0] += 1
                            pi_ = rp()
                            if br == 0:
                                k.mm(PS[pi_][0:64, :], qbd[:, :], kselT[:, gq * 512:(gq + 1) * 512], True, True, [T_qbd, T_kst[gq]], [PT[pi_]])
                                k.tt("dve", ssm[b_][:].rearrange("r (j a i) -> r j a i", j=4, a=2), PS[pi_][0:64, :].rearrange("r (j a i) -> r j a i", j=4, a=2),
                                     pen3[:, gq * 4:(gq + 1) * 4, :].unsqueeze(3).to_broadcast([64, 4, 2, 64]), ALU.add, [PT[pi_], T_pens], [T_ssm[b_]])
                            else:
                                k.mm(PS[pi_][0:64, :], qbd[:, :], kwinT[:, :], True, True, [T_qbd, T_kwT], [PT[pi_]])
                                k.tt("dve", ssm[b_][:, 0:4], PS[pi_][0:64, 0:4], CW[:, :], ALU.add, [PT[pi_], T_cst], [T_ssm[b_]])
                            if br == 0:
                                k.act(pex[b_][:], ssm[b_][:], AF.Exp, [T_ssm[b_]], [T_pex[b_]])
                            else:
                                k.act(pex[b_][:, 0:4], ssm[b_][:, 0:4], AF.Exp, [T_ssm[b_]], [T_pex[b_]])
                                k.act(pex[b_][:, 4:512], PS[pi_][0:64, 4:512], AF.Exp, [PT[pi_]], [T_pex[b_]])
                            pit2 = rp()
                            for jj in range(4):
                                k.tr(PSb[pit2][:, jj * 64:(jj + 1) * 64], pex[b_][:, jj * 128:(jj + 1) * 128], ident_b[0:64, 0:64], [T_pex[b_], T_identb], [PT[pit2]])
                            k.evac(PTs[b_][:], PSb[pit2][:, 0:256], [PT[pit2]], [T_PTs[b_]])
                            for jj in range(4):
                                if br == 0:
                                    rhs_ = vsl[:, gq * 4 + jj, :]; Tr_ = T_vsl[gq]
                                else:
                                    rhs_ = vwn[:, jj, :]; Tr_ = T_vwn
                                k.mm(PO[0:64, 0:130], PTs[b_][:, jj * 64:(jj + 1) * 64], rhs_, first, False, [T_PTs[b_], Tr_], [T_PO])
                                first = False
                        pi_ = rp()
                        k.mm(PS[pi_][0:64, 0:4], qbd[:, :], knew[:, br, :], True, True, [T_qbd, T_knew], [PT[pi_]])
                        b_ = cvt[0] % 2; cvt[0] += 1
                        k.tt("dve", ssm[b_][:, 0:4], PS[pi_][0:64, 0:4], CN[:, :], ALU.add, [PT[pi_], T_cst], [T_ssm[b_]])
                        k.act(pex[b_][:, 0:4], ssm[b_][:, 0:4], AF.Exp, [T_ssm[b_]], [T_pex[b_]])
                        pit2 = rp()
                        k.tr(PSb[pit2][0:4, 0:64], pex[b_][:, 0:4], ident_b[0:64, 0:64], [T_pex[b_], T_identb], [PT[pit2]])
                        k.cp("dve", PTn[:], PSb[pit2][0:4, 0:64], [PT[pit2]], [T_PTn])
                        k.mm(PO[0:64, 0:130], PTn[:, :], vnew[0:4, s, br, :], False, True, [T_PTn, T_vnew], [T_PO])
                        for half in range(2):
                            rsl = slice(32 * half, 32 * half + 32)
                            c0 = half * 65
                            S.op("dve", (lambda rsl, c0: lambda e: e.reciprocal(out=rs2[rsl, :], in_=PO[rsl, c0 + 64:c0 + 65]))(rsl, c0), [T_PO], [T_rs2])
                            k.tt("dve", rs2[rsl, :], rs2[rsl, :], grow[rsl, 1 + br:2 + br], ALU.mult, [T_rs2, T_grow], [T_rs2])
                            k.ts("dve", tmo[rsl, :], PO[rsl, c0:c0 + 64], rs2[rsl, 0:1], None, ALU.mult, None, [T_PO, T_rs2], [T_tmo])
                            k.tt("dve", ob[rsl, :], ob[rsl, :], tmo[rsl, :], ALU.add, [T_ob, T_tmo], [T_ob])
                    k.cp("act", obb[:], ob[:], [T_ob], [T_obb])
                    pi_ = rp()
                    k.tr(PSb[pi_][0:64, 0:64], obb[:, :], ident_b[0:64, 0:64], [T_obb, T_identb], [PT[pi_]])
                    for h in range(8):
                        kvh = h // 4; g = h % 4; db = (h % 2) * 64
                        k.cp("dve", ybT[db:db + 64, h // 2, TP + s * 4:TP + (s + 1) * 4], PSb[pi_][0:64, kvh * 32 + g * 4:kvh * 32 + g * 4 + 4], [PT[pi_]], [T_yb[4]])
                S.flush()

        with ExitStack() as pe_:
            mixT = sb("mixT", [128, KC, NT], BF16, pe_); T_mix = [[Tl() for _ in TBS] for _ in range(KC)]
            with ExitStack() as pe1:
                wba = sb("wba", [128, 4, D], BF16, pe1); T_wba = Tl()
                wbb = sb("wbb", [128, 4, D], BF16, pe1); T_wbb = Tl()
                wzm = [sb("wzm%d" % i, [128, KC, 2, 128], BF16, pe1) for i in range(2)]; T_wzm = [Tl(), Tl()]
                gsa = [sb("gsa%d" % i, [128, 512], F32, pe1) for i in range(2)]; T_gsa = [Tl(), Tl()]
                gsb = [sb("gsb%d" % i, [128, 512], F32, pe1) for i in range(2)]; T_gsb = [Tl(), Tl()]
                k.ldc(wba[:], wba_d.rearrange("(kc p) n -> p kc n", p=128), [T_wba])
                k.ldc(wbb[:], wbb_d.rearrange("(kc p) n -> p kc n", p=128), [T_wbb])
                wi = win_d.rearrange("(kc p) n -> p kc n", p=128)
                it = 0
                for c in range(KC):
                    wz = wzm[c % 2]; Twz = T_wzm[c % 2]
                    k.ldc(wz[:, :, 0, :], wi[:, :, 2328 + c * 128:2328 + (c + 1) * 128], [Twz])
                    k.ldc(wz[:, :, 1, :], wi[:, :, 3352 + c * 128:3352 + (c + 1) * 128], [Twz])
                    for bi, (t0, n) in enumerate(TBS):
                        ga = gsa[it % 2]; Tga = T_gsa[it % 2]; gb = gsb[it % 2]; Tgb = T_gsb[it % 2]; it += 1
                        psA, TA = nextps()
                        for k4 in range(4):
                            k.mm(psA[:, 0:n], wba[:, k4, c * 128:(c + 1) * 128], uT[:, k4, t0:t0 + n], k4 == 0, k4 == 3, [T_wba, T_u[k4][bi]], [TA])
                        psG, TG = nextps()
                        for kc in range(KC):
                            k.mm(psG[:, 0:n], wz[:, kc, 0, :], hT[:, kc, t0:t0 + n], kc == 0, kc == KC - 1, [Twz, T_h[kc][bi]], [TG])
                        k.act(ga[:, 0:n], psG[:, 0:n], AF.Sigmoid, [TG], [Tga])
                        k.tt("dve", ga[:, 0:n], ga[:, 0:n], psA[:, 0:n], ALU.mult, [Tga, TA], [Tga])
                        psB, TB = nextps()
                        for k4 in range(4):
                            k.mm(psB[:, 0:n], wbb[:, k4, c * 128:(c + 1) * 128], ybT[:, k4, t0:t0 + n], k4 == 0, k4 == 3, [T_wbb, T_yb[bi]], [TB])
                        psH, TH = nextps()
                        for kc in range(KC):
                            k.mm(psH[:, 0:n], wz[:, kc, 1, :], hT[:, kc, t0:t0 + n], kc == 0, kc == KC - 1, [Twz, T_h[kc][bi]], [TH])
                        k.act(gb[:, 0:n], psH[:, 0:n], AF.Sigmoid, [TH], [Tgb])
                        k.tt("dve", gb[:, 0:n], gb[:, 0:n], psB[:, 0:n], ALU.mult, [Tgb, TB], [Tgb])
                        k.tt("pool", mixT[:, c, t0:t0 + n], ga[:, 0:n], gb[:, 0:n], ALU.add, [Tga, Tgb], [T_mix[c][bi]])
                S.flush()
                if STOP.startswith('E1'):
                    S.wait_all_outputs("sp"); S.flush(); S.close(); return nc
            with ExitStack() as pe2:
                wout = sb("wout", [128, KC, D], BF16, pe2); T_wout = Tl()
                xr = [sb("xr%d" % i, [128, 512], F32, pe2) for i in range(2)]; T_xr = [Tl(), Tl()]
                k.ldc(wout[:], wout_d.rearrange("(kc p) n -> p kc n", p=128), [T_wout])
                it = 0
                for c in range(KC):
                    for bi, (t0, n) in enumerate(TBS):
                        x_ = xr[it % 2]; Tx = T_xr[it % 2]; it += 1
                        k.ld(x_[:, 0:n], xT_d[:, c, t0:t0 + n], [Tx])
                        ps, Tp = nextps()
                        for kc in range(KC):
                            k.mm(ps[:, 0:n], wout[:, kc, c * 128:(c + 1) * 128], mixT[:, kc, t0:t0 + n], kc == 0, kc == KC - 1, [T_wout, T_mix[kc][bi]], [Tp])
                        if bi < 4:
                            k.stt(x1T[:, c, t0:t0 + n], ps[:, 0:n], modT[:, 16 + c, 0:1], x_[:, 0:n], ALU.mult, ALU.add, [Tp, Tx, T_mod], [T_x1[c][bi]])
                        else:
                            for s in range(NS):
                                k.stt(x1T[:, c, t0 + s * TS:t0 + (s + 1) * TS], ps[:, s * TS:(s + 1) * TS], modT[:, 16 + c, 1 + s:2 + s],
                                      x_[:, s * TS:(s + 1) * TS], ALU.mult, ALU.add, [Tp, Tx, T_mod], [T_x1[c][bi]])
                S.flush()

        mid_scope.close()
        with ExitStack() as pf:
            sq = [sb("sqF%d" % i, [128, 1040], BF16, pf) for i in range(2)]; T_sq = [Tl(), Tl()]
            rstd = sb("rstdF", [128, NT], F32, pf); T_rstd = [Tl() for _ in TBS]
            tmp = [sb("tmpF0", [128, 1040], F32, pf)] * 2; T_tmp = [Tl()] * 2
            actT = sb("actT", [128, 22, 1040], BF16, pf); T_act = [[Tl() for _ in range(3)] for _ in range(22)]
            wup = [sb("wup%d" % i, [128, KC, 2, 128], BF16, pf) for i in range(2)]; T_wup = [Tl(), Tl()]
            wdn = [sb("wdn%d" % i, [128, 22, 128], BF16, pf) for i in range(2)]; T_wdn = [Tl(), Tl()]
            U = [sb("U%d" % i, [128, 514], F32, pf) for i in range(4)]; T_U = [Tl() for _ in range(4)]
            cv = [sb("cv%d" % i, [128, 512], F32, pf) for i in range(4)]; T_cv = [Tl() for _ in range(4)]
            gl = [sb("gl%d" % i, [128, 512], F32, pf) for i in range(2)]; T_gl = [Tl(), Tl()]
            halo = sb("halo", [128, 44, 2], F32, pf); T_halo = [Tl() for _ in range(44)]
            convo = sb("convo", [128, 44, 10], F32, pf); T_convo = Tl()
            wcv = sb("wcv", [128, 44, 3], F32, pf); bcv = sb("bcv", [128, 44], F32, pf); T_wcv = Tl()
            sprev = sb("sprev", [128, 44, 4, 2], F32, pf); T_sprev = Tl()
            ups = [sb("ups%d" % i, [128, 4, 6], F32, pf) for i in range(2)]; T_ups = [Tl(), Tl()]
            cvs = [sb("cvs%d" % i, [128, 4, 4], F32, pf) for i in range(2)]; T_cvs = [Tl(), Tl()]
            gf = sb("gf", [128, KC], F32, pf); T_gf = Tl()
            yo = [sb("yo0", [128, 1040], F32, pf)] * 2; T_yo = [Tl()] * 2
            k.ld(wcv[:], wcv_d[:, :, :], [T_wcv]); k.ld(bcv[:], bcv_d[:, :], [T_wcv])
            k.ld(sprev[:], sprev_d[:, :, :, :], [T_sprev])
            k.ld(gf[:], gf_d[:, :], [T_gf])
            wupv = wup_d.rearrange("(kc p) n -> p kc n", p=128)
            wdnv = wdn_d.rearrange("(c p) n -> p c n", p=128)
            HALVES = [(0, [(0, 512), (512, 512)]), (1024, [(1024, 512), (1536, 512), (2048, 16)])]

            def rms_stats(src, Tsrc, hs, blocks, nb0):
                ntk = sum(n for _, n in blocks)
                for kc in range(KC):
                    s_ = sq[kc % 2]; Ts = T_sq[kc % 2]
                    k.act(s_[:, 0:ntk], src[:, kc, hs:hs + ntk], AF.Square, Tsrc(kc), [Ts])
                    for bi, (t0, n) in enumerate(blocks):
                        k.mm(PS[bi][:, 0:n], ones_b[:], s_[:, t0 - hs:t0 - hs + n], kc == 0, kc == KC - 1, [T_ones, Ts], [PT[bi]])
                for bi, (t0, n) in enumerate(blocks):
                    k.act(rstd[:, t0:t0 + n], PS[bi][:, 0:n], AF.Sqrt, [PT[bi]], [T_rstd[nb0 + bi]], bias=EPS, scale=1.0 / D)
                    S.op("dve", (lambda o: (lambda e: e.reciprocal(out=o, in_=o)))(rstd[:, t0:t0 + n]), [T_rstd[nb0 + bi]], [T_rstd[nb0 + bi]])

            uidx = [0]
            for hi, (hs, blocks) in enumerate(HALVES):
                nb0 = 0 if hi == 0 else 2
                ntk = sum(n for _, n in blocks)
                rms_stats(x1T, lambda kc: [T_x1[kc][nb0 + b] for b in range(len(blocks))], hs, blocks, nb0)
                for kc in range(KC):
                    t_ = tmp[kc % 2]; Tt = T_tmp[kc % 2]
                    Tx = [T_x1[kc][nb0 + b] for b in range(len(blocks))]
                    Th = [T_h[kc][nb0 + b] for b in range(len(blocks))]
                    k.tt("dve", t_[:, 0:ntk], x1T[:, kc, hs:hs + ntk], rstd[:, hs:hs + ntk], ALU.mult, Tx + T_rstd[nb0:nb0 + len(blocks)], [Tt])
                    npr = ntk if hi == 0 else 1024
                    k.act(hT[:, kc, hs:hs + npr], t_[:, 0:npr], AF.Identity, [Tt, T_A2, T_mod], Th, bias=modT[:, 24 + kc, 0:1], scale=A2[:, kc, 0:1])
                    if hi == 1:
                        for s in range(NS):
                            c0 = 1024 + s * TS
                            k.act(hT[:, kc, TP + s * TS:TP + (s + 1) * TS], t_[:, c0:c0 + TS], AF.Identity, [Tt, T_A2, T_mod], Th,
                                  bias=modT[:, 24 + kc, 1 + s:2 + s], scale=A2[:, kc, 1 + s:2 + s])
                for cp_ in range(22):
                    wu_ = wup[cp_ % 2]; Twu = T_wup[cp_ % 2]
                    k.ldc(wu_[:, :, 0, :], wupv[:, :, cp_ * 128:(cp_ + 1) * 128], [Twu])
                    k.ldc(wu_[:, :, 1, :], wupv[:, :, DFF + cp_ * 128:DFF + (cp_ + 1) * 128], [Twu])
                    for cc in range(1):
                        c = cp_
                        for bi, (t0, n) in enumerate(blocks):
                            gbi = nb0 + bi
                            res = []
                            for ag in range(2):
                                idx = c + 22 * ag
                                ps, Tp = nextps()
                                for kc in range(KC):
                                    k.mm(ps[:, 0:n], wu_[:, kc, ag, cc * 128:(cc + 1) * 128], hT[:, kc, t0:t0 + n], kc == 0, kc == KC - 1,
                                         [Twu, T_h[kc][gbi]], [Tp])
                                w0 = wcv[:, idx, 0:1]; w1 = wcv[:, idx, 1:2]; w2 = wcv[:, idx, 2:3]; bb = bcv[:, idx:idx + 1]
                                if n == 512:
                                    ui = uidx[0] % 4; uidx[0] += 1
                                    U_ = U[ui]; TU = T_U[ui]; cv_ = cv[ui]; Tcv = T_cv[ui]
                                    if t0 == 0:
                                        k.memset("pool", U_[:, 0:2], 0.0, [TU])
                                    else:
                                        k.cp("pool", U_[:, 0:2], halo[:, idx, :], [T_halo[idx]], [TU])
                                    k.cp("act", U_[:, 2:514], ps[:, 0:512], [Tp], [TU])
                                    k.cp("pool", halo[:, idx, :], U_[:, 512:514], [TU], [T_halo[idx]])
                                    if t0 == 1536:
                                        k.cp("pool", convo[:, idx, 0:2], U_[:, 512:514], [TU], [T_convo])
                                    k.ts("dve", cv_[:], U_[:, 2:514], w2, bb, ALU.mult, ALU.add, [TU, T_wcv], [Tcv])
                                    k.stt(cv_[:], U_[:, 1:513], w1, cv_[:], ALU.mult, ALU.add, [TU, T_wcv, Tcv], [Tcv])
                                    k.stt(cv_[:], U_[:, 0:512], w0, cv_[:], ALU.mult, ALU.add, [TU, T_wcv, Tcv], [Tcv])
                                    res.append((cv_[:], Tcv))
                                else:
                                    u_ = ups[ag]; Tu_ = T_ups[ag]; c_ = cvs[ag]; Tc_ = T_cvs[ag]
                                    k.cp("pool", u_[:, :, 0:2], sprev[:, idx, :, :], [T_sprev], [Tu_])
                                    k.cp("act", u_[:, :, 2:6], ps[:, 0:16].rearrange("p (s t) -> p s t", s=4), [Tp], [Tu_])
                                    k.cp("pool", convo[:, idx, 2:10].rearrange("p (s r) -> p s r", s=4), u_[:, :, 4:6], [Tu_], [T_convo])
                                    k.ts("dve", c_[:], u_[:, :, 2:6], w2, bb, ALU.mult, ALU.add, [Tu_, T_wcv], [Tc_])
                                    k.stt(c_[:], u_[:, :, 1:5], w1, c_[:], ALU.mult, ALU.add, [Tu_, T_wcv, Tc_], [Tc_])
                                    k.stt(c_[:], u_[:, :, 0:4], w0, c_[:], ALU.mult, ALU.add, [Tu_, T_wcv, Tc_], [Tc_])
                                    res.append((c_[:].rearrange("p s t -> p (s t)"), Tc_))
                            (ca, Tca), (cg, Tcg) = res
                            g_ = gl[(c + bi) % 2]; Tg_ = T_gl[(c + bi) % 2]
                            k.act(g_[:, 0:n], ca, AF.Gelu_apprx_tanh, [Tca], [Tg_])
                            k.tt("dve", actT[:, c, t0 - hs:t0 - hs + n], g_[:, 0:n], cg, ALU.mult, [Tg_, Tcg], [T_act[c][bi]])
                for m in range(KC):
                    wd_ = wdn[m % 2]; Twd = T_wdn[m % 2]
                    k.ldc(wd_[:], wdnv[:, :, m * 128:(m + 1) * 128], [Twd])
                    for bi, (t0, n) in enumerate(blocks):
                        gbi = nb0 + bi
                        ps, Tp = nextps()
                        for c in range(22):
                            k.mm(ps[:, 0:n], wd_[:, c, :], actT[:, c, t0 - hs:t0 - hs + n], c == 0, c == 21, [Twd, T_act[c][bi]], [Tp])
                        if n == 512:
                            k.stt(x1T[:, m, t0:t0 + n], ps[:, 0:n], modT[:, 40 + m, 0:1], x1T[:, m, t0:t0 + n], ALU.mult, ALU.add,
                                  [Tp, T_mod, T_x1[m][gbi]], [T_x1[m][gbi]])
                        else:
                            for s in range(NS):
                                xs_ = x1T[:, m, t0 + s * TS:t0 + (s + 1) * TS]
                                k.stt(xs_, ps[:, s * TS:(s + 1) * TS], modT[:, 40 + m, 1 + s:2 + s], xs_, ALU.mult, ALU.add,
                                      [Tp, T_mod, T_x1[m][gbi]], [T_x1[m][gbi]])
                rms_stats(x1T, lambda kc: [T_x1[kc][nb0 + b] for b in range(len(blocks))], hs, blocks, nb0)
                for m in range(KC):
                    t_ = tmp[m % 2]; Tt = T_tmp[m % 2]
                    y_ = yo[m % 2]; Ty = T_yo[m % 2]
                    Tx = [T_x1[m][nb0 + b] for b in range(len(blocks))]
                    k.tt("dve", t_[:, 0:ntk], x1T[:, m, hs:hs + ntk], rstd[:, hs:hs + ntk], ALU.mult, Tx + T_rstd[nb0:nb0 + len(blocks)], [Tt])
                    k.act(y_[:, 0:ntk], t_[:, 0:ntk], AF.Copy, [Tt, T_gf], [Ty], scale=gf[:, m:m + 1])
                    k.st(yT_o[:, m, hs:hs + ntk], y_[:, 0:ntk], [Ty])
            k.st(conv_o[:, :], convo[:].rearrange("p i r -> p (i r)"), [T_convo])
            S.wait_all_outputs("sp")
            S.flush()
    S.close()
    return nc


_NC_CACHE = {}


def _prep_inputs(inp):
    f = lambda a: np.ascontiguousarray(a, dtype=np.float32)
    xp = np.asarray(inp["x_prompt"]); xs = np.asarray(inp["x_sample"])
    cp_ = np.asarray(inp["c_prompt"]); cs_ = np.asarray(inp["c_sample"])

    def fm(vec):
        return f(np.asarray(vec).reshape(KC, 128).T)

    shared = {
        "w_ada": f(np.asarray(inp["w_ada"])[0]),
        "b_adaT": f(np.asarray(inp["b_ada"])[0].reshape(48, 128).T),
        "g1T": fm(inp["g_norm1"][0]), "g2T": fm(inp["g_norm2"][0]), "gfT": fm(inp["g_final"]),
        "w_in": f(np.asarray(inp["w_in"])[0]),
        "ln_g_bc": f(np.broadcast_to(np.asarray(inp["ln_v_g"])[0][None, :], (128, 512))),
        "ln_b_bc": f(np.broadcast_to(np.asarray(inp["ln_v_b"])[0][None, :], (128, 512))),
        "ident": np.eye(128, dtype=np.float32),
    }
    ws = np.asarray(inp["w_spatial"])[0]
    shared["wsT"] = f(ws.transpose(2, 0, 1))
    shared["wssT"] = f(ws[:, :4, :4].transpose(2, 0, 1))
    shared["bs"] = f(np.asarray(inp["b_spatial"])[0][None])
    b1 = np.asarray(inp["cmp_b1"])[0]; b2 = np.asarray(inp["cmp_b2"])[0]
    shared["b1r"] = f(np.concatenate([b1.T, b1.T], axis=0))
    shared["b2r"] = f(np.concatenate([b2.T, b2.T], axis=0))
    shared["cmp_w1"] = f(np.asarray(inp["cmp_w1"])[0])
    shared["cmp_w2"] = f(np.asarray(inp["cmp_w2"])[0])
    shared["w_branch_a"] = f(np.asarray(inp["w_branch_a"])[0])
    shared["w_branch_b"] = f(np.asarray(inp["w_branch_b"])[0])
    shared["w_out"] = f(np.asarray(inp["w_out"])[0])
    shared["w_up"] = f(np.asarray(inp["w_up"])[0])
    shared["w_down"] = f(np.asarray(inp["w_down"])[0])
    shared["w_convT"] = f(np.asarray(inp["w_conv"])[0].reshape(3, 44, 128).transpose(2, 1, 0))
    shared["b_convT"] = f(np.asarray(inp["b_conv"])[0].reshape(44, 128).T)
    t = np.arange(2048)[:, None]; n = np.arange(32)[None, :]
    avail = (n + 1) * 64 <= t + 1
    cur = t // 64
    forced = (n == 0) | (n == cur) | (n == cur - 1)
    future = n > cur
    mk = np.stack([np.where(avail, 0.0, -1e30), avail.astype(np.float32), (~(forced | future)).astype(np.float32),
                   np.where(forced, 1e4, np.where(future, -1.0, 0.0))], axis=0)
    shared["msk"] = f(mk.reshape(4, 16, 128, 32).transpose(2, 0, 1, 3))
    key = np.arange(2048)[None, :]; r = np.arange(64)[:, None]
    shared["Emat"] = f((key // 64 == (r % 32)).astype(np.float32))
    b_ = np.arange(128)[:, None]; a_ = np.arange(128)[None, :]
    shared["Cm"] = f(np.stack([np.where(a_ >= b_, 0.0, NEGB), np.where(a_ < b_, 0.0, NEGB)], axis=1))
    sconv = np.asarray(inp["state_ffn_conv"])[0]
    if not STOP:
        shared["cache2d"] = np.asarray(inp["cache_kv"], dtype=np.float32).reshape(5120 * 128, 512)
    shared["b2k"] = f(np.concatenate([b2[0], b2[0]])[:, None])
    shared["b2v"] = f(np.broadcast_to(np.concatenate([b2[1], b2[1]])[None, :], (128, 128)))
    rr = np.arange(64); kvh_r = rr // 32; sl_r = rr % 32; g_r = sl_r // 4; tok_r = sl_r % 4; used = sl_r < 16
    shared["Gm"] = f(((kvh_r[:, None] == kvh_r[None, :]) & (tok_r[:, None] == tok_r[None, :]) & used[:, None]).astype(np.float32))
    shared["SelT"] = f((np.arange(4)[:, None] == tok_r[None, :]).astype(np.float32))
    shared["Hsel"] = f((np.arange(8)[None, :] == (4 * kvh_r + np.minimum(g_r, 3))[:, None]).astype(np.float32))
    shared["CN"] = f(np.where(np.arange(4)[None, :] <= tok_r[:, None], 0.0, NEGB))
    shared["CW"] = f(np.where(np.arange(4)[None, :] > tok_r[:, None], 0.0, NEGB))
    ptab = np.asarray(inp["page_table"]).astype(np.int32)
    pe = np.asarray(inp["cmp_pe"])[0]
    pe_bc = np.broadcast_to(pe.transpose(1, 0, 2)[None, :, :, None, :], (2, 64, 2, 2, 64)).reshape(128, 2, 2, 64)
    shared["pe_bc"] = f(pe_bc)
    maps = []
    swin = np.asarray(inp["state_kv_win"])[0].reshape(32, 512, 256)
    for c in range(NCORES):
        xall = np.concatenate([xp[c], xs[4 * c:4 * c + 4].reshape(16, D)], axis=0)
        xT = f(xall.T.reshape(KC, 128, NT).transpose(1, 0, 2))
        call = np.concatenate([cp_[c:c + 1], cs_[4 * c:4 * c + 4]], axis=0)
        cT = f(call.T.reshape(KC, 128, 5).transpose(1, 0, 2))
        m = dict(shared)
        m["xT"] = xT
        m["cT"] = cT
        m["state_win"] = f(swin[4 * c:4 * c + 4])
        m["pt_bc"] = np.ascontiguousarray(np.broadcast_to(ptab[4 * c:4 * c + 4][:, None, :], (4, 128, 128)), dtype=np.int32)
        m["sprevT"] = f(sconv[4 * c:4 * c + 4].reshape(4, 2, 44, 128).transpose(3, 2, 0, 1))
        maps.append(m)
    return maps


def kernel(**inp):
    if "nc" not in _NC_CACHE:
        _NC_CACHE["nc"] = build_program()
    nc = _NC_CACHE["nc"]
    maps = _prep_inputs(inp)
    res = run_bass_kernel_spmd(nc, maps, core_ids=list(range(NCORES)))
    R = res.results
    y_prompt = np.zeros((8, 2048, 1024), np.float32)
    y_sample = np.zeros((32, 4, 1024), np.float32)
    kv_prompt = np.zeros((1, 8, 2048, 4, 2, 64), np.float32)
    kv_sample = np.zeros((1, 32, 4, 4, 2, 64), np.float32)
    win_prompt = np.zeros((1, 8, 512, 2, 2, 64), np.float32)
    win_sample = np.zeros((1, 32, 512, 2, 2, 64), np.float32)
    v_chunk = np.zeros((1, 32, 4, 512), np.float32)
    conv_prompt = np.zeros((1, 8, 2, 5632), np.float32)
    conv_sample = np.zeros((1, 32, 2, 5632), np.float32)
    for c in range(NCORES):
        r = R[c]
        kv = r["kv_tok"]
        kv_prompt[0, c] = kv[:TP].reshape(2048, 4, 2, 64)
        kv_sample[0, 4 * c:4 * c + 4] = kv[TP:].reshape(4, 4, 4, 2, 64)
        win_prompt[0, c] = r["win_p"].reshape(512, 2, 2, 64)
        win_sample[0, 4 * c:4 * c + 4] = r["win_s"].reshape(4, 512, 2, 2, 64)
        v_chunk[0, 4 * c:4 * c + 4] = r["vchunk"].reshape(4, 4, 512)
        yT = r["yT"].transpose(2, 1, 0).reshape(NT, D)
        y_prompt[c] = yT[:TP]
        y_sample[4 * c:4 * c + 4] = yT[TP:].reshape(4, 4, D)
        cv = r["convT"].reshape(128, 44, 10)
        conv_prompt[0, c] = cv[:, :, 0:2].transpose(2, 1, 0).reshape(2, 5632)
        conv_sample[0, 4 * c:4 * c + 4] = cv[:, :, 2:10].reshape(128, 44, 4, 2).transpose(2, 3, 1, 0).reshape(4, 2, 5632)
    return (y_prompt, y_sample, kv_prompt, kv_sample, win_prompt, win_sample, v_chunk, conv_prompt, conv_sample)
```

```python
import numpy as np
from contextlib import ExitStack
import concourse.bass as bass
import concourse.mybir as mybir
from concourse.bass_utils import run_bass_kernel_spmd

F32 = mybir.dt.float32
BF16 = mybir.dt.bfloat16
I32 = mybir.dt.int32
AF = mybir.ActivationFunctionType
ALU = mybir.AluOpType
AX = mybir.AxisListType

NCORES = 8
D = 1024
KC = 8
TP = 2048
NS = 4
TS = 4
NT = TP + NS * TS
IN_COLS = 4376
DFF = 2816
NPG = 128
EPS = 1e-6
NEGB = -30000.0
TBS = [(0, 512), (512, 512), (1024, 512), (1536, 512), (2048, 16)]
TTS = [(i * 128, 128) for i in range(16)] + [(TP + s * TS, TS) for s in range(NS)]


class Tl:
    __slots__ = ("name", "w", "r", "ps")

    def __init__(self, name="", ps=False):
        self.name = name
        self.w = None
        self.r = []
        self.ps = ps


class Sched:
    ENG = ("pe", "act", "dve", "pool", "sp")

    def __init__(self, nc, n_dma_sems=(32, 4, 24)):
        self.nc = nc
        self.sems = {}
        self._stack = []
        for e in self.ENG:
            self.sems[e] = self._sem("s_" + e)
        self.seq = {e: 0 for e in self.ENG}
        self.epoch = {e: 0 for e in self.ENG}
        self.cur = {e: e for e in self.ENG}
        self.known = {e: {} for e in self.ENG}
        self.lists = {e: [] for e in self.ENG}
        self.dpool = {}
        for q, n in zip(("sp", "act", "pool"), n_dma_sems):
            self.dpool[q] = dict(keys=[], cnt=[], nxt=0)
            for i in range(n):
                k = "d_%s_%d" % (q, i)
                self.sems[k] = self._sem(k)
                self.dpool[q]["keys"].append(k)
                self.dpool[q]["cnt"].append(0)
        self.out_events = []

    def _sem(self, name):
        cm = self.nc.semaphore(name)
        s = cm.__enter__()
        self._stack.append(cm)
        return s

    def close(self):
        for cm in reversed(self._stack):
            cm.__exit__(None, None, None)

    def _deps(self, e, reads, writes):
        need = {}
        for t in reads:
            if t.w is not None:
                k, v = t.w
                if need.get(k, 0) < v:
                    need[k] = v
            if t.ps:
                for (k, v) in t.r:
                    if k.split("#")[0] != e and need.get(k, 0) < v:
                        need[k] = v
        for t in writes:
            if t.w is not None:
                k, v = t.w
                if need.get(k, 0) < v:
                    need[k] = v
            for (k, v) in t.r:
                if need.get(k, 0) < v:
                    need[k] = v
        waits = []
        kn = self.known[e]
        for k, v in need.items():
            if e == "pe" and k.split("#")[0] == "pe":
                continue
            if kn.get(k, 0) >= v:
                continue
            kn[k] = v
            waits.append((k, v))
        return waits

    def _mark(self, ev, reads, writes):
        for t in reads:
            t.r.append(ev)
            if len(t.r) > 64:
                mx = {}
                for k, v in t.r:
                    if mx.get(k, 0) < v:
                        mx[k] = v
                t.r = list(mx.items())
        for t in writes:
            t.w = ev
            t.r = []

    def op(self, e, fn, reads=(), writes=()):
        waits = self._deps(e, reads, writes)
        if self.seq[e] >= 6000:
            self.epoch[e] += 1
            self.cur[e] = "%s#%d" % (e, self.epoch[e])
            self.sems[self.cur[e]] = self._sem("s_%s_%d" % (e, self.epoch[e]))
            self.seq[e] = 0
        self.seq[e] += 1
        ev = (self.cur[e], self.seq[e])
        self.lists[e].append(("op", waits, fn, self.cur[e]))
        self._mark(ev, reads, writes)
        return ev

    def dma(self, q, fn, reads=(), writes=(), is_output=False):
        waits = self._deps(q, reads, writes)
        p = self.dpool[q]
        i = p["nxt"]
        p["nxt"] = (i + 1) % len(p["keys"])
        k = p["keys"][i]
        prev = p["cnt"][i]
        if prev > 0 and self.known[q].get(k, 0) < prev:
            waits.append((k, prev))
            self.known[q][k] = prev
        p["cnt"][i] = prev + 16
        ev = (k, prev + 16)
        self.lists[q].append(("dma", waits, fn, k))
        self._mark(ev, reads, writes)
        if is_output:
            self.out_events.append(ev)
        return ev

    def wait_all_outputs(self, e="sp"):
        need = {}
        for k, v in self.out_events:
            if need.get(k, 0) < v:
                need[k] = v
        waits = [(k, v) for k, v in need.items() if self.known[e].get(k, 0) < v]
        for k, v in waits:
            self.known[e][k] = v
        self.lists[e].append(("wait", waits))
        self.out_events = []

    def flush(self):
        dw = []
        for q, p in self.dpool.items():
            for kk, cnt in zip(p["keys"], p["cnt"]):
                if cnt > 0 and self.known["sp"].get(kk, 0) < cnt:
                    dw.append((kk, cnt))
                    self.known["sp"][kk] = cnt
        self.lists["sp"].append(("wait", dw))
        lists = self.lists
        self.lists = {e: [] for e in self.ENG}
        sems = self.sems
        with self.nc.Block() as block:
            def mk(e):
                def body(eng):
                    for item in lists[e]:
                        for (k, v) in item[1]:
                            eng.wait_ge(sems[k], v)
                        if item[0] == "op":
                            item[2](eng).then_inc(sems[item[3]], 1)
                        elif item[0] == "dma":
                            item[2](eng).then_inc(sems[item[3]], 16)
                return body
            block.tensor(mk("pe"))
            block.scalar(mk("act"))
            block.vector(mk("dve"))
            block.gpsimd(mk("pool"))
            block.sync(mk("sp"))


class K:
    def __init__(self, nc):
        self.nc = nc
        self.S = Sched(nc)
        self.dram = {}
        self._evac = 0

    def din(self, name, shape, dt=F32):
        t = self.nc.dram_tensor(name, list(shape), dt, kind="ExternalInput")
        self.dram[name] = t
        return t.ap()

    def dout(self, name, shape, dt=F32):
        t = self.nc.dram_tensor(name, list(shape), dt, kind="ExternalOutput")
        self.dram[name] = t
        return t.ap()

    def mm(self, out, lhsT, rhs, start, stop, R, W, sgc=False):
        self.S.op("pe", lambda e: e.matmul(out, lhsT=lhsT, rhs=rhs, start=start, stop=stop, skip_group_check=sgc), R, W)

    def tr(self, out, in_, ident, R, W):
        self.S.op("pe", lambda e: e.transpose(out=out, in_=in_, identity=ident), R, W)

    def act(self, out, in_, func, R, W, bias=None, scale=None, accum_out=None):
        kw = {}
        if bias is not None:
            kw["bias"] = bias
        if scale is not None:
            kw["scale"] = scale
        if accum_out is not None:
            kw["accum_out"] = accum_out
        self.S.op("act", lambda e: e.activation(out=out, in_=in_, func=func, **kw), R, W)

    def tt(self, eng, out, in0, in1, op, R, W):
        self.S.op(eng, lambda e: e.tensor_tensor(out=out, in0=in0, in1=in1, op=op), R, W)

    def ts(self, eng, out, in0, s1, s2, op0, op1, R, W):
        if op1 is None:
            self.S.op(eng, lambda e: e.tensor_scalar(out=out, in0=in0, scalar1=s1, scalar2=None, op0=op0), R, W)
        else:
            self.S.op(eng, lambda e: e.tensor_scalar(out=out, in0=in0, scalar1=s1, scalar2=s2, op0=op0, op1=op1), R, W)

    def stt(self, out, in0, scalar, in1, op0, op1, R, W):
        self.S.op("dve", lambda e: e.scalar_tensor_tensor(out=out, in0=in0, scalar=scalar, in1=in1, op0=op0, op1=op1), R, W)

    def cp(self, eng, out, in_, R, W):
        if eng == "act":
            self.S.op("act", lambda e: e.copy(out=out, in_=in_), R, W)
        else:
            self.S.op(eng, lambda e: e.tensor_copy(out=out, in_=in_), R, W)

    def evac(self, out, in_, R, W):
        self._evac ^= 1
        self.cp("act" if self._evac else "dve", out, in_, R, W)

    def memset(self, eng, ap, val, W):
        self.S.op(eng, lambda e: e.memset(ap, val), (), W)

    def ld(self, out, in_, W, R=(), q="sp"):
        self.S.dma(q, lambda e: e.dma_start(out=out, in_=in_), R, W)

    def ldc(self, out, in_, W, R=()):
        self.S.dma("pool", lambda e: e.dma_start(out=out, in_=in_), R, W)

    def st(self, out, in_, R, q="sp"):
        self.S.dma(q, lambda e: e.dma_start(out=out, in_=in_), R, (), is_output=True)


import os
STOP = os.environ.get('KSTOP', '')
DCUT = int(os.environ.get('DCUT', '9'))


def build_program():
    nc = bass.Bass("TRN2", target_bir_lowering=False)
    k = K(nc)
    S = k.S
    xT_d = k.din("xT", [128, KC, NT])
    cT_d = k.din("cT", [128, KC, 5])
    wada_d = k.din("w_ada", [D, 6 * D])
    bada_d = k.din("b_adaT", [128, 48])
    g1_d = k.din("g1T", [128, KC])
    g2_d = k.din("g2T", [128, KC])
    gf_d = k.din("gfT", [128, KC])
    win_d = k.din("w_in", [D, IN_COLS])
    lng_d = k.din("ln_g_bc", [128, 512])
    lnb_d = k.din("ln_b_bc", [128, 512])
    ident_d = k.din("ident", [128, 128])
    pe_d = k.din("pe_bc", [128, 2, 2, 64])
    swin_d = k.din("state_win", [NS, 512, 256])

    wsT_d = k.din("wsT", [128, 4, 128])
    wss_d = k.din("wssT", [4, 4, 4])
    bs_d = k.din("bs", [1, 4, 128])
    b1r_d = k.din("b1r", [128, 2])
    b2r_d = k.din("b2r", [128, 2])
    w1_d = k.din("cmp_w1", [2, 64, 64, 64])
    w2_d = k.din("cmp_w2", [2, 64, 64])
    msk_d = k.din("msk", [128, 4, 16, 32])
    E_d = k.din("Emat", [64, 2048])
    Cm_d = k.din("Cm", [128, 2, 128])
    wba_d = k.din("w_branch_a", [512, D])
    wbb_d = k.din("w_branch_b", [512, D])
    wout_d = k.din("w_out", [D, D])
    wup_d = k.din("w_up", [D, 2 * DFF])
    wdn_d = k.din("w_down", [DFF, D])
    wcv_d = k.din("w_convT", [128, 44, 3])
    bcv_d = k.din("b_convT", [128, 44])
    sprev_d = k.din("sprevT", [128, 44, 4, 2])
    cache_d = k.din("cache2d", [5120 * 128, 512]) if not STOP else None
    pt_d = k.din("pt_bc", [NS, 128, 128], I32)
    Gm_d = k.din("Gm", [64, 64]); SelT_d = k.din("SelT", [4, 64]); Hsel_d = k.din("Hsel", [64, 8])
    CN_d = k.din("CN", [64, 4]); CW_d = k.din("CW", [64, 4])
    b2k_d = k.din("b2k", [128, 1]); b2v_d = k.din("b2v", [128, 128])
    yT_o = k.dout("yT", [128, KC, NT])
    conv_o = k.dout("convT", [128, 440])
    kv_o = k.dout("kv_tok", [NT, 512])
    winp_o = k.dout("win_p", [512, 256])
    wins_o = k.dout("win_s", [NS, 512, 256])
    vch_o = k.dout("vchunk", [NS * TS, 512])

    with ExitStack() as top:
        def sb(name, shape, dt, es=top):
            return es.enter_context(nc.sbuf_tensor("sb_" + name, list(shape), dt))

        PS = [top.enter_context(nc.psum_tensor("ps%d" % i, [128, 512], F32)) for i in range(8)]
        PT = [Tl("ps%d" % i, ps=True) for i in range(8)]
        psi = [0]

        def nextps():
            i = psi[0]
            psi[0] = (i + 1) % 8
            return PS[i], PT[i]

        ident_f = sb("ident_f", [128, 128], F32); T_identf = Tl()
        ident_b = sb("ident_b", [128, 128], BF16); T_identb = Tl()
        ones_b = sb("ones_b", [128, 128], BF16); T_ones = Tl()
        modT = sb("modT", [128, 48, 5], F32); T_mod = Tl()
        A1 = sb("A1", [128, KC, 5], F32); T_A1 = Tl()
        A2 = sb("A2", [128, KC, 5], F32); T_A2 = Tl()
        hT = sb("hT", [128, KC, NT], BF16)
        T_h = [[Tl() for _ in TBS] for _ in range(KC)]
        x1raw = sb("x1raw", [128, KC * NT], F32)
        mid_scope = top.enter_context(ExitStack())
        uT = sb("uT", [128, 4, NT], BF16, mid_scope)
        T_u = [[Tl() for _ in TBS] for _ in range(4)]
        x1T = x1raw[:, :].rearrange("p (k t) -> p k t", k=KC)
        xbv = x1raw[:, :].bitcast(BF16)
        T_x1 = [[Tl() for _ in TBS] for _ in range(KC)]

        k.ld(ident_f[:], ident_d[:, :], [T_identf])
        k.cp("dve", ident_b[:], ident_f[:], [T_identf], [T_identb])
        k.memset("pool", ones_b[:], 1.0, [T_ones])

        with ExitStack() as pa:
            cs = sb("cs", [128, KC, 5], F32, pa); T_cs = Tl()
            bT = sb("bT", [128, 48], F32, pa); T_bT = Tl()
            g1 = sb("g1", [128, KC], F32, pa); T_g1 = Tl()
            g2 = sb("g2", [128, KC], F32, pa); T_g2 = Tl()
            wab = [sb("wab%d" % i, [128, KC, 512], F32, pa) for i in range(2)]
            T_wab = [Tl(), Tl()]
            xa = x1T; T_xa = [Tl() for _ in range(KC)]
            sq = [sb("sq%d" % i, [128, NT], BF16, pa) for i in range(2)]; T_sq = [Tl(), Tl()]
            rstd = sb("rstd", [128, NT], F32, pa); T_rstd = [Tl() for _ in TBS]
            tmp = [sb("tmpA%d" % i, [128, NT], F32, pa) for i in range(2)]; T_tmp = [Tl(), Tl()]

            k.ld(cs[:], cT_d[:, :, :], [T_cs])
            k.ld(bT[:], bada_d[:, :], [T_bT])
            k.ld(g1[:], g1_d[:, :], [T_g1])
            k.ld(g2[:], g2_d[:, :], [T_g2])
            k.act(cs[:], cs[:], AF.Silu, [T_cs], [T_cs])
            for kc in range(KC):
                k.ld(xa[:, kc, :], xT_d[:, kc, :], [T_xa[kc]])
            psm, T_psm = PS[7], PT[7]
            wv = wada_d.rearrange("(kc p) n -> p kc n", p=128)
            for jb in range(12):
                w = wab[jb % 2]; Tw = T_wab[jb % 2]
                for kc in range(KC):
                    k.ld(w[:, kc, :], wv[:, kc, jb * 512:(jb + 1) * 512], [Tw], q="sp")
                for j in range(4):
                    col = (jb * 4 + j) * 5
                    for kc in range(KC):
                        k.mm(psm[:, col:col + 5], w[:, kc, j * 128:(j + 1) * 128], cs[:, kc, :],
                             kc == 0, kc == KC - 1, [Tw, T_cs], [T_psm])
            k.tt("dve", modT[:], psm[:, 0:240].rearrange("p (j r) -> p j r", r=5),
                 bT[:, :].unsqueeze(2).to_broadcast([128, 48, 5]), ALU.add, [T_psm, T_bT], [T_mod])
            for (Ax, TAx, gx, Tgx, off) in ((A1, T_A1, g1, T_g1, 8), (A2, T_A2, g2, T_g2, 32)):
                k.ts("dve", Ax[:], modT[:, off:off + 8, :], 1.0, None, ALU.add, None, [T_mod], [TAx])
                k.tt("dve", Ax[:], Ax[:], gx[:, :].unsqueeze(2).to_broadcast([128, KC, 5]), ALU.mult, [TAx, Tgx], [TAx])
            for kc in range(KC):
                s_ = sq[kc % 2]; Ts = T_sq[kc % 2]
                k.act(s_[:], xa[:, kc, :], AF.Square, [T_xa[kc]], [Ts])
                for bi, (t0, n) in enumerate(TBS):
                    k.mm(PS[bi][:, 0:n], ones_b[:], s_[:, t0:t0 + n], kc == 0, kc == KC - 1, [T_ones, Ts], [PT[bi]])
            for bi, (t0, n) in enumerate(TBS):
                k.act(rstd[:, t0:t0 + n], PS[bi][:, 0:n], AF.Sqrt, [PT[bi]], [T_rstd[bi]], bias=EPS, scale=1.0 / D)
                k.S.op("dve", (lambda o: (lambda e: e.reciprocal(out=o, in_=o)))(rstd[:, t0:t0 + n]), [T_rstd[bi]], [T_rstd[bi]])
            for kc in range(KC):
                t_ = tmp[kc % 2]; Tt = T_tmp[kc % 2]
                k.tt("dve", t_[:], xa[:, kc, :], rstd[:], ALU.mult, [T_xa[kc]] + T_rstd, [Tt])
                k.act(hT[:, kc, 0:TP], t_[:, 0:TP], AF.Identity, [Tt, T_A1, T_mod], T_h[kc][0:4],
                      bias=modT[:, kc, 0:1], scale=A1[:, kc, 0:1])
                for s in range(NS):
                    c0 = TP + s * TS
                    k.act(hT[:, kc, c0:c0 + TS], t_[:, c0:c0 + TS], AF.Identity, [Tt, T_A1, T_mod], [T_h[kc][4]],
                          bias=modT[:, kc, 1 + s:2 + s], scale=A1[:, kc, 1 + s:2 + s])
            S.flush()

        with ExitStack() as pb_:
            att = pb_
            vn = xbv[:, 0:10240].rearrange("p (t c) -> p t c", t=20); T_vn = [Tl() for _ in TTS]
            gates = sb("gates", [128, 20, 24], F32, att); T_gates = [Tl() for _ in TTS]
            pe_bc = sb("pe_bc_sb", [128, 2, 2, 64], F32, att); T_pe = Tl()
            qTs = sb("qTs", [128, 4, 16], BF16, att); T_qTs = Tl()
            kTs = sb("kTs", [128, 4, 16], BF16, att); T_kTs = Tl()
            vnew = sb("vnew", [4, NS, 2, 130], BF16, att); T_vnew = Tl()
            pr = att.enter_context(ExitStack())
            qT = sb("qT", [128, 4, NT], BF16, pr); T_q = [[Tl() for _ in TBS] for _ in range(4)]
            kT = sb("kT", [128, 4, NT], BF16, pr); T_k = [[Tl() for _ in TBS] for _ in range(4)]
            vsel = sb("vsel", [128, 20, 2, 65], BF16, pr); T_vsel = [Tl() for _ in TTS]
            vwin = sb("vwin", [128, 20, 2, 65], BF16, pr); T_vwin = [Tl() for _ in TTS]
            Xc = xbv[:, 10240:14336].rearrange("p (t s c) -> p t s c", t=16, s=2); T_Xc = [Tl() for _ in range(16)]
            lng = sb("lng", [128, 512], F32, pr); T_lng = Tl()
            lnb = sb("lnb", [128, 512], F32, pr); T_lnb = Tl()
            k.ld(pe_bc[:], pe_d[:, :, :, :], [T_pe])
            k.ld(lng[:], lng_d[:, :], [T_lng])
            k.ld(lnb[:], lnb_d[:, :], [T_lnb])
            k.memset("pool", vsel[:], 1.0, T_vsel)
            k.memset("pool", vwin[:], 1.0, T_vwin)

            with ExitStack() as pb:
                wu = sb("wu", [128, KC, 512], BF16, pb); T_wu = Tl()
                wq = sb("wq", [128, KC, 512], BF16, pb); T_wq = Tl()
                wvv = wq; T_wv = T_wq
                wkd = wu[:, :, :].rearrange("p k (j u d) -> p k j u d", j=4, u=2); T_wkd = T_wu
                wkv = sb("wkv", [128, KC, 792], BF16, pb); T_wkv = Tl()
                vg = [sb("vg%d" % i, [128, 512], F32, pb) for i in range(2)]; T_vg = [Tl(), Tl()]
                vt = [sb("vt%d" % i, [128, 512], F32, pb) for i in range(2)]; T_vt = [Tl(), Tl()]
                st6 = [sb("st6%d" % i, [128, 8], F32, pb) for i in range(2)]; T_st6 = [Tl(), Tl()]
                mv = [sb("mv%d" % i, [128, 4], F32, pb) for i in range(2)]; T_mv = [Tl(), Tl()]
                kvo = [sb("kvo0", [128, 768], F32, pb)] * 2; T_kvo = [Tl()] * 2

                wi = win_d.rearrange("(kc p) n -> p kc n", p=128)
                k.ldc(wu[:], wi[:, :, 0:512], [T_wu])
                k.ldc(wq[:], wi[:, :, 1024:1536], [T_wq])
                k.ldc(wkv[:], wi[:, :, 1536:2328], [T_wkv])

                def fm_proj(wt, Tw, nch, wsl, evac):
                    for m in range(nch):
                        for bi, (t0, n) in enumerate(TBS):
                            ps, Tp = nextps()
                            for kc in range(KC):
                                k.mm(ps[:, 0:n], wsl(wt, kc, m), hT[:, kc, t0:t0 + n], kc == 0, kc == KC - 1,
                                     [Tw, T_h[kc][bi]], [Tp])
                            evac(m, bi, t0, n, ps, Tp)

                fm_proj(wu, T_wu, 4, lambda wt, kc, m: wt[:, kc, m * 128:(m + 1) * 128],
                        lambda m, bi, t0, n, ps, Tp: k.act(uT[:, m, t0:t0 + n], ps[:, 0:n], AF.Gelu_apprx_tanh, [Tp], [T_u[m][bi]]))
                fm_proj(wq, T_wq, 4, lambda wt, kc, m: wt[:, kc, m * 128:(m + 1) * 128],
                        lambda m, bi, t0, n, ps, Tp: k.act(qT[:, m, t0:t0 + n], ps[:, 0:n], AF.Copy, [Tp], [T_q[m][bi]], scale=0.125))
                for j, (slot, kvh) in enumerate(((2, 0), (2, 1), (4, 0), (4, 1))):
                    c0 = 1536 + slot * 128 + kvh * 64
                    for dup in range(2):
                        k.ldc(wkd[:, :, j, dup, :], wi[:, :, c0:c0 + 64], [T_wkd])
                fm_proj(wkd, T_wkd, 4, lambda wt, kc, m: wt[:, kc, m, :, :],
                        lambda m, bi, t0, n, ps, Tp: k.evac(kT[:, m, t0:t0 + n], ps[:, 0:n], [Tp], [T_k[m][bi]]))

                k.ldc(wvv[:], wi[:, :, 512:1024], [T_wv])
                def tb_of(t0):
                    return min(t0 // 512, 4)

                for ti, (t0, n) in enumerate(TTS):
                    bi = tb_of(t0)
                    ps, Tp = nextps()
                    for kc in range(KC):
                        k.mm(ps[0:n, :], hT[:, kc, t0:t0 + n], wvv[:, kc, :], kc == 0, kc == KC - 1, [T_wv, T_h[kc][bi]], [Tp])
                    g_ = vg[ti % 2]; Tg = T_vg[ti % 2]
                    t_ = vt[ti % 2]; Tt = T_vt[ti % 2]
                    s6 = st6[ti % 2]; Ts6 = T_st6[ti % 2]
                    m_ = mv[ti % 2]; Tm = T_mv[ti % 2]
                    k.act(g_[0:n, :], ps[0:n, :], AF.Gelu_apprx_tanh, [Tp], [Tg])
                    S.op("dve", (lambda o, i: (lambda e: e.bn_stats(out=o, in_=i)))(s6[0:n, 0:6], g_[0:n, :]), [Tg], [Ts6])
                    S.op("dve", (lambda o, i: (lambda e: e.bn_aggr(out=o, in_=i)))(m_[0:n, 0:2], s6[0:n, 0:6]), [Ts6], [Tm])
                    k.act(m_[0:n, 2:3], m_[0:n, 1:2], AF.Sqrt, [Tm], [Tm], bias=EPS, scale=1.0)
                    S.op("dve", (lambda o, i: (lambda e: e.reciprocal(out=o, in_=i)))(m_[0:n, 3:4], m_[0:n, 2:3]), [Tm], [Tm])
                    k.ts("dve", t_[0:n, :], g_[0:n, :], m_[0:n, 0:1], m_[0:n, 3:4], ALU.subtract, ALU.mult, [Tg, Tm], [Tt])
                    k.tt("dve", t_[0:n, :], t_[0:n, :], lng[0:n, :], ALU.mult, [Tt, T_lng], [Tt])
                    k.tt("dve", t_[0:n, :], t_[0:n, :], lnb[0:n, :], ALU.add, [Tt, T_lnb], [Tt])
                    k.cp("pool", vn[0:n, ti, :], t_[0:n, :], [Tt], [T_vn[ti]])
                    if ti >= 16:
                        s = ti - 16
                        k.st(vch_o[s * TS:(s + 1) * TS, :], t_[0:n, :], [Tt])
                    psa, Tpa = nextps()
                    psb, Tpb = nextps()
                    for kc in range(KC):
                        k.mm(psa[0:n, :], hT[:, kc, t0:t0 + n], wkv[:, kc, 0:512], kc == 0, kc == KC - 1, [T_wkv, T_h[kc][bi]], [Tpa])
                    for kc in range(KC):
                        k.mm(psb[0:n, 0:280], hT[:, kc, t0:t0 + n], wkv[:, kc, 512:792], kc == 0, kc == KC - 1, [T_wkv, T_h[kc][bi]], [Tpb])
                    o_ = kvo[ti % 2]; To = T_kvo[ti % 2]
                    k.cp("act", o_[0:n, 0:512], psa[0:n, :], [Tpa], [To])
                    k.cp("act", o_[0:n, 512:768], psb[0:n, 0:256], [Tpb], [To])
                    k.st(kv_o[t0:t0 + n, :], o_[0:n, 0:512], [To])
                    if ti < 16:
                        k.tt("dve", Xc[:, ti, :, :].rearrange("p s (k d) -> p s k d", k=2),
                             psa[:, 0:256].rearrange("p (s k d) -> p s k d", s=2, k=2), pe_bc[:], ALU.add, [Tpa, T_pe], [T_Xc[ti]])
                    k.cp("dve", vsel[0:n, ti, :, 0:64], psa[0:n, 384:512].rearrange("p (k d) -> p k d", k=2), [Tpa], [T_vsel[ti]])
                    k.cp("dve", vwin[0:n, ti, :, 0:64], psb[0:n, 128:256].rearrange("p (k d) -> p k d", k=2), [Tpb], [T_vwin[ti]])
                    k.act(gates[0:n, ti, :], psb[0:n, 256:280], AF.Sigmoid, [Tpb], [T_gates[ti]])
                    if 12 <= ti < 16:
                        r0 = (ti - 12) * 128
                        k.st(winp_o[r0:r0 + 128, :], o_[:, 512:768], [To])
                    if ti >= 16:
                        s = ti - 16
                        k.st(wins_o[s, 512 - TS:512, :], o_[0:n, 512:768], [To])
                for s in range(NS):
                    k.S.dma("sp", (lambda s: (lambda e: e.dma_start(out=wins_o[s, 0:512 - TS, :], in_=swin_d[s, TS:512, :])))(s), (), (), is_output=True)
                S.flush()

            kcT = sb("kcT", [128, 2, 16, 2], BF16, pr); T_kc = Tl()
            vcT = sb("vcT", [128, 2, 16, 2], BF16, pr); T_vcT = Tl()
            vc = sb("vc", [32, 2, 64], BF16, pr); T_vc = Tl()
            PSb = [PS[i][:, :].bitcast(BF16) for i in range(8)]

            def tb_of(t0):
                return min(t0 // 512, 4)

            with ExitStack() as pc:
                Wd = xbv[:, 14336:14336 + 16384].rearrange("p (s d m) -> p s d m", s=2, d=64); T_Wd = Tl()
                wsT_f = sb("wsT_f", [128, 4, 128], F32, pc); T_wsf = Tl()
                wsT = sb("wsT_b", [128, 4, 128], BF16, pc); T_ws = Tl()
                wss_f = sb("wss_f", [4, 4, 4], F32, pc); T_wssf = Tl()
                wss = sb("wss", [4, 4, 4], BF16, pc); T_wss = Tl()
                bs_f = sb("bs_f", [1, 4, 128], F32, pc); T_bs = Tl()
                ones_f = sb("ones_f", [1, 128], F32, pc); T_onesf = Tl()
                W2d = sb("W2d", [128, 2, 2, 128], BF16, pc); T_W2d = Tl()
                b1r = sb("b1r", [128, 2], F32, pc); b2r = sb("b2r", [128, 2], F32, pc); T_b12 = Tl()
                hidT = sb("hidT", [128, 2, 32], BF16, pc); T_hid = Tl()

                k.ld(wsT_f[:], wsT_d[:, :, :], [T_wsf])
                k.ld(wss_f[:], wss_d[:, :, :], [T_wssf])
                k.ld(bs_f[:], bs_d[:, :, :], [T_bs])
                k.ld(b1r[:], b1r_d[:, :], [T_b12])
                k.ld(b2r[:], b2r_d[:, :], [T_b12])
                k.memset("dve", ones_f[:], 1.0, [T_onesf])
                S.op("pool", lambda e: e.affine_select(out=wsT[:], in_=wsT_f[:], pattern=[[0, 4], [1, 128]], compare_op=ALU.is_ge,
                                                       fill=0.0, base=0, channel_multiplier=-1), [T_wsf], [T_ws])
                S.op("pool", lambda e: e.affine_select(out=wss[:], in_=wss_f[:], pattern=[[0, 4], [1, 4]], compare_op=ALU.is_ge,
                                                       fill=0.0, base=0, channel_multiplier=-1), [T_wssf], [T_wss])
                k.memset("pool", Wd, 0.0, [T_Wd])
                k.memset("pool", W2d[:], 0.0, [T_W2d])
                for sl in range(2):
                    for half in range(2):
                        k.ldc(Wd[half * 64:(half + 1) * 64, sl, :, half * 64:(half + 1) * 64], w1_d[sl, :, :, :], [T_Wd])
                    for parity in range(2):
                        for dup in range(2):
                            k.ldc(W2d[parity * 64:(parity + 1) * 64, sl, parity, dup * 64:(dup + 1) * 64], w2_d[sl, :, :], [T_W2d])

                for ti, (t0, n) in enumerate(TTS):
                    bi = tb_of(t0)
                    ps, Tp = nextps()
                    for g in range(4):
                        if ti < 16:
                            k.mm(ps[:, g * 128:(g + 1) * 128], vn[:, ti, g * 128:(g + 1) * 128], wsT[:, g, :], True, False, [T_vn[ti], T_ws], [Tp])
                            k.mm(ps[:, g * 128:(g + 1) * 128], ones_f[0:1, :], bs_f[0:1, g, :], False, True, [T_onesf, T_bs], [Tp])
                        else:
                            k.mm(ps[:, g * 128:g * 128 + 4], vn[0:4, ti, g * 128:(g + 1) * 128], wss[0:4, g, :], True, False, [T_vn[ti], T_wss], [Tp])
                            k.mm(ps[:, g * 128:g * 128 + 4], ones_f[0:1, :], bs_f[0:1, g, 0:4], False, True, [T_onesf, T_bs], [Tp])
                    uv = uT[:, :, t0:t0 + n]
                    Tus = [T_u[g][bi] for g in range(4)]
                    k.tt("dve", uv, uv, ps[:, :].rearrange("p (g t) -> p g t", g=4)[:, :, 0:n], ALU.mult, [Tp] + Tus, Tus)

                Xc5 = Xc.rearrange("p t s (k d) -> p t s k d", k=2)
                for sl in range(2):
                    ps, Tp = nextps()
                    for d in range(64):
                        k.mm(ps[:, 0:32].rearrange("p (t k) -> p t k", k=2), Wd[:, sl, d, :], Xc5[:, :, sl, :, d], d == 0, d == 63,
                             [T_Wd] + T_Xc, [Tp])
                    k.act(hidT[:, sl, :], ps[:, 0:32], AF.Gelu_apprx_tanh, [Tp, T_b12], [T_hid], bias=b1r[:, sl:sl + 1])
                    for parity in range(2):
                        ps2, Tp2 = nextps()
                        k.mm(ps2[:, 0:32], W2d[:, sl, parity, :], hidT[:, sl, :], True, True, [T_W2d, T_hid], [Tp2])
                        dst = (kcT if sl == 0 else vcT)
                        Td = T_kc if sl == 0 else T_vcT
                        k.act(dst[:, :, :, parity], ps2[:, 0:32].rearrange("p (t k) -> p k t", k=2), AF.Identity, [Tp2, T_b12], [Td],
                              bias=b2r[:, sl:sl + 1])
                for kvh in range(2):
                    pi_ = psi[0]
                    ps, Tp = nextps()
                    k.tr(PSb[pi_][0:32, 0:64], vcT[0:64, kvh, :, :].rearrange("p t q -> p (t q)"), ident_b[0:64, 0:64], [T_vcT, T_identb], [Tp])
                    k.cp("dve", vc[:, kvh, :], PSb[pi_][0:32, 0:64], [Tp], [T_vc])
                S.flush()
                if STOP == 'C':
                    S.wait_all_outputs("sp"); S.flush(); S.close(); return nc

            with ExitStack() as pd:
                ybT = xbv[:, 0:4 * NT].rearrange("p (c t) -> p c t", c=4); T_yb = [Tl() for _ in TBS]
                ybacc = sb("ybacc", [128, 4, 512], F32, pd); T_ya = [Tl() for _ in range(4)]
                msk = sb("msk", [128, 4, 16, 32], F32, pd); T_msk = Tl()
                Eb = sb("Eb", [64, 2048], BF16, pd); T_E = Tl()
                Cm = sb("Cm", [128, 2, 128], BF16, pd); T_C = Tl()
                penT = sb("penT", [64, 512], BF16, pd); T_penT = Tl()
                PTb = [sb("PTb%d" % i, [128, 512], BF16, pd) for i in range(4)]; T_PT = [Tl() for _ in range(4)]
                sm = sb("sm", [128, 8, 32], F32, pd); T_sm = Tl()
                ee = sb("ee", [128, 8, 32], F32, pd); T_ee = Tl()
                pp = sb("pp", [128, 8, 32], F32, pd); T_pp = Tl()
                pbf = sb("pbf", [128, 8, 32], BF16, pd); T_pbf = Tl()
                pT = sb("pT", [32, 8, 128], BF16, pd); T_pT = Tl()
                imp = sb("imp", [128, 2, 32], F32, pd); T_imp = Tl()
                t8 = sb("t8", [128, 16], F32, pd); T_t8 = Tl()
                wk8 = sb("wk8", [128, 32], F32, pd); T_wk8 = Tl()
                penf = sb("penf", [128, 2, 32], F32, pd); T_penf = Tl()
                pen = sb("pen", [128, 2, 32], BF16, pd); T_pen = Tl()
                mx = sb("mx", [128, 8], F32, pd); T_mx = Tl()
                rs = sb("rs", [128, 8], F32, pd); T_rs = Tl()
                rs4 = sb("rs4", [128, 4], F32, pd); T_rs4 = Tl()
                tmpo = sb("tmpo", [128, 4, 64], F32, pd); T_tmpo = Tl()
                ybb = sb("ybb", [128, 512], BF16, pd); T_ybb = Tl()

                k.ld(msk[:], msk_d[:, :, :, :], [T_msk])
                k.ldc(Eb[:], E_d[:, :], [T_E])
                k.ldc(Cm[:], Cm_d[:, :, :], [T_C])
                if 'NOS' in STOP:
                    k.memset("pool", ybT[:, :, TP:NT], 0.0, [T_yb[4]])

                rot = [4]

                def rps():
                    i = rot[0]
                    rot[0] = 4 + (i - 3) % 4
                    return i

                def B3(ap, sh):
                    return ap.to_broadcast(sh)

                for qb in range(4):
                    for j in range(4):
                        qt = 4 * qb + j
                        if DCUT < 1:
                            continue
                        piA = rps(); piB = rps()
                        for h in range(8):
                            base = (h % 2) * 64; ch = h // 2; kvh = h // 4
                            bk = piA if h % 2 == 0 else piB
                            k.mm(PS[bk][:, (h // 2) * 32:(h // 2 + 1) * 32], qT[base:base + 64, ch, qt * 128:(qt + 1) * 128],
                                 kcT[base:base + 64, kvh, :, :].rearrange("p t q -> p (t q)"), True, True, [T_q[ch][qb], T_kc], [PT[bk]])
                        for par, bk in ((0, piA), (1, piB)):
                            k.tt("dve", sm[:, par::2, :], PS[bk][:, 0:128].rearrange("p (h n) -> p h n", h=4),
                                 B3(msk[:, 0, qt, :].unsqueeze(1), [128, 4, 32]), ALU.add, [PT[bk], T_msk], [T_sm])
                        S.op("dve", lambda e: e.tensor_reduce(out=mx[:, 0:8], in_=sm[:], axis=AX.X, op=ALU.max), [T_sm], [T_mx])
                        k.tt("dve", sm[:], sm[:], B3(mx[:, 0:8].unsqueeze(2), [128, 8, 32]), ALU.subtract, [T_sm, T_mx], [T_sm])
                        k.act(ee[:], sm[:], AF.Exp, [T_sm], [T_ee])
                        k.tt("dve", ee[:], ee[:], B3(msk[:, 1, qt, :].unsqueeze(1), [128, 8, 32]), ALU.mult, [T_ee, T_msk], [T_ee])
                        S.op("dve", lambda e: e.tensor_reduce(out=rs[:, 0:8], in_=ee[:], axis=AX.X, op=ALU.add), [T_ee], [T_rs])
                        k.ts("dve", rs[:], rs[:], 1e-30, None, ALU.max, None, [T_rs], [T_rs])
                        S.op("dve", lambda e: e.reciprocal(out=rs[:], in_=rs[:]), [T_rs], [T_rs])
                        k.tt("dve", pp[:], ee[:], B3(rs[:, 0:8].unsqueeze(2), [128, 8, 32]), ALU.mult, [T_ee, T_rs], [T_pp])
                        k.cp("pool", pbf[:], pp[:], [T_pp], [T_pbf])
                        if qb >= 2 and 'NOTOPK' not in STOP:
                            S.op("dve", lambda e: e.tensor_reduce(out=imp[:], in_=pp[:].rearrange("p (k g) n -> p k n g", k=2), axis=AX.X, op=ALU.add),
                                 [T_pp], [T_imp])
                            k.tt("dve", imp[:], imp[:], B3(msk[:, 2, qt, :].unsqueeze(1), [128, 2, 32]), ALU.mult, [T_imp, T_msk], [T_imp])
                            k.tt("dve", imp[:], imp[:], B3(msk[:, 3, qt, :].unsqueeze(1), [128, 2, 32]), ALU.add, [T_imp, T_msk], [T_imp])
                            for kvh in range(2):
                                S.op("dve", (lambda kvh: lambda e: e.max(out=t8[:, 0:8], in_=imp[:, kvh, :]))(kvh), [T_imp], [T_t8])
                                S.op("dve", (lambda kvh: lambda e: e.match_replace(out=wk8[:], in_to_replace=t8[:, 0:8], in_values=imp[:, kvh, :], imm_value=-1e30))(kvh),
                                     [T_imp, T_t8], [T_wk8])
                                S.op("dve", lambda e: e.max(out=t8[:, 8:16], in_=wk8[:]), [T_wk8], [T_t8])
                                k.ts("dve", penf[:, kvh, :], imp[:, kvh, :], t8[:, 15:16], -NEGB, ALU.is_ge, ALU.mult, [T_imp, T_t8], [T_penf])
                            k.ts("dve", pen[:], penf[:], NEGB, None, ALU.add, None, [T_penf], [T_pen])
                            pi2 = rps()
                            k.tr(PSb[pi2][0:64, 0:128], pen[:].rearrange("p k n -> p (k n)"), ident_b[:], [T_pen, T_identb], [PT[pi2]])
                            k.cp("act", penT[:, j * 128:(j + 1) * 128], PSb[pi2][0:64, 0:128], [PT[pi2]], [T_penT])
                        if DCUT < 2:
                            continue
                        pi3 = rps()
                        for h in range(8):
                            k.tr(PSb[pi3][0:32, h * 128:(h + 1) * 128], pbf[:, h, :], ident_b[:], [T_pbf, T_identb], [PT[pi3]])
                        k.cp("act", pT[:].rearrange("p h q -> p (h q)"), PSb[pi3][0:32, 0:1024], [PT[pi3]], [T_pT])
                        if DCUT < 3:
                            continue
                        pi4 = rps(); pso, Tpo = PS[pi4], PT[pi4]
                        for h in range(8):
                            k.mm(pso[:, h * 64:(h + 1) * 64], pT[:, h, :], vc[:, h // 4, :], True, True, [T_pT, T_vc], [Tpo])
                        k.tt("dve", ybacc[:, j, :].rearrange("p (h d) -> p h d", h=8), pso[:, :].rearrange("p (h d) -> p h d", h=8),
                             B3(gates[:, qt, 0:8].unsqueeze(2), [128, 8, 64]), ALU.mult, [Tpo, T_gates[qt]], [T_ya[j]])

                    for kvh in range(2 if 'NOSEL' not in STOP else 0):
                        for bri, (vaug, T_va) in ((1, (vsel, T_vsel)), (2, (vwin, T_vwin))):
                            first = [True] * 4
                            kt_lo = 0 if bri == 1 else max(0, 4 * qb - 4)
                            q0 = qb * 512
                            km = (0 if bri == 1 else 2) + kvh
                            units = [(kt, hh) for kt in range(kt_lo, 4 * qb + 4) for hh in range(4)]

                            def scores(ui, kt, hh):
                                jlo = max(0, kt - 4 * qb)
                                jhi = 3 if bri == 1 else min(3, kt + 4 - 4 * qb)
                                c0, c1 = jlo * 128, (jhi + 1) * 128
                                h = 4 * kvh + hh
                                base = (h % 2) * 64; ch = h // 2
                                pi_ = rps(); ps, Tp = PS[pi_], PT[pi_]
                                grp = [(ps[:, c0:c1], kT[base:base + 64, km, kt * 128:(kt + 1) * 128], qT[base:base + 64, ch, q0 + c0:q0 + c1],
                                        [T_k[km][kt // 4], T_q[ch][qb]])]
                                if bri == 1 and qb >= 2:
                                    grp.append((ps[:, c0:c1], Eb[kvh * 32:(kvh + 1) * 32, kt * 128:(kt + 1) * 128], penT[kvh * 32:(kvh + 1) * 32, c0:c1],
                                                [T_E, T_penT]))
                                if kt >= 4 * qb:
                                    jd = kt - 4 * qb
                                    grp.append((ps[:, jd * 128:(jd + 1) * 128], ident_b[:], Cm[:, 0, :], [T_identb, T_C]))
                                if bri == 2 and 0 <= kt + 4 - 4 * qb <= 3:
                                    j4 = kt + 4 - 4 * qb
                                    grp.append((ps[:, j4 * 128:(j4 + 1) * 128], ident_b[:], Cm[:, 1, :], [T_identb, T_C]))
                                for gi, (o_, l_, r_, R_) in enumerate(grp):
                                    k.mm(o_, l_, r_, gi == 0, gi == len(grp) - 1, R_, [Tp])
                                pb_i = ui % 4
                                k.act(PTb[pb_i][:, c0:c1], ps[:, c0:c1], AF.Exp, [Tp], [T_PT[pb_i]])
                                return pb_i, jlo, jhi

                            def pvs(kt, hh, pb_i, jlo, jhi):
                                for j in range(jlo, jhi + 1):
                                    k.mm(PS[j][:, hh * 65:(hh + 1) * 65], PTb[pb_i][:, j * 128:(j + 1) * 128], vaug[:, kt, kvh, :],
                                         first[j], kt == 4 * qb + j, [T_PT[pb_i], T_va[kt]], [PT[j]], sgc=True)
                                    first[j] = False

                            LAG = 2
                            pend = []
                            for ui in range(len(units) + LAG):
                                if ui < len(units):
                                    pend.append(scores(ui, *units[ui]))
                                if ui >= LAG:
                                    pvs(*units[ui - LAG], *pend[ui - LAG])
                            for j in range(4):
                                qt = 4 * qb + j
                                po3 = PS[j][:, 0:260].rearrange("p (h c) -> p h c", c=65)
                                S.op("dve", (lambda po3: lambda e: e.reciprocal(out=rs4[:], in_=po3[:, :, 64]))(po3), [PT[j]], [T_rs4])
                                k.tt("dve", rs4[:], rs4[:], gates[:, qt, bri * 8 + 4 * kvh:bri * 8 + 4 * kvh + 4], ALU.mult, [T_rs4, T_gates[qt]], [T_rs4])
                                k.tt("dve", tmpo[:], po3[:, :, 0:64], B3(rs4[:, 0:4].unsqueeze(2), [128, 4, 64]), ALU.mult, [PT[j], T_rs4], [T_tmpo])
                                ya = ybacc[:, j, kvh * 256:(kvh + 1) * 256].rearrange("p (h d) -> p h d", h=4)
                                k.tt("dve", ya, ya, tmpo[:], ALU.add, [T_ya[j], T_tmpo], [T_ya[j]])
                    for j in range(4 if DCUT >= 4 else 0):
                        qt = 4 * qb + j
                        k.cp("act", ybb[:], ybacc[:, j, :], [T_ya[j]], [T_ybb])
                        pi_ = rps()
                        for c in range(4):
                            k.tr(PSb[pi_][:, c * 128:(c + 1) * 128], ybb[:, c * 128:(c + 1) * 128], ident_b[:], [T_ybb, T_identb], [PT[pi_]])
                        k.cp("dve", ybT[:, :, qt * 128:(qt + 1) * 128], PSb[pi_][:, 0:512].rearrange("p (c q) -> p c q", c=4), [PT[pi_]], [T_yb[qb]])
                k.cp("dve", qTs[:], qT[:, :, TP:NT], [T_q[m][4] for m in range(4)], [T_qTs])
                k.cp("dve", kTs[:], kT[:, :, TP:NT], [T_k[m][4] for m in range(4)], [T_kTs])
                for s in range(NS):
                    k.cp("pool", vnew[0:4, s, 0, :], vsel[0:4, 16 + s, :, :].rearrange("p k c -> p (k c)"), [T_vsel[16 + s]], [T_vnew])
                    k.cp("pool", vnew[0:4, s, 1, :], vwin[0:4, 16 + s, :, :].rearrange("p k c -> p (k c)"), [T_vwin[16 + s]], [T_vnew])
                S.flush()
                if STOP.startswith('D'):
                    S.wait_all_outputs("sp"); S.flush(); S.close(); return nc
            pr.close()
            with ExitStack() as ps_:
                kselT = sb("kselT", [128, NPG * 128], BF16, ps_); T_kst = [Tl() for _ in range(32)]
                vsl = sb("vsl", [128, NPG, 130], BF16, ps_); T_vsl = [Tl() for _ in range(32)]
                NPB = 4
                pgbuf = sb("pgbuf", [128, NPB, 512], F32, ps_)
                pg = [pgbuf[:, i, :] for i in range(NPB)]; T_pg = [Tl() for _ in range(NPB)]
                XcsB = [xbv[:, 8256 + b * 2048:8256 + (b + 1) * 2048].rearrange("p (t s c) -> p t s c", t=8, s=2) for b in range(2)]
                T_XcsB = [Tl(), Tl()]
                Wd = xbv[:, 14336:14336 + 16384].rearrange("p (s d m) -> p s d m", s=2, d=64); T_Wd2 = Tl()
                ptb = sb("ptb", [128, 128], I32, ps_); T_ptb = Tl()
                idxf = sb("idxf", [128, 128], F32, ps_); T_idxf = Tl()
                idxi = ptb; T_idxi = T_ptb
                iot_i = sb("iot_i", [128, 1], I32, ps_); iot_f = sb("iot_f", [128, 1], F32, ps_); T_iot = Tl()
                hidTs = sb("hidTs", [128, 2, 256], BF16, ps_); T_hid = Tl()
                kcTs = sb("kcTs", [128, 256], BF16, ps_); T_kcs = Tl()
                vcs = sb("vcs", [128, 2, 128], BF16, ps_); T_vcs = Tl()
                W2p = sb("W2p", [128, 2, 2, 128], BF16, ps_); T_W2p = Tl()
                w2v = sb("w2v", [128, 64], BF16, ps_); T_w2v = Tl()
                b1s = sb("b1s", [128, 2], F32, ps_); b2k = sb("b2k", [128, 1], F32, ps_); b2v = sb("b2v", [128, 128], F32, ps_); T_bs2 = Tl()
                qbd = sb("qbd", [128, 64], BF16, ps_); T_qbd = Tl()
                knew = sb("knew", [128, 2, 4], BF16, ps_); T_knew = Tl()
                ssm = [pgbuf[0:64, 0, :]] * 2; T_ssm = [T_pg[0]] * 2
                pex = [sb("pex0", [64, 512], BF16, ps_)] * 2; T_pex = [Tl()] * 2
                PTs = [sb("PTs0", [128, 256], BF16, ps_)] * 2; T_PTs = [Tl()] * 2
                PTn = sb("PTn", [4, 64], BF16, ps_); T_PTn = Tl()
                pc = pgbuf[0:64, 1, 0:256]; T_pc = T_pg[1]
                ec = pc; T_ec = T_pc
                pcb = sb("pcb", [64, 256], BF16, ps_); T_pcb = Tl()
                impf = pgbuf[0:64, 2, 0:256]; T_impf = T_pg[2]
                t8s = sb("t8s", [64, 16], F32, ps_); T_t8s = Tl()
                wks = pgbuf[0:64, 2, 256:512]; T_wks = T_pg[2]
                pens = sb("pens", [64, 256], F32, ps_); T_pens = Tl()
                mxs = sb("mxs", [64, 2], F32, ps_); T_mxs = Tl()
                pTs = sb("pTs", [128, 2, 64], BF16, ps_); T_pTs = Tl()
                kwinT = sb("kwinT", [128, 512], BF16, ps_); T_kwT = Tl()
                vwn = sb("vwn", [128, 4, 130], BF16, ps_); T_vwn = Tl()
                swt = [pgbuf[:, 3, 0:256]] * 2; T_swt = [T_pg[3]] * 2
                Gm = sb("Gm", [64, 64], F32, ps_); SelT = sb("SelT", [4, 64], F32, ps_); Hsel = sb("Hsel", [64, 8], F32, ps_)
                CN = sb("CN", [64, 4], F32, ps_); CW = sb("CW", [64, 4], F32, ps_); T_cst = Tl()
                gr3 = sb("gr3", [64, 3, 8], F32, ps_); grow = sb("grow", [64, 3], F32, ps_); T_grow = Tl()
                ob = sb("ob", [64, 64], F32, ps_); T_ob = Tl()
                obb = sb("obb", [64, 64], BF16, ps_); T_obb = Tl()
                rs2 = sb("rs2", [64, 1], F32, ps_); T_rs2 = Tl()
                tmo = pgbuf[0:64, 1, 256:320]; T_tmo = T_pg[1]
                cache2d = cache_d

                for (t_, d_) in ((Gm, Gm_d), (SelT, SelT_d), (Hsel, Hsel_d), (CN, CN_d), (CW, CW_d)):
                    k.ld(t_[:], d_[:, :], [T_cst])
                k.ld(b1s[:], b1r_d[:, :], [T_bs2]); k.ld(b2k[:], b2k_d[:, :], [T_bs2]); k.ld(b2v[:], b2v_d[:, :], [T_bs2])
                k.memset("pool", W2p[:], 0.0, [T_W2p])
                for parity in range(2):
                    for kvh in range(2):
                        k.ldc(W2p[parity * 64:(parity + 1) * 64, parity, kvh, kvh * 64:(kvh + 1) * 64], w2_d[0, :, :], [T_W2p])
                    k.ldc(w2v[parity * 64:(parity + 1) * 64, :], w2_d[1, :, :], [T_w2v])
                k.memset("pool", vsl[:], 1.0, T_vsl)
                k.memset("pool", vwn[:], 1.0, [T_vwn])
                S.op("pool", lambda e: e.iota(iot_i[:], pattern=[[0, 1]], base=0, channel_multiplier=1), (), [T_iot])
                k.cp("dve", iot_f[:], iot_i[:], [T_iot], [T_iot])

                rot = [1]

                def rp():
                    i = rot[0]
                    rot[0] = 1 + (i % 7)
                    return i

                PO, T_PO = PS[0], PT[0]
                cvt = [0]
                for s in range(NS if 'NOS' not in STOP else 0):
                    tg = 16 + s
                    k.ld(ptb[:], pt_d[s, :, :], [T_ptb])
                    k.cp("dve", idxf[:], ptb[:], [T_ptb], [T_idxf])
                    k.ts("dve", idxf[:], idxf[:], 128.0, iot_f[:, 0:1], ALU.mult, ALU.add, [T_idxf, T_iot], [T_idxf])
                    k.cp("dve", idxi[:], idxf[:], [T_idxf], [T_idxi])
                    k.memset("pool", qbd[:], 0.0, [T_qbd])
                    for h in range(8):
                        kvh = h // 4; g = h % 4; sb_ = (h % 2) * 64
                        k.cp("dve", qbd[kvh * 64:(kvh + 1) * 64, kvh * 32 + g * 4:kvh * 32 + g * 4 + 4], qTs[sb_:sb_ + 64, h // 2, s * 4:(s + 1) * 4],
                             [T_qTs], [T_qbd])
                    for br in range(2):
                        for kvh in range(2):
                            k.cp("dve", knew[kvh * 64:(kvh + 1) * 64, br, :], kTs[kvh * 64:(kvh + 1) * 64, br * 2 + kvh, s * 4:(s + 1) * 4], [T_kTs], [T_knew])
                    pi_ = rp()
                    k.mm(PS[pi_][0:64, 0:24], SelT[:, :], gates[0:4, tg, :], True, True, [T_cst, T_gates[tg]], [PT[pi_]])
                    k.tt("dve", gr3[:], PS[pi_][0:64, 0:24].rearrange("p (b h) -> p b h", b=3), Hsel[:, :].unsqueeze(1).to_broadcast([64, 3, 8]), ALU.mult,
                         [PT[pi_], T_cst], [T_grow])
                    S.op("dve", lambda e: e.tensor_reduce(out=grow[:], in_=gr3[:], axis=AX.X, op=ALU.add), [T_grow], [T_grow])

                    for j in range(NPG):
                        p_ = pg[j % NPB]; Tp_ = T_pg[j % NPB]
                        S.dma("pool", (lambda p_, j: lambda e: e.indirect_dma_start(out=p_[:, :], out_offset=None, in_=cache2d[:, :],
                                                                                    in_offset=bass.IndirectOffsetOnAxis(ap=idxi[:, j:j + 1], axis=0)))(p_, j),
                              [T_idxi], [Tp_])
                        jj = j % 8
                        Xcs = XcsB[(j // 8) % 2]; T_Xcs = T_XcsB[(j // 8) % 2]
                        Xcs5 = Xcs.rearrange("p t s (k d) -> p t s k d", k=2)
                        k.tt("dve", Xcs[:, jj, :, :], p_[:, 0:256].rearrange("p (s c) -> p s c", s=2), pe_bc[:].rearrange("p s k d -> p s (k d)"), ALU.add,
                             [Tp_, T_pe], [T_Xcs])
                        k.cp("act", vsl[:, j, :].rearrange("p (k c) -> p k c", k=2)[:, :, 0:64], p_[:, 384:512].rearrange("p (k d) -> p k d", k=2), [Tp_], [T_vsl[j // 4]])
                        if j % 4 == 0:
                            pit = rp()
                        k.tr(PS[pit][:, (j % 4) * 128:(j % 4 + 1) * 128], p_[:, 256:384], ident_f[:], [Tp_, T_identf], [PT[pit]])
                        if j % 4 == 3:
                            k.evac(kselT[:, (j - 3) * 128:(j + 1) * 128], PS[pit][:, :], [PT[pit]], [T_kst[j // 4]])
                        if jj == 7:
                            ch = j // 8
                            for sl in range(2):
                                pic = rp()
                                for d in range(64):
                                    k.mm(PS[pic][:, 0:16].rearrange("p (t k) -> p t k", k=2), Wd[:, sl, d, :], Xcs5[:, :, sl, :, d], d == 0, d == 63,
                                         [T_Wd2, T_Xcs], [PT[pic]])
                                k.act(hidTs[:, sl, ch * 16:(ch + 1) * 16], PS[pic][:, 0:16], AF.Gelu_apprx_tanh, [PT[pic], T_bs2], [T_hid], bias=b1s[:, sl:sl + 1])
                    hv = hidTs[:, :, :].rearrange("p s (g k) -> p s g k", k=2)
                    for parity in range(2):
                        pi_ = rp()
                        for kvh in range(2):
                            k.mm(PS[pi_][:, 0:128], W2p[:, parity, kvh, :], hv[:, 0, :, kvh], kvh == 0, kvh == 1, [T_W2p, T_hid], [PT[pi_]])
                        k.act(kcTs[:, parity * 128:(parity + 1) * 128], PS[pi_][:, 0:128], AF.Identity, [PT[pi_], T_bs2], [T_kcs], bias=b2k[:, 0:1])
                    for parity in range(2):
                        pi_ = rp()
                        for kvh in range(2):
                            k.mm(PS[pi_][:, kvh * 64:(kvh + 1) * 64], hv[parity * 64:(parity + 1) * 64, 1, :, kvh], w2v[parity * 64:(parity + 1) * 64, :], True, True,
                                 [T_hid, T_w2v], [PT[pi_]])
                        k.tt("dve", vcs[:, parity, :], PS[pi_][:, 0:128], b2v[:, :], ALU.add, [PT[pi_], T_bs2], [T_vcs])
                    pi_ = rp()
                    k.mm(PS[pi_][0:64, 0:256], qbd[:, :], kcTs[:, :], True, True, [T_qbd, T_kcs], [PT[pi_]])
                    S.op("dve", (lambda pi_: lambda e: e.tensor_reduce(out=mxs[:, 0:1], in_=PS[pi_][0:64, 0:256], axis=AX.X, op=ALU.max))(pi_), [PT[pi_]], [T_mxs])
                    k.ts("dve", mxs[:, 0:1], mxs[:, 0:1], -1.0, None, ALU.mult, None, [T_mxs], [T_mxs])
                    k.act(ec[:], PS[pi_][0:64, 0:256], AF.Exp, [PT[pi_], T_mxs], [T_ec, T_mxs], bias=mxs[:, 0:1], accum_out=mxs[:, 1:2])
                    S.op("dve", lambda e: e.reciprocal(out=mxs[:, 1:2], in_=mxs[:, 1:2]), [T_mxs], [T_mxs])
                    k.ts("dve", pc[:], ec[:], mxs[:, 1:2], None, ALU.mult, None, [T_mxs, T_pc], [T_pc])
                    k.cp("pool", pcb[:], pc[:], [T_pc], [T_pcb])
                    pi2 = rp()
                    k.mm(PS[pi2][0:64, 0:256], Gm[:, :], pc[:, :], True, True, [T_cst, T_pc], [PT[pi2]])
                    k.cp("act", impf[:], PS[pi2][0:64, 0:256], [PT[pi2]], [T_impf])
                    k.memset("dve", impf[:, 0:1], 1e4, [T_impf])
                    k.memset("dve", impf[:, 255:256], 1e4, [T_impf])
                    S.op("dve", lambda e: e.max(out=t8s[:, 0:8], in_=impf[:]), [T_impf], [T_t8s])
                    S.op("dve", lambda e: e.match_replace(out=wks[:], in_to_replace=t8s[:, 0:8], in_values=impf[:], imm_value=-1e30), [T_impf, T_t8s], [T_wks])
                    S.op("dve", lambda e: e.max(out=t8s[:, 8:16], in_=wks[:]), [T_wks], [T_t8s])
                    k.ts("dve", pens[:], impf[:], t8s[:, 14:15], -NEGB, ALU.is_ge, ALU.mult, [T_impf, T_t8s], [T_pens])
                    k.ts("dve", pens[:], pens[:], NEGB, None, ALU.add, None, [T_pens], [T_pens])
                    pi3 = rp()
                    for parity in range(2):
                        k.tr(PSb[pi3][:, parity * 64:(parity + 1) * 64], pcb[:, parity * 128:(parity + 1) * 128], ident_b[0:64, 0:64], [T_pcb, T_identb], [PT[pi3]])
                    k.cp("act", pTs[:].rearrange("p a r -> p (a r)"), PSb[pi3][:, 0:128], [PT[pi3]], [T_pTs])
                    pi4 = rp()
                    for parity in range(2):
                        k.mm(PS[pi4][0:64, 0:128], pTs[:, parity, :], vcs[:, parity, :], parity == 0, parity == 1, [T_pTs, T_vcs], [PT[pi4]])
                    for half in range(2):
                        rsl = slice(32 * half, 32 * half + 32)
                        k.ts("dve", ob[rsl, :], PS[pi4][rsl, half * 64:(half + 1) * 64], grow[rsl, 0:1], None, ALU.mult, None, [PT[pi4], T_grow], [T_ob])

                    for t in range(4):
                        w_ = swt[t % 2]; Tw_ = T_swt[t % 2]
                        k.ld(w_[:], swin_d[s, t * 128:(t + 1) * 128, :], [Tw_])
                        if t == 0:
                            piw = rp()
                        k.tr(PS[piw][:, t * 128:(t + 1) * 128], w_[:, 0:128], ident_f[:], [Tw_, T_identf], [PT[piw]])
                        k.cp("pool", vwn[:, t, :].rearrange("p (k c) -> p k c", k=2)[:, :, 0:64], w_[:, 128:256].rearrange("p (k d) -> p k d", k=2), [Tw_], [T_vwn])
                    k.evac(kwinT[:], PS[piw][:, :], [PT[piw]], [T_kwT])

                    pen3 = pens[:].rearrange("r (a j) -> r j a", a=2)
                    for br in range(2):
                        ngrp = 32 if br == 0 else 1
                        first = True
                        for gq in range(ngrp):
                            b_ = cvt[0] % 2; cvt[0] += 1
                            pi_ = rp()
                            if br == 0:
                                k.mm(PS[pi_][0:64, :], qbd[:, :], kselT[:, gq * 512:(gq + 1) * 512], True, True, [T_qbd, T_kst[gq]], [PT[pi_]])
                                k.tt("dve", ssm[b_][:].rearrange("r (j a i) -> r j a i", j=4, a=2), PS[pi_][0:64, :].rearrange("r (j a i) -> r j a i", j=4, a=2),
                                     pen3[:, gq * 4:(gq + 1) * 4, :].unsqueeze(3).to_broadcast([64, 4, 2, 64]), ALU.add, [PT[pi_], T_pens], [T_ssm[b_]])
                            else:
                                k.mm(PS[pi_][0:64, :], qbd[:, :], kwinT[:, :], True, True, [T_qbd, T_kwT], [PT[pi_]])
                                k.tt("dve", ssm[b_][:, 0:4], PS[pi_][0:64, 0:4], CW[:, :], ALU.add, [PT[pi_], T_cst], [T_ssm[b_]])
                            if br == 0:
                                k.act(pex[b_][:], ssm[b_][:], AF.Exp, [T_ssm[b_]], [T_pex[b_]])
                            else:
                                k.act(pex[b_][:, 0:4], ssm[b_][:, 0:4], AF.Exp, [T_ssm[b_]], [T_pex[b_]])
                                k.act(pex[b_][:, 4:512], PS[pi_][0:64, 4:512], AF.Exp, [PT[pi_]], [T_pex[b_]])
                            pit2 = rp()
                            for jj in range(4):
                                k.tr(PSb[pit2][:, jj * 64:(jj + 1) * 64], pex[b_][:, jj * 128:(jj + 1) * 128], ident_b[0:64, 0:64], [T_pex[b_], T_identb], [PT[pit2]])
                            k.evac(PTs[b_][:], PSb[pit2][:, 0:256], [PT[pit2]], [T_PTs[b_]])
                            for jj in range(4):
                                if br == 0:
                                    rhs_ = vsl[:, gq * 4 + jj, :]; Tr_ = T_vsl[gq]
                                else:
                                    rhs_ = vwn[:, jj, :]; Tr_ = T_vwn
                                k.mm(PO[0:64, 0:130], PTs[b_][:, jj * 64:(jj + 1) * 64], rhs_, first, False, [T_PTs[b_], Tr_], [T_PO])
                                first = False
                        pi_ = rp()
                        k.mm(PS[pi_][0:64, 0:4], qbd[:, :], knew[:, br, :], True, True, [T_qbd, T_knew], [PT[pi_]])
                        b_ = cvt[0] % 2; cvt[0] += 1
                        k.tt("dve", ssm[b_][:, 0:4], PS[pi_][0:64, 0:4], CN[:, :], ALU.add, [PT[pi_], T_cst], [T_ssm[b_]])
                        k.act(pex[b_][:, 0:4], ssm[b_][:, 0:4], AF.Exp, [T_ssm[b_]], [T_pex[b_]])
                        pit2 = rp()
                        k.tr(PSb[pit2][0:4, 0:64], pex[b_][:, 0:4], ident_b[0:64, 0:64], [T_pex[b_], T_identb], [PT[pit2]])
                        k.cp("dve", PTn[:], PSb[pit2][0:4, 0:64], [PT[pit2]], [T_PTn])
                        k.mm(PO[0:64, 0:130], PTn[:, :], vnew[0:4, s, br, :], False, True, [T_PTn, T_vnew], [T_PO])
                        for half in range(2):
                            rsl = slice(32 * half, 32 * half + 32)
                            c0 = half * 65
                            S.op("dve", (lambda rsl, c0: lambda e: e.reciprocal(out=rs2[rsl, :], in_=PO[rsl, c0 + 64:c0 + 65]))(rsl, c0), [T_PO], [T_rs2])
                            k.tt("dve", rs2[rsl, :], rs2[rsl, :], grow[rsl, 1 + br:2 + br], ALU.mult, [T_rs2, T_grow], [T_rs2])
                            k.ts("dve", tmo[rsl, :], PO[rsl, c0:c0 + 64], rs2[rsl, 0:1], None, ALU.mult, None, [T_PO, T_rs2], [T_tmo])
                            k.tt("dve", ob[rsl, :], ob[rsl, :], tmo[rsl, :], ALU.add, [T_ob, T_tmo], [T_ob])
                    k.cp("act", obb[:], ob[:], [T_ob], [T_obb])
                    pi_ = rp()
                    k.tr(PSb[pi_][0:64, 0:64], obb[:, :], ident_b[0:64, 0:64], [T_obb, T_identb], [PT[pi_]])
                    for h in range(8):
                        kvh = h // 4; g = h % 4; db = (h % 2) * 64
                        k.cp("dve", ybT[db:db + 64, h // 2, TP + s * 4:TP + (s + 1) * 4], PSb[pi_][0:64, kvh * 32 + g * 4:kvh * 32 + g * 4 + 4], [PT[pi_]], [T_yb[4]])
                S.flush()

        with ExitStack() as pe_:
            mixT = sb("mixT", [128, KC, NT], BF16, pe_); T_mix = [[Tl() for _ in TBS] for _ in range(KC)]
            with ExitStack() as pe1:
                wba = sb("wba", [128, 4, D], BF16, pe1); T_wba = Tl()
                wbb = sb("wbb", [128, 4, D], BF16, pe1); T_wbb = Tl()
                wzm = [sb("wzm%d" % i, [128, KC, 2, 128], BF16, pe1) for i in range(2)]; T_wzm = [Tl(), Tl()]
                gsa = [sb("gsa%d" % i, [128, 512], F32, pe1) for i in range(2)]; T_gsa = [Tl(), Tl()]
                gsb = [sb("gsb%d" % i, [128, 512], F32, pe1) for i in range(2)]; T_gsb = [Tl(), Tl()]
                k.ldc(wba[:], wba_d.rearrange("(kc p) n -> p kc n", p=128), [T_wba])
                k.ldc(wbb[:], wbb_d.rearrange("(kc p) n -> p kc n", p=128), [T_wbb])
                wi = win_d.rearrange("(kc p) n -> p kc n", p=128)
                it = 0
                for c in range(KC):
                    wz = wzm[c % 2]; Twz = T_wzm[c % 2]
                    k.ldc(wz[:, :, 0, :], wi[:, :, 2328 + c * 128:2328 + (c + 1) * 128], [Twz])
                    k.ldc(wz[:, :, 1, :], wi[:, :, 3352 + c * 128:3352 + (c + 1) * 128], [Twz])
                    for bi, (t0, n) in enumerate(TBS):
                        ga = gsa[it % 2]; Tga = T_gsa[it % 2]; gb = gsb[it % 2]; Tgb = T_gsb[it % 2]; it += 1
                        psA, TA = nextps()
                        for k4 in range(4):
                            k.mm(psA[:, 0:n], wba[:, k4, c * 128:(c + 1) * 128], uT[:, k4, t0:t0 + n], k4 == 0, k4 == 3, [T_wba, T_u[k4][bi]], [TA])
                        psG, TG = nextps()
                        for kc in range(KC):
                            k.mm(psG[:, 0:n], wz[:, kc, 0, :], hT[:, kc, t0:t0 + n], kc == 0, kc == KC - 1, [Twz, T_h[kc][bi]], [TG])
                        k.act(ga[:, 0:n], psG[:, 0:n], AF.Sigmoid, [TG], [Tga])
                        k.tt("dve", ga[:, 0:n], ga[:, 0:n], psA[:, 0:n], ALU.mult, [Tga, TA], [Tga])
                        psB, TB = nextps()
                        for k4 in range(4):
                            k.mm(psB[:, 0:n], wbb[:, k4, c * 128:(c + 1) * 128], ybT[:, k4, t0:t0 + n], k4 == 0, k4 == 3, [T_wbb, T_yb[bi]], [TB])
                        psH, TH = nextps()
                        for kc in range(KC):
                            k.mm(psH[:, 0:n], wz[:, kc, 1, :], hT[:, kc, t0:t0 + n], kc == 0, kc == KC - 1, [Twz, T_h[kc][bi]], [TH])
                        k.act(gb[:, 0:n], psH[:, 0:n], AF.Sigmoid, [TH], [Tgb])
                        k.tt("dve", gb[:, 0:n], gb[:, 0:n], psB[:, 0:n], ALU.mult, [Tgb, TB], [Tgb])
                        k.tt("pool", mixT[:, c, t0:t0 + n], ga[:, 0:n], gb[:, 0:n], ALU.add, [Tga, Tgb], [T_mix[c][bi]])
                S.flush()
                if STOP.startswith('E1'):
                    S.wait_all_outputs("sp"); S.flush(); S.close(); return nc
            with ExitStack() as pe2:
                wout = sb("wout", [128, KC, D], BF16, pe2); T_wout = Tl()
                xr = [sb("xr%d" % i, [128, 512], F32, pe2) for i in range(2)]; T_xr = [Tl(), Tl()]
                k.ldc(wout[:], wout_d.rearrange("(kc p) n -> p kc n", p=128), [T_wout])
                it = 0
                for c in range(KC):
                    for bi, (t0, n) in enumerate(TBS):
                        x_ = xr[it % 2]; Tx = T_xr[it % 2]; it += 1
                        k.ld(x_[:, 0:n], xT_d[:, c, t0:t0 + n], [Tx])
                        ps, Tp = nextps()
                        for kc in range(KC):
                            k.mm(ps[:, 0:n], wout[:, kc, c * 128:(c + 1) * 128], mixT[:, kc, t0:t0 + n], kc == 0, kc == KC - 1, [T_wout, T_mix[kc][bi]], [Tp])
                        if bi < 4:
                            k.stt(x1T[:, c, t0:t0 + n], ps[:, 0:n], modT[:, 16 + c, 0:1], x_[:, 0:n], ALU.mult, ALU.add, [Tp, Tx, T_mod], [T_x1[c][bi]])
                        else:
                            for s in range(NS):
                                k.stt(x1T[:, c, t0 + s * TS:t0 + (s + 1) * TS], ps[:, s * TS:(s + 1) * TS], modT[:, 16 + c, 1 + s:2 + s],
                                      x_[:, s * TS:(s + 1) * TS], ALU.mult, ALU.add, [Tp, Tx, T_mod], [T_x1[c][bi]])
                S.flush()

        mid_scope.close()
        with ExitStack() as pf:
            sq = [sb("sqF%d" % i, [128, 1040], BF16, pf) for i in range(2)]; T_sq = [Tl(), Tl()]
            rstd = sb("rstdF", [128, NT], F32, pf); T_rstd = [Tl() for _ in TBS]
            tmp = [sb("tmpF0", [128, 1040], F32, pf)] * 2; T_tmp = [Tl()] * 2
            actT = sb("actT", [128, 22, 1040], BF16, pf); T_act = [[Tl() for _ in range(3)] for _ in range(22)]
            wup = [sb("wup%d" % i, [128, KC, 2, 128], BF16, pf) for i in range(2)]; T_wup = [Tl(), Tl()]
            wdn = [sb("wdn%d" % i, [128, 22, 128], BF16, pf) for i in range(2)]; T_wdn = [Tl(), Tl()]
            U = [sb("U%d" % i, [128, 514], F32, pf) for i in range(4)]; T_U = [Tl() for _ in range(4)]
            cv = [sb("cv%d" % i, [128, 512], F32, pf) for i in range(4)]; T_cv = [Tl() for _ in range(4)]
            gl = [sb("gl%d" % i, [128, 512], F32, pf) for i in range(2)]; T_gl = [Tl(), Tl()]
            halo = sb("halo", [128, 44, 2], F32, pf); T_halo = [Tl() for _ in range(44)]
            convo = sb("convo", [128, 44, 10], F32, pf); T_convo = Tl()
            wcv = sb("wcv", [128, 44, 3], F32, pf); bcv = sb("bcv", [128, 44], F32, pf); T_wcv = Tl()
            sprev = sb("sprev", [128, 44, 4, 2], F32, pf); T_sprev = Tl()
            ups = [sb("ups%d" % i, [128, 4, 6], F32, pf) for i in range(2)]; T_ups = [Tl(), Tl()]
            cvs = [sb("cvs%d" % i, [128, 4, 4], F32, pf) for i in range(2)]; T_cvs = [Tl(), Tl()]
            gf = sb("gf", [128, KC], F32, pf); T_gf = Tl()
            yo = [sb("yo0", [128, 1040], F32, pf)] * 2; T_yo = [Tl()] * 2
            k.ld(wcv[:], wcv_d[:, :, :], [T_wcv]); k.ld(bcv[:], bcv_d[:, :], [T_wcv])
            k.ld(sprev[:], sprev_d[:, :, :, :], [T_sprev])
            k.ld(gf[:], gf_d[:, :], [T_gf])
            wupv = wup_d.rearrange("(kc p) n -> p kc n", p=128)
            wdnv = wdn_d.rearrange("(c p) n -> p c n", p=128)
            HALVES = [(0, [(0, 512), (512, 512)]), (1024, [(1024, 512), (1536, 512), (2048, 16)])]

            def rms_stats(src, Tsrc, hs, blocks, nb0):
                ntk = sum(n for _, n in blocks)
                for kc in range(KC):
                    s_ = sq[kc % 2]; Ts = T_sq[kc % 2]
                    k.act(s_[:, 0:ntk], src[:, kc, hs:hs + ntk], AF.Square, Tsrc(kc), [Ts])
                    for bi, (t0, n) in enumerate(blocks):
                        k.mm(PS[bi][:, 0:n], ones_b[:], s_[:, t0 - hs:t0 - hs + n], kc == 0, kc == KC - 1, [T_ones, Ts], [PT[bi]])
                for bi, (t0, n) in enumerate(blocks):
                    k.act(rstd[:, t0:t0 + n], PS[bi][:, 0:n], AF.Sqrt, [PT[bi]], [T_rstd[nb0 + bi]], bias=EPS, scale=1.0 / D)
                    S.op("dve", (lambda o: (lambda e: e.reciprocal(out=o, in_=o)))(rstd[:, t0:t0 + n]), [T_rstd[nb0 + bi]], [T_rstd[nb0 + bi]])

            uidx = [0]
            for hi, (hs, blocks) in enumerate(HALVES):
                nb0 = 0 if hi == 0 else 2
                ntk = sum(n for _, n in blocks)
                rms_stats(x1T, lambda kc: [T_x1[kc][nb0 + b] for b in range(len(blocks))], hs, blocks, nb0)
                for kc in range(KC):
                    t_ = tmp[kc % 2]; Tt = T_tmp[kc % 2]
                    Tx = [T_x1[kc][nb0 + b] for b in range(len(blocks))]
                    Th = [T_h[kc][nb0 + b] for b in range(len(blocks))]
                    k.tt("dve", t_[:, 0:ntk], x1T[:, kc, hs:hs + ntk], rstd[:, hs:hs + ntk], ALU.mult, Tx + T_rstd[nb0:nb0 + len(blocks)], [Tt])
                    npr = ntk if hi == 0 else 1024
                    k.act(hT[:, kc, hs:hs + npr], t_[:, 0:npr], AF.Identity, [Tt, T_A2, T_mod], Th, bias=modT[:, 24 + kc, 0:1], scale=A2[:, kc, 0:1])
                    if hi == 1:
                        for s in range(NS):
                            c0 = 1024 + s * TS
                            k.act(hT[:, kc, TP + s * TS:TP + (s + 1) * TS], t_[:, c0:c0 + TS], AF.Identity, [Tt, T_A2, T_mod], Th,
                                  bias=modT[:, 24 + kc, 1 + s:2 + s], scale=A2[:, kc, 1 + s:2 + s])
                for cp_ in range(22):
                    wu_ = wup[cp_ % 2]; Twu = T_wup[cp_ % 2]
                    k.ldc(wu_[:, :, 0, :], wupv[:, :, cp_ * 128:(cp_ + 1) * 128], [Twu])
                    k.ldc(wu_[:, :, 1, :], wupv[:, :, DFF + cp_ * 128:DFF + (cp_ + 1) * 128], [Twu])
                    for cc in range(1):
                        c = cp_
                        for bi, (t0, n) in enumerate(blocks):
                            gbi = nb0 + bi
                            res = []
                            for ag in range(2):
                                idx = c + 22 * ag
                                ps, Tp = nextps()
                                for kc in range(KC):
                                    k.mm(ps[:, 0:n], wu_[:, kc, ag, cc * 128:(cc + 1) * 128], hT[:, kc, t0:t0 + n], kc == 0, kc == KC - 1,
                                         [Twu, T_h[kc][gbi]], [Tp])
                                w0 = wcv[:, idx, 0:1]; w1 = wcv[:, idx, 1:2]; w2 = wcv[:, idx, 2:3]; bb = bcv[:, idx:idx + 1]
                                if n == 512:
                                    ui = uidx[0] % 4; uidx[0] += 1
                                    U_ = U[ui]; TU = T_U[ui]; cv_ = cv[ui]; Tcv = T_cv[ui]
                                    if t0 == 0:
                                        k.memset("pool", U_[:, 0:2], 0.0, [TU])
                                    else:
                                        k.cp("pool", U_[:, 0:2], halo[:, idx, :], [T_halo[idx]], [TU])
                                    k.cp("act", U_[:, 2:514], ps[:, 0:512], [Tp], [TU])
                                    k.cp("pool", halo[:, idx, :], U_[:, 512:514], [TU], [T_halo[idx]])
                                    if t0 == 1536:
                                        k.cp("pool", convo[:, idx, 0:2], U_[:, 512:514], [TU], [T_convo])
                                    k.ts("dve", cv_[:], U_[:, 2:514], w2, bb, ALU.mult, ALU.add, [TU, T_wcv], [Tcv])
                                    k.stt(cv_[:], U_[:, 1:513], w1, cv_[:], ALU.mult, ALU.add, [TU, T_wcv, Tcv], [Tcv])
                                    k.stt(cv_[:], U_[:, 0:512], w0, cv_[:], ALU.mult, ALU.add, [TU, T_wcv, Tcv], [Tcv])
                                    res.append((cv_[:], Tcv))
                                else:
                                    u_ = ups[ag]; Tu_ = T_ups[ag]; c_ = cvs[ag]; Tc_ = T_cvs[ag]
                                    k.cp("pool", u_[:, :, 0:2], sprev[:, idx, :, :], [T_sprev], [Tu_])
                                    k.cp("act", u_[:, :, 2:6], ps[:, 0:16].rearrange("p (s t) -> p s t", s=4), [Tp], [Tu_])
                                    k.cp("pool", convo[:, idx, 2:10].rearrange("p (s r) -> p s r", s=4), u_[:, :, 4:6], [Tu_], [T_convo])
                                    k.ts("dve", c_[:], u_[:, :, 2:6], w2, bb, ALU.mult, ALU.add, [Tu_, T_wcv], [Tc_])
                                    k.stt(c_[:], u_[:, :, 1:5], w1, c_[:], ALU.mult, ALU.add, [Tu_, T_wcv, Tc_], [Tc_])
                                    k.stt(c_[:], u_[:, :, 0:4], w0, c_[:], ALU.mult, ALU.add, [Tu_, T_wcv, Tc_], [Tc_])
                                    res.append((c_[:].rearrange("p s t -> p (s t)"), Tc_))
                            (ca, Tca), (cg, Tcg) = res
                            g_ = gl[(c + bi) % 2]; Tg_ = T_gl[(c + bi) % 2]
                            k.act(g_[:, 0:n], ca, AF.Gelu_apprx_tanh, [Tca], [Tg_])
                            k.tt("dve", actT[:, c, t0 - hs:t0 - hs + n], g_[:, 0:n], cg, ALU.mult, [Tg_, Tcg], [T_act[c][bi]])
                for m in range(KC):
                    wd_ = wdn[m % 2]; Twd = T_wdn[m % 2]
                    k.ldc(wd_[:], wdnv[:, :, m * 128:(m + 1) * 128], [Twd])
                    for bi, (t0, n) in enumerate(blocks):
                        gbi = nb0 + bi
                        ps, Tp = nextps()
                        for c in range(22):
                            k.mm(ps[:, 0:n], wd_[:, c, :], actT[:, c, t0 - hs:t0 - hs + n], c == 0, c == 21, [Twd, T_act[c][bi]], [Tp])
                        if n == 512:
                            k.stt(x1T[:, m, t0:t0 + n], ps[:, 0:n], modT[:, 40 + m, 0:1], x1T[:, m, t0:t0 + n], ALU.mult, ALU.add,
                                  [Tp, T_mod, T_x1[m][gbi]], [T_x1[m][gbi]])
                        else:
                            for s in range(NS):
                                xs_ = x1T[:, m, t0 + s * TS:t0 + (s + 1) * TS]
                                k.stt(xs_, ps[:, s * TS:(s + 1) * TS], modT[:, 40 + m, 1 + s:2 + s], xs_, ALU.mult, ALU.add,
                                      [Tp, T_mod, T_x1[m][gbi]], [T_x1[m][gbi]])
                rms_stats(x1T, lambda kc: [T_x1[kc][nb0 + b] for b in range(len(blocks))], hs, blocks, nb0)
                for m in range(KC):
                    t_ = tmp[m % 2]; Tt = T_tmp[m % 2]
                    y_ = yo[m % 2]; Ty = T_yo[m % 2]
                    Tx = [T_x1[m][nb0 + b] for b in range(len(blocks))]
                    k.tt("dve", t_[:, 0:ntk], x1T[:, m, hs:hs + ntk], rstd[:, hs:hs + ntk], ALU.mult, Tx + T_rstd[nb0:nb0 + len(blocks)], [Tt])
                    k.act(y_[:, 0:ntk], t_[:, 0:ntk], AF.Copy, [Tt, T_gf], [Ty], scale=gf[:, m:m + 1])
                    k.st(yT_o[:, m, hs:hs + ntk], y_[:, 0:ntk], [Ty])
            k.st(conv_o[:, :], convo[:].rearrange("p i r -> p (i r)"), [T_convo])
            S.wait_all_outputs("sp")
            S.flush()
    S.close()
    return nc


_NC_CACHE = {}


def _prep_inputs(inp):
    f = lambda a: np.ascontiguousarray(a, dtype=np.float32)
    xp = np.asarray(inp["x_prompt"]); xs = np.asarray(inp["x_sample"])
    cp_ = np.asarray(inp["c_prompt"]); cs_ = np.asarray(inp["c_sample"])

    def fm(vec):
        return f(np.asarray(vec).reshape(KC, 128).T)

    shared = {
        "w_ada": f(np.asarray(inp["w_ada"])[0]),
        "b_adaT": f(np.asarray(inp["b_ada"])[0].reshape(48, 128).T),
        "g1T": fm(inp["g_norm1"][0]), "g2T": fm(inp["g_norm2"][0]), "gfT": fm(inp["g_final"]),
        "w_in": f(np.asarray(inp["w_in"])[0]),
        "ln_g_bc": f(np.broadcast_to(np.asarray(inp["ln_v_g"])[0][None, :], (128, 512))),
        "ln_b_bc": f(np.broadcast_to(np.asarray(inp["ln_v_b"])[0][None, :], (128, 512))),
        "ident": np.eye(128, dtype=np.float32),
    }
    ws = np.asarray(inp["w_spatial"])[0]
    shared["wsT"] = f(ws.transpose(2, 0, 1))
    shared["wssT"] = f(ws[:, :4, :4].transpose(2, 0, 1))
    shared["bs"] = f(np.asarray(inp["b_spatial"])[0][None])
    b1 = np.asarray(inp["cmp_b1"])[0]; b2 = np.asarray(inp["cmp_b2"])[0]
    shared["b1r"] = f(np.concatenate([b1.T, b1.T], axis=0))
    shared["b2r"] = f(np.concatenate([b2.T, b2.T], axis=0))
    shared["cmp_w1"] = f(np.asarray(inp["cmp_w1"])[0])
    shared["cmp_w2"] = f(np.asarray(inp["cmp_w2"])[0])
    shared["w_branch_a"] = f(np.asarray(inp["w_branch_a"])[0])
    shared["w_branch_b"] = f(np.asarray(inp["w_branch_b"])[0])
    shared["w_out"] = f(np.asarray(inp["w_out"])[0])
    shared["w_up"] = f(np.asarray(inp["w_up"])[0])
    shared["w_down"] = f(np.asarray(inp["w_down"])[0])
    shared["w_convT"] = f(np.asarray(inp["w_conv"])[0].reshape(3, 44, 128).transpose(2, 1, 0))
    shared["b_convT"] = f(np.asarray(inp["b_conv"])[0].reshape(44, 128).T)
    t = np.arange(2048)[:, None]; n = np.arange(32)[None, :]
    avail = (n + 1) * 64 <= t + 1
    cur = t // 64
    forced = (n == 0) | (n == cur) | (n == cur - 1)
    future = n > cur
    mk = np.stack([np.where(avail, 0.0, -1e30), avail.astype(np.float32), (~(forced | future)).astype(np.float32),
                   np.where(forced, 1e4, np.where(future, -1.0, 0.0))], axis=0)
    shared["msk"] = f(mk.reshape(4, 16, 128, 32).transpose(2, 0, 1, 3))
    key = np.arange(2048)[None, :]; r = np.arange(64)[:, None]
    shared["Emat"] = f((key // 64 == (r % 32)).astype(np.float32))
    b_ = np.arange(128)[:, None]; a_ = np.arange(128)[None, :]
    shared["Cm"] = f(np.stack([np.where(a_ >= b_, 0.0, NEGB), np.where(a_ < b_, 0.0, NEGB)], axis=1))
    sconv = np.asarray(inp["state_ffn_conv"])[0]
    if not STOP:
        shared["cache2d"] = np.asarray(inp["cache_kv"], dtype=np.float32).reshape(5120 * 128, 512)
    shared["b2k"] = f(np.concatenate([b2[0], b2[0]])[:, None])
    shared["b2v"] = f(np.broadcast_to(np.concatenate([b2[1], b2[1]])[None, :], (128, 128)))
    rr = np.arange(64); kvh_r = rr // 32; sl_r = rr % 32; g_r = sl_r // 4; tok_r = sl_r % 4; used = sl_r < 16
    shared["Gm"] = f(((kvh_r[:, None] == kvh_r[None, :]) & (tok_r[:, None] == tok_r[None, :]) & used[:, None]).astype(np.float32))
    shared["SelT"] = f((np.arange(4)[:, None] == tok_r[None, :]).astype(np.float32))
    shared["Hsel"] = f((np.arange(8)[None, :] == (4 * kvh_r + np.minimum(g_r, 3))[:, None]).astype(np.float32))
    shared["CN"] = f(np.where(np.arange(4)[None, :] <= tok_r[:, None], 0.0, NEGB))
    shared["CW"] = f(np.where(np.arange(4)[None, :] > tok_r[:, None], 0.0, NEGB))
    ptab = np.asarray(inp["page_table"]).astype(np.int32)
    pe = np.asarray(inp["cmp_pe"])[0]
    pe_bc = np.broadcast_to(pe.transpose(1, 0, 2)[None, :, :, None, :], (2, 64, 2, 2, 64)).reshape(128, 2, 2, 64)
    shared["pe_bc"] = f(pe_bc)
    maps = []
    swin = np.asarray(inp["state_kv_win"])[0].reshape(32, 512, 256)
    for c in range(NCORES):
        xall = np.concatenate([xp[c], xs[4 * c:4 * c + 4].reshape(16, D)], axis=0)
        xT = f(xall.T.reshape(KC, 128, NT).transpose(1, 0, 2))
        call = np.concatenate([cp_[c:c + 1], cs_[4 * c:4 * c + 4]], axis=0)
        cT = f(call.T.reshape(KC, 128, 5).transpose(1, 0, 2))
        m = dict(shared)
        m["xT"] = xT
        m["cT"] = cT
        m["state_win"] = f(swin[4 * c:4 * c + 4])
        m["pt_bc"] = np.ascontiguousarray(np.broadcast_to(ptab[4 * c:4 * c + 4][:, None, :], (4, 128, 128)), dtype=np.int32)
        m["sprevT"] = f(sconv[4 * c:4 * c + 4].reshape(4, 2, 44, 128).transpose(3, 2, 0, 1))
        maps.append(m)
    return maps


def kernel(**inp):
    if "nc" not in _NC_CACHE:
        _NC_CACHE["nc"] = build_program()
    nc = _NC_CACHE["nc"]
    maps = _prep_inputs(inp)
    res = run_bass_kernel_spmd(nc, maps, core_ids=list(range(NCORES)))
    R = res.results
    y_prompt = np.zeros((8, 2048, 1024), np.float32)
    y_sample = np.zeros((32, 4, 1024), np.float32)
    kv_prompt = np.zeros((1, 8, 2048, 4, 2, 64), np.float32)
    kv_sample = np.zeros((1, 32, 4, 4, 2, 64), np.float32)
    win_prompt = np.zeros((1, 8, 512, 2, 2, 64), np.float32)
    win_sample = np.zeros((1, 32, 512, 2, 2, 64), np.float32)
    v_chunk = np.zeros((1, 32, 4, 512), np.float32)
    conv_prompt = np.zeros((1, 8, 2, 5632), np.float32)
    conv_sample = np.zeros((1, 32, 2, 5632), np.float32)
    for c in range(NCORES):
        r = R[c]
        kv = r["kv_tok"]
        kv_prompt[0, c] = kv[:TP].reshape(2048, 4, 2, 64)
        kv_sample[0, 4 * c:4 * c + 4] = kv[TP:].reshape(4, 4, 4, 2, 64)
        win_prompt[0, c] = r["win_p"].reshape(512, 2, 2, 64)
        win_sample[0, 4 * c:4 * c + 4] = r["win_s"].reshape(4, 512, 2, 2, 64)
        v_chunk[0, 4 * c:4 * c + 4] = r["vchunk"].reshape(4, 4, 512)
        yT = r["yT"].transpose(2, 1, 0).reshape(NT, D)
        y_prompt[c] = yT[:TP]
        y_sample[4 * c:4 * c + 4] = yT[TP:].reshape(4, 4, D)
        cv = r["convT"].reshape(128, 44, 10)
        conv_prompt[0, c] = cv[:, :, 0:2].transpose(2, 1, 0).reshape(2, 5632)
        conv_sample[0, 4 * c:4 * c + 4] = cv[:, :, 2:10].reshape(128, 44, 4, 2).transpose(2, 3, 1, 0).reshape(4, 2, 5632)
    return (y_prompt, y_sample, kv_prompt, kv_sample, win_prompt, win_sample, v_chunk, conv_prompt, conv_sample)
```

```python
import numpy as np
from contextlib import ExitStack
import concourse.bass as bass
import concourse.mybir as mybir
from concourse.bass_utils import run_bass_kernel_spmd

F32 = mybir.dt.float32
BF16 = mybir.dt.bfloat16
I32 = mybir.dt.int32
AF = mybir.ActivationFunctionType
ALU = mybir.AluOpType
AX = mybir.AxisListType

NCORES = 8
D = 1024
KC = 8
TP = 2048
NS = 4
TS = 4
NT = TP + NS * TS
IN_COLS = 4376
DFF = 2816
NPG = 128
EPS = 1e-6
NEGB = -30000.0
TBS = [(0, 512), (512, 512), (1024, 512), (1536, 512), (2048, 16)]
TTS = [(i * 128, 128) for i in range(16)] + [(TP + s * TS, TS) for s in range(NS)]


class Tl:
    __slots__ = ("name", "w", "r", "ps")

    def __init__(self, name="", ps=False):
        self.name = name
        self.w = None
        self.r = []
        self.ps = ps


class Sched:
    ENG = ("pe", "act", "dve", "pool", "sp")

    def __init__(self, nc, n_dma_sems=(32, 4, 24)):
        self.nc = nc
        self.sems = {}
        self._stack = []
        for e in self.ENG:
            self.sems[e] = self._sem("s_" + e)
        self.seq = {e: 0 for e in self.ENG}
        self.epoch = {e: 0 for e in self.ENG}
        self.cur = {e: e for e in self.ENG}
        self.known = {e: {} for e in self.ENG}
        self.lists = {e: [] for e in self.ENG}
        self.dpool = {}
        for q, n in zip(("sp", "act", "pool"), n_dma_sems):
            self.dpool[q] = dict(keys=[], cnt=[], nxt=0)
            for i in range(n):
                k = "d_%s_%d" % (q, i)
                self.sems[k] = self._sem(k)
                self.dpool[q]["keys"].append(k)
                self.dpool[q]["cnt"].append(0)
        self.out_events = []

    def _sem(self, name):
        cm = self.nc.semaphore(name)
        s = cm.__enter__()
        self._stack.append(cm)
        return s

    def close(self):
        for cm in reversed(self._stack):
            cm.__exit__(None, None, None)

    def _deps(self, e, reads, writes):
        need = {}
        for t in reads:
            if t.w is not None:
                k, v = t.w
                if need.get(k, 0) < v:
                    need[k] = v
            if t.ps:
                for (k, v) in t.r:
                    if k.split("#")[0] != e and need.get(k, 0) < v:
                        need[k] = v
        for t in writes:
            if t.w is not None:
                k, v = t.w
                if need.get(k, 0) < v:
                    need[k] = v
            for (k, v) in t.r:
                if need.get(k, 0) < v:
                    need[k] = v
        waits = []
        kn = self.known[e]
        for k, v in need.items():
            if e == "pe" and k.split("#")[0] == "pe":
                continue
            if kn.get(k, 0) >= v:
                continue
            kn[k] = v
            waits.append((k, v))
        return waits

    def _mark(self, ev, reads, writes):
        for t in reads:
            t.r.append(ev)
            if len(t.r) > 64:
                mx = {}
                for k, v in t.r:
                    if mx.get(k, 0) < v:
                        mx[k] = v
                t.r = list(mx.items())
        for t in writes:
            t.w = ev
            t.r = []

    def op(self, e, fn, reads=(), writes=()):
        waits = self._deps(e, reads, writes)
        if self.seq[e] >= 6000:
            self.epoch[e] += 1
            self.cur[e] = "%s#%d" % (e, self.epoch[e])
            self.sems[self.cur[e]] = self._sem("s_%s_%d" % (e, self.epoch[e]))
            self.seq[e] = 0
        self.seq[e] += 1
        ev = (self.cur[e], self.seq[e])
        self.lists[e].append(("op", waits, fn, self.cur[e]))
        self._mark(ev, reads, writes)
        return ev

    def dma(self, q, fn, reads=(), writes=(), is_output=False):
        waits = self._deps(q, reads, writes)
        p = self.dpool[q]
        i = p["nxt"]
        p["nxt"] = (i + 1) % len(p["keys"])
        k = p["keys"][i]
        prev = p["cnt"][i]
        if prev > 0 and self.known[q].get(k, 0) < prev:
            waits.append((k, prev))
            self.known[q][k] = prev
        p["cnt"][i] = prev + 16
        ev = (k, prev + 16)
        self.lists[q].append(("dma", waits, fn, k))
        self._mark(ev, reads, writes)
        if is_output:
            self.out_events.append(ev)
        return ev

    def wait_all_outputs(self, e="sp"):
        need = {}
        for k, v in self.out_events:
            if need.get(k, 0) < v:
                need[k] = v
        waits = [(k, v) for k, v in need.items() if self.known[e].get(k, 0) < v]
        for k, v in waits:
            self.known[e][k] = v
        self.lists[e].append(("wait", waits))
        self.out_events = []

    def flush(self):
        dw = []
        for q, p in self.dpool.items():
            for kk, cnt in zip(p["keys"], p["cnt"]):
                if cnt > 0 and self.known["sp"].get(kk, 0) < cnt:
                    dw.append((kk, cnt))
                    self.known["sp"][kk] = cnt
        self.lists["sp"].append(("wait", dw))
        lists = self.lists
        self.lists = {e: [] for e in self.ENG}
        sems = self.sems
        with self.nc.Block() as block:
            def mk(e):
                def body(eng):
                    for item in lists[e]:
                        for (k, v) in item[1]:
                            eng.wait_ge(sems[k], v)
                        if item[0] == "op":
                            item[2](eng).then_inc(sems[item[3]], 1)
                        elif item[0] == "dma":
                            item[2](eng).then_inc(sems[item[3]], 16)
                return body
            block.tensor(mk("pe"))
            block.scalar(mk("act"))
            block.vector(mk("dve"))
            block.gpsimd(mk("pool"))
            block.sync(mk("sp"))


class K:
    def __init__(self, nc):
        self.nc = nc
        self.S = Sched(nc)
        self.dram = {}
        self._evac = 0

    def din(self, name, shape, dt=F32):
        t = self.nc.dram_tensor(name, list(shape), dt, kind="ExternalInput")
        self.dram[name] = t
        return t.ap()

    def dout(self, name, shape, dt=F32):
        t = self.nc.dram_tensor(name, list(shape), dt, kind="ExternalOutput")
        self.dram[name] = t
        return t.ap()

    def mm(self, out, lhsT, rhs, start, stop, R, W, sgc=False):
        self.S.op("pe", lambda e: e.matmul(out, lhsT=lhsT, rhs=rhs, start=start, stop=stop, skip_group_check=sgc), R, W)

    def tr(self, out, in_, ident, R, W):
        self.S.op("pe", lambda e: e.transpose(out=out, in_=in_, identity=ident), R, W)

    def act(self, out, in_, func, R, W, bias=None, scale=None, accum_out=None):
        kw = {}
        if bias is not None:
            kw["bias"] = bias
        if scale is not None:
            kw["scale"] = scale
        if accum_out is not None:
            kw["accum_out"] = accum_out
        self.S.op("act", lambda e: e.activation(out=out, in_=in_, func=func, **kw), R, W)

    def tt(self, eng, out, in0, in1, op, R, W):
        self.S.op(eng, lambda e: e.tensor_tensor(out=out, in0=in0, in1=in1, op=op), R, W)

    def ts(self, eng, out, in0, s1, s2, op0, op1, R, W):
        if op1 is None:
            self.S.op(eng, lambda e: e.tensor_scalar(out=out, in0=in0, scalar1=s1, scalar2=None, op0=op0), R, W)
        else:
            self.S.op(eng, lambda e: e.tensor_scalar(out=out, in0=in0, scalar1=s1, scalar2=s2, op0=op0, op1=op1), R, W)

    def stt(self, out, in0, scalar, in1, op0, op1, R, W):
        self.S.op("dve", lambda e: e.scalar_tensor_tensor(out=out, in0=in0, scalar=scalar, in1=in1, op0=op0, op1=op1), R, W)

    def cp(self, eng, out, in_, R, W):
        if eng == "act":
            self.S.op("act", lambda e: e.copy(out=out, in_=in_), R, W)
        else:
            self.S.op(eng, lambda e: e.tensor_copy(out=out, in_=in_), R, W)

    def evac(self, out, in_, R, W):
        self._evac ^= 1
        self.cp("act" if self._evac else "dve", out, in_, R, W)

    def memset(self, eng, ap, val, W):
        self.S.op(eng, lambda e: e.memset(ap, val), (), W)

    def ld(self, out, in_, W, R=(), q="sp"):
        self.S.dma(q, lambda e: e.dma_start(out=out, in_=in_), R, W)

    def ldc(self, out, in_, W, R=()):
        self.S.dma("pool", lambda e: e.dma_start(out=out, in_=in_), R, W)

    def st(self, out, in_, R, q="sp"):
        self.S.dma(q, lambda e: e.dma_start(out=out, in_=in_), R, (), is_output=True)


import os
STOP = os.environ.get('KSTOP', '')
DCUT = int(os.environ.get('DCUT', '9'))


def build_program():
    nc = bass.Bass("TRN2", target_bir_lowering=False)
    k = K(nc)
    S = k.S
    xT_d = k.din("xT", [128, KC, NT])
    cT_d = k.din("cT", [128, KC, 5])
    wada_d = k.din("w_ada", [D, 6 * D])
    bada_d = k.din("b_adaT", [128, 48])
    g1_d = k.din("g1T", [128, KC])
    g2_d = k.din("g2T", [128, KC])
    gf_d = k.din("gfT", [128, KC])
    win_d = k.din("w_in", [D, IN_COLS])
    lng_d = k.din("ln_g_bc", [128, 512])
    lnb_d = k.din("ln_b_bc", [128, 512])
    ident_d = k.din("ident", [128, 128])
    pe_d = k.din("pe_bc", [128, 2, 2, 64])
    swin_d = k.din("state_win", [NS, 512, 256])

    wsT_d = k.din("wsT", [128, 4, 128])
    wss_d = k.din("wssT", [4, 4, 4])
    bs_d = k.din("bs", [1, 4, 128])
    b1r_d = k.din("b1r", [128, 2])
    b2r_d = k.din("b2r", [128, 2])
    w1_d = k.din("cmp_w1", [2, 64, 64, 64])
    w2_d = k.din("cmp_w2", [2, 64, 64])
    msk_d = k.din("msk", [128, 4, 16, 32])
    E_d = k.din("Emat", [64, 2048])
    Cm_d = k.din("Cm", [128, 2, 128])
    wba_d = k.din("w_branch_a", [512, D])
    wbb_d = k.din("w_branch_b", [512, D])
    wout_d = k.din("w_out", [D, D])
    wup_d = k.din("w_up", [D, 2 * DFF])
    wdn_d = k.din("w_down", [DFF, D])
    wcv_d = k.din("w_convT", [128, 44, 3])
    bcv_d = k.din("b_convT", [128, 44])
    sprev_d = k.din("sprevT", [128, 44, 4, 2])
    cache_d = k.din("cache2d", [5120 * 128, 512]) if not STOP else None
    pt_d = k.din("pt_bc", [NS, 128, 128], I32)
    Gm_d = k.din("Gm", [64, 64]); SelT_d = k.din("SelT", [4, 64]); Hsel_d = k.din("Hsel", [64, 8])
    CN_d = k.din("CN", [64, 4]); CW_d = k.din("CW", [64, 4])
    b2k_d = k.din("b2k", [128, 1]); b2v_d = k.din("b2v", [128, 128])
    yT_o = k.dout("yT", [128, KC, NT])
    conv_o = k.dout("convT", [128, 440])
    kv_o = k.dout("kv_tok", [NT, 512])
    winp_o = k.dout("win_p", [512, 256])
    wins_o = k.dout("win_s", [NS, 512, 256])
    vch_o = k.dout("vchunk", [NS * TS, 512])

    with ExitStack() as top:
        def sb(name, shape, dt, es=top):
            return es.enter_context(nc.sbuf_tensor("sb_" + name, list(shape), dt))

        PS = [top.enter_context(nc.psum_tensor("ps%d" % i, [128, 512], F32)) for i in range(8)]
        PT = [Tl("ps%d" % i, ps=True) for i in range(8)]
        psi = [0]

        def nextps():
            i = psi[0]
            psi[0] = (i + 1) % 8
            return PS[i], PT[i]

        ident_f = sb("ident_f", [128, 128], F32); T_identf = Tl()
        ident_b = sb("ident_b", [128, 128], BF16); T_identb = Tl()
        ones_b = sb("ones_b", [128, 128], BF16); T_ones = Tl()
        modT = sb("modT", [128, 48, 5], F32); T_mod = Tl()
        A1 = sb("A1", [128, KC, 5], F32); T_A1 = Tl()
        A2 = sb("A2", [128, KC, 5], F32); T_A2 = Tl()
        hT = sb("hT", [128, KC, NT], BF16)
        T_h = [[Tl() for _ in TBS] for _ in range(KC)]
        x1raw = sb("x1raw", [128, KC * NT], F32)
        mid_scope = top.enter_context(ExitStack())
        uT = sb("uT", [128, 4, NT], BF16, mid_scope)
        T_u = [[Tl() for _ in TBS] for _ in range(4)]
        x1T = x1raw[:, :].rearrange("p (k t) -> p k t", k=KC)
        xbv = x1raw[:, :].bitcast(BF16)
        T_x1 = [[Tl() for _ in TBS] for _ in range(KC)]

        k.ld(ident_f[:], ident_d[:, :], [T_identf])
        k.cp("dve", ident_b[:], ident_f[:], [T_identf], [T_identb])
        k.memset("pool", ones_b[:], 1.0, [T_ones])

        with ExitStack() as pa:
            cs = sb("cs", [128, KC, 5], F32, pa); T_cs = Tl()
            bT = sb("bT", [128, 48], F32, pa); T_bT = Tl()
            g1 = sb("g1", [128, KC], F32, pa); T_g1 = Tl()
            g2 = sb("g2", [128, KC], F32, pa); T_g2 = Tl()
            wab = [sb("wab%d" % i, [128, KC, 512], F32, pa) for i in range(2)]
            T_wab = [Tl(), Tl()]
            xa = x1T; T_xa = [Tl() for _ in range(KC)]
            sq = [sb("sq%d" % i, [128, NT], BF16, pa) for i in range(2)]; T_sq = [Tl(), Tl()]
            rstd = sb("rstd", [128, NT], F32, pa); T_rstd = [Tl() for _ in TBS]
            tmp = [sb("tmpA%d" % i, [128, NT], F32, pa) for i in range(2)]; T_tmp = [Tl(), Tl()]

            k.ld(cs[:], cT_d[:, :, :], [T_cs])
            k.ld(bT[:], bada_d[:, :], [T_bT])
            k.ld(g1[:], g1_d[:, :], [T_g1])
            k.ld(g2[:], g2_d[:, :], [T_g2])
            k.act(cs[:], cs[:], AF.Silu, [T_cs], [T_cs])
            for kc in range(KC):
                k.ld(xa[:, kc, :], xT_d[:, kc, :], [T_xa[kc]])
            psm, T_psm = PS[7], PT[7]
            wv = wada_d.rearrange("(kc p) n -> p kc n", p=128)
            for jb in range(12):
                w = wab[jb % 2]; Tw = T_wab[jb % 2]
                for kc in range(KC):
                    k.ld(w[:, kc, :], wv[:, kc, jb * 512:(jb + 1) * 512], [Tw], q="sp")
                for j in range(4):
                    col = (jb * 4 + j) * 5
                    for kc in range(KC):
                        k.mm(psm[:, col:col + 5], w[:, kc, j * 128:(j + 1) * 128], cs[:, kc, :],
                             kc == 0, kc == KC - 1, [Tw, T_cs], [T_psm])
            k.tt("dve", modT[:], psm[:, 0:240].rearrange("p (j r) -> p j r", r=5),
                 bT[:, :].unsqueeze(2).to_broadcast([128, 48, 5]), ALU.add, [T_psm, T_bT], [T_mod])
            for (Ax, TAx, gx, Tgx, off) in ((A1, T_A1, g1, T_g1, 8), (A2, T_A2, g2, T_g2, 32)):
                k.ts("dve", Ax[:], modT[:, off:off + 8, :], 1.0, None, ALU.add, None, [T_mod], [TAx])
                k.tt("dve", Ax[:], Ax[:], gx[:, :].unsqueeze(2).to_broadcast([128, KC, 5]), ALU.mult, [TAx, Tgx], [TAx])
            for kc in range(KC):
                s_ = sq[kc % 2]; Ts = T_sq[kc % 2]
                k.act(s_[:], xa[:, kc, :], AF.Square, [T_xa[kc]], [Ts])
                for bi, (t0, n) in enumerate(TBS):
                    k.mm(PS[bi][:, 0:n], ones_b[:], s_[:, t0:t0 + n], kc == 0, kc == KC - 1, [T_ones, Ts], [PT[bi]])
            for bi, (t0, n) in enumerate(TBS):
                k.act(rstd[:, t0:t0 + n], PS[bi][:, 0:n], AF.Sqrt, [PT[bi]], [T_rstd[bi]], bias=EPS, scale=1.0 / D)
                k.S.op("dve", (lambda o: (lambda e: e.reciprocal(out=o, in_=o)))(rstd[:, t0:t0 + n]), [T_rstd[bi]], [T_rstd[bi]])
            for kc in range(KC):
                t_ = tmp[kc % 2]; Tt = T_tmp[kc % 2]
                k.tt("dve", t_[:], xa[:, kc, :], rstd[:], ALU.mult, [T_xa[kc]] + T_rstd, [Tt])
                k.act(hT[:, kc, 0:TP], t_[:, 0:TP], AF.Identity, [Tt, T_A1, T_mod], T_h[kc][0:4],
                      bias=modT[:, kc, 0:1], scale=A1[:, kc, 0:1])
                for s in range(NS):
                    c0 = TP + s * TS
                    k.act(hT[:, kc, c0:c0 + TS], t_[:, c0:c0 + TS], AF.Identity, [Tt, T_A1, T_mod], [T_h[kc][4]],
                          bias=modT[:, kc, 1 + s:2 + s], scale=A1[:, kc, 1 + s:2 + s])
            S.flush()

        with ExitStack() as pb_:
            att = pb_
            vn = xbv[:, 0:10240].rearrange("p (t c) -> p t c", t=20); T_vn = [Tl() for _ in TTS]
            gates = sb("gates", [128, 20, 24], F32, att); T_gates = [Tl() for _ in TTS]
            pe_bc = sb("pe_bc_sb", [128, 2, 2, 64], F32, att); T_pe = Tl()
            qTs = sb("qTs", [128, 4, 16], BF16, att); T_qTs = Tl()
            kTs = sb("kTs", [128, 4, 16], BF16, att); T_kTs = Tl()
            vnew = sb("vnew", [4, NS, 2, 130], BF16, att); T_vnew = Tl()
            pr = att.enter_context(ExitStack())
            qT = sb("qT", [128, 4, NT], BF16, pr); T_q = [[Tl() for _ in TBS] for _ in range(4)]
            kT = sb("kT", [128, 4, NT], BF16, pr); T_k = [[Tl() for _ in TBS] for _ in range(4)]
            vsel = sb("vsel", [128, 20, 2, 65], BF16, pr); T_vsel = [Tl() for _ in TTS]
            vwin = sb("vwin", [128, 20, 2, 65], BF16, pr); T_vwin = [Tl() for _ in TTS]
            Xc = xbv[:, 10240:14336].rearrange("p (t s c) -> p t s c", t=16, s=2); T_Xc = [Tl() for _ in range(16)]
            lng = sb("lng", [128, 512], F32, pr); T_lng = Tl()
            lnb = sb("lnb", [128, 512], F32, pr); T_lnb = Tl()
            k.ld(pe_bc[:], pe_d[:, :, :, :], [T_pe])
            k.ld(lng[:], lng_d[:, :], [T_lng])
            k.ld(lnb[:], lnb_d[:, :], [T_lnb])
            k.memset("pool", vsel[:], 1.0, T_vsel)
            k.memset("pool", vwin[:], 1.0, T_vwin)

            with ExitStack() as pb:
                wu = sb("wu", [128, KC, 512], BF16, pb); T_wu = Tl()
                wq = sb("wq", [128, KC, 512], BF16, pb); T_wq = Tl()
                wvv = wq; T_wv = T_wq
                wkd = wu[:, :, :].rearrange("p k (j u d) -> p k j u d", j=4, u=2); T_wkd = T_wu
                wkv = sb("wkv", [128, KC, 792], BF16, pb); T_wkv = Tl()
                vg = [sb("vg%d" % i, [128, 512], F32, pb) for i in range(2)]; T_vg = [Tl(), Tl()]
                vt = [sb("vt%d" % i, [128, 512], F32, pb) for i in range(2)]; T_vt = [Tl(), Tl()]
                st6 = [sb("st6%d" % i, [128, 8], F32, pb) for i in range(2)]; T_st6 = [Tl(), Tl()]
                mv = [sb("mv%d" % i, [128, 4], F32, pb) for i in range(2)]; T_mv = [Tl(), Tl()]
                kvo = [sb("kvo0", [128, 768], F32, pb)] * 2; T_kvo = [Tl()] * 2

                wi = win_d.rearrange("(kc p) n -> p kc n", p=128)
                k.ldc(wu[:], wi[:, :, 0:512], [T_wu])
                k.ldc(wq[:], wi[:, :, 1024:1536], [T_wq])
                k.ldc(wkv[:], wi[:, :, 1536:2328], [T_wkv])

                def fm_proj(wt, Tw, nch, wsl, evac):
                    for m in range(nch):
                        for bi, (t0, n) in enumerate(TBS):
                            ps, Tp = nextps()
                            for kc in range(KC):
                                k.mm(ps[:, 0:n], wsl(wt, kc, m), hT[:, kc, t0:t0 + n], kc == 0, kc == KC - 1,
                                     [Tw, T_h[kc][bi]], [Tp])
                            evac(m, bi, t0, n, ps, Tp)

                fm_proj(wu, T_wu, 4, lambda wt, kc, m: wt[:, kc, m * 128:(m + 1) * 128],
                        lambda m, bi, t0, n, ps, Tp: k.act(uT[:, m, t0:t0 + n], ps[:, 0:n], AF.Gelu_apprx_tanh, [Tp], [T_u[m][bi]]))
                fm_proj(wq, T_wq, 4, lambda wt, kc, m: wt[:, kc, m * 128:(m + 1) * 128],
                        lambda m, bi, t0, n, ps, Tp: k.act(qT[:, m, t0:t0 + n], ps[:, 0:n], AF.Copy, [Tp], [T_q[m][bi]], scale=0.125))
                for j, (slot, kvh) in enumerate(((2, 0), (2, 1), (4, 0), (4, 1))):
                    c0 = 1536 + slot * 128 + kvh * 64
                    for dup in range(2):
                        k.ldc(wkd[:, :, j, dup, :], wi[:, :, c0:c0 + 64], [T_wkd])
                fm_proj(wkd, T_wkd, 4, lambda wt, kc, m: wt[:, kc, m, :, :],
                        lambda m, bi, t0, n, ps, Tp: k.evac(kT[:, m, t0:t0 + n], ps[:, 0:n], [Tp], [T_k[m][bi]]))

                k.ldc(wvv[:], wi[:, :, 512:1024], [T_wv])
                def tb_of(t0):
                    return min(t0 // 512, 4)

                for ti, (t0, n) in enumerate(TTS):
                    bi = tb_of(t0)
                    ps, Tp = nextps()
                    for kc in range(KC):
                        k.mm(ps[0:n, :], hT[:, kc, t0:t0 + n], wvv[:, kc, :], kc == 0, kc == KC - 1, [T_wv, T_h[kc][bi]], [Tp])
                    g_ = vg[ti % 2]; Tg = T_vg[ti % 2]
                    t_ = vt[ti % 2]; Tt = T_vt[ti % 2]
                    s6 = st6[ti % 2]; Ts6 = T_st6[ti % 2]
                    m_ = mv[ti % 2]; Tm = T_mv[ti % 2]
                    k.act(g_[0:n, :], ps[0:n, :], AF.Gelu_apprx_tanh, [Tp], [Tg])
                    S.op("dve", (lambda o, i: (lambda e: e.bn_stats(out=o, in_=i)))(s6[0:n, 0:6], g_[0:n, :]), [Tg], [Ts6])
                    S.op("dve", (lambda o, i: (lambda e: e.bn_aggr(out=o, in_=i)))(m_[0:n, 0:2], s6[0:n, 0:6]), [Ts6], [Tm])
                    k.act(m_[0:n, 2:3], m_[0:n, 1:2], AF.Sqrt, [Tm], [Tm], bias=EPS, scale=1.0)
                    S.op("dve", (lambda o, i: (lambda e: e.reciprocal(out=o, in_=i)))(m_[0:n, 3:4], m_[0:n, 2:3]), [Tm], [Tm])
                    k.ts("dve", t_[0:n, :], g_[0:n, :], m_[0:n, 0:1], m_[0:n, 3:4], ALU.subtract, ALU.mult, [Tg, Tm], [Tt])
                    k.tt("dve", t_[0:n, :], t_[0:n, :], lng[0:n, :], ALU.mult, [Tt, T_lng], [Tt])
                    k.tt("dve", t_[0:n, :], t_[0:n, :], lnb[0:n, :], ALU.add, [Tt, T_lnb], [Tt])
                    k.cp("pool", vn[0:n, ti, :], t_[0:n, :], [Tt], [T_vn[ti]])
                    if ti >= 16:
                        s = ti - 16
                        k.st(vch_o[s * TS:(s + 1) * TS, :], t_[0:n, :], [Tt])
                    psa, Tpa = nextps()
                    psb, Tpb = nextps()
                    for kc in range(KC):
                        k.mm(psa[0:n, :], hT[:, kc, t0:t0 + n], wkv[:, kc, 0:512], kc == 0, kc == KC - 1, [T_wkv, T_h[kc][bi]], [Tpa])
                    for kc in range(KC):
                        k.mm(psb[0:n, 0:280], hT[:, kc, t0:t0 + n], wkv[:, kc, 512:792], kc == 0, kc == KC - 1, [T_wkv, T_h[kc][bi]], [Tpb])
                    o_ = kvo[ti % 2]; To = T_kvo[ti % 2]
                    k.cp("act", o_[0:n, 0:512], psa[0:n, :], [Tpa], [To])
                    k.cp("act", o_[0:n, 512:768], psb[0:n, 0:256], [Tpb], [To])
                    k.st(kv_o[t0:t0 + n, :], o_[0:n, 0:512], [To])
                    if ti < 16:
                        k.tt("dve", Xc[:, ti, :, :].rearrange("p s (k d) -> p s k d", k=2),
                             psa[:, 0:256].rearrange("p (s k d) -> p s k d", s=2, k=2), pe_bc[:], ALU.add, [Tpa, T_pe], [T_Xc[ti]])
                    k.cp("dve", vsel[0:n, ti, :, 0:64], psa[0:n, 384:512].rearrange("p (k d) -> p k d", k=2), [Tpa], [T_vsel[ti]])
                    k.cp("dve", vwin[0:n, ti, :, 0:64], psb[0:n, 128:256].rearrange("p (k d) -> p k d", k=2), [Tpb], [T_vwin[ti]])
                    k.act(gates[0:n, ti, :], psb[0:n, 256:280], AF.Sigmoid, [Tpb], [T_gates[ti]])
                    if 12 <= ti < 16:
                        r0 = (ti - 12) * 128
                        k.st(winp_o[r0:r0 + 128, :], o_[:, 512:768], [To])
                    if ti >= 16:
                        s = ti - 16
                        k.st(wins_o[s, 512 - TS:512, :], o_[0:n, 512:768], [To])
                for s in range(NS):
                    k.S.dma("sp", (lambda s: (lambda e: e.dma_start(out=wins_o[s, 0:512 - TS, :], in_=swin_d[s, TS:512, :])))(s), (), (), is_output=True)
                S.flush()

            kcT = sb("kcT", [128, 2, 16, 2], BF16, pr); T_kc = Tl()
            vcT = sb("vcT", [128, 2, 16, 2], BF16, pr); T_vcT = Tl()
            vc = sb("vc", [32, 2, 64], BF16, pr); T_vc = Tl()
            PSb = [PS[i][:, :].bitcast(BF16) for i in range(8)]

            def tb_of(t0):
                return min(t0 // 512, 4)

            with ExitStack() as pc:
                Wd = xbv[:, 14336:14336 + 16384].rearrange("p (s d m) -> p s d m", s=2, d=64); T_Wd = Tl()
                wsT_f = sb("wsT_f", [128, 4, 128], F32, pc); T_wsf = Tl()
                wsT = sb("wsT_b", [128, 4, 128], BF16, pc); T_ws = Tl()
                wss_f = sb("wss_f", [4, 4, 4], F32, pc); T_wssf = Tl()
                wss = sb("wss", [4, 4, 4], BF16, pc); T_wss = Tl()
                bs_f = sb("bs_f", [1, 4, 128], F32, pc); T_bs = Tl()
                ones_f = sb("ones_f", [1, 128], F32, pc); T_onesf = Tl()
                W2d = sb("W2d", [128, 2, 2, 128], BF16, pc); T_W2d = Tl()
                b1r = sb("b1r", [128, 2], F32, pc); b2r = sb("b2r", [128, 2], F32, pc); T_b12 = Tl()
                hidT = sb("hidT", [128, 2, 32], BF16, pc); T_hid = Tl()

                k.ld(wsT_f[:], wsT_d[:, :, :], [T_wsf])
                k.ld(wss_f[:], wss_d[:, :, :], [T_wssf])
                k.ld(bs_f[:], bs_d[:, :, :], [T_bs])
                k.ld(b1r[:], b1r_d[:, :], [T_b12])
                k.ld(b2r[:], b2r_d[:, :], [T_b12])
                k.memset("dve", ones_f[:], 1.0, [T_onesf])
                S.op("pool", lambda e: e.affine_select(out=wsT[:], in_=wsT_f[:], pattern=[[0, 4], [1, 128]], compare_op=ALU.is_ge,
                                                       fill=0.0, base=0, channel_multiplier=-1), [T_wsf], [T_ws])
                S.op("pool", lambda e: e.affine_select(out=wss[:], in_=wss_f[:], pattern=[[0, 4], [1, 4]], compare_op=ALU.is_ge,
                                                       fill=0.0, base=0, channel_multiplier=-1), [T_wssf], [T_wss])
                k.memset("pool", Wd, 0.0, [T_Wd])
                k.memset("pool", W2d[:], 0.0, [T_W2d])
                for sl in range(2):
                    for half in range(2):
                        k.ldc(Wd[half * 64:(half + 1) * 64, sl, :, half * 64:(half + 1) * 64], w1_d[sl, :, :, :], [T_Wd])
                    for parity in range(2):
                        for dup in range(2):
                            k.ldc(W2d[parity * 64:(parity + 1) * 64, sl, parity, dup * 64:(dup + 1) * 64], w2_d[sl, :, :], [T_W2d])

                for ti, (t0, n) in enumerate(TTS):
                    bi = tb_of(t0)
                    ps, Tp = nextps()
                    for g in range(4):
                        if ti < 16:
                            k.mm(ps[:, g * 128:(g + 1) * 128], vn[:, ti, g * 128:(g + 1) * 128], wsT[:, g, :], True, False, [T_vn[ti], T_ws], [Tp])
                            k.mm(ps[:, g * 128:(g + 1) * 128], ones_f[0:1, :], bs_f[0:1, g, :], False, True, [T_onesf, T_bs], [Tp])
                        else:
                            k.mm(ps[:, g * 128:g * 128 + 4], vn[0:4, ti, g * 128:(g + 1) * 128], wss[0:4, g, :], True, False, [T_vn[ti], T_wss], [Tp])
                            k.mm(ps[:, g * 128:g * 128 + 4], ones_f[0:1, :], bs_f[0:1, g, 0:4], False, True, [T_onesf, T_bs], [Tp])
                    uv = uT[:, :, t0:t0 + n]
                    Tus = [T_u[g][bi] for g in range(4)]
                    k.tt("dve", uv, uv, ps[:, :].rearrange("p (g t) -> p g t", g=4)[:, :, 0:n], ALU.mult, [Tp] + Tus, Tus)

                Xc5 = Xc.rearrange("p t s (k d) -> p t s k d", k=2)
                for sl in range(2):
                    ps, Tp = nextps()
                    for d in range(64):
                        k.mm(ps[:, 0:32].rearrange("p (t k) -> p t k", k=2), Wd[:, sl, d, :], Xc5[:, :, sl, :, d], d == 0, d == 63,
                             [T_Wd] + T_Xc, [Tp])
                    k.act(hidT[:, sl, :], ps[:, 0:32], AF.Gelu_apprx_tanh, [Tp, T_b12], [T_hid], bias=b1r[:, sl:sl + 1])
                    for parity in range(2):
                        ps2, Tp2 = nextps()
                        k.mm(ps2[:, 0:32], W2d[:, sl, parity, :], hidT[:, sl, :], True, True, [T_W2d, T_hid], [Tp2])
                        dst = (kcT if sl == 0 else vcT)
                        Td = T_kc if sl == 0 else T_vcT
                        k.act(dst[:, :, :, parity], ps2[:, 0:32].rearrange("p (t k) -> p k t", k=2), AF.Identity, [Tp2, T_b12], [Td],
                              bias=b2r[:, sl:sl + 1])
                for kvh in range(2):
                    pi_ = psi[0]
                    ps, Tp = nextps()
                    k.tr(PSb[pi_][0:32, 0:64], vcT[0:64, kvh, :, :].rearrange("p t q -> p (t q)"), ident_b[0:64, 0:64], [T_vcT, T_identb], [Tp])
                    k.cp("dve", vc[:, kvh, :], PSb[pi_][0:32, 0:64], [Tp], [T_vc])
                S.flush()
                if STOP == 'C':
                    S.wait_all_outputs("sp"); S.flush(); S.close(); return nc

            with ExitStack() as pd:
                ybT = xbv[:, 0:4 * NT].rearrange("p (c t) -> p c t", c=4); T_yb = [Tl() for _ in TBS]
                ybacc = sb("ybacc", [128, 4, 512], F32, pd); T_ya = [Tl() for _ in range(4)]
                msk = sb("msk", [128, 4, 16, 32], F32, pd); T_msk = Tl()
                Eb = sb("Eb", [64, 2048], BF16, pd); T_E = Tl()
                Cm = sb("Cm", [128, 2, 128], BF16, pd); T_C = Tl()
                penT = sb("penT", [64, 512], BF16, pd); T_penT = Tl()
                PTb = [sb("PTb%d" % i, [128, 512], BF16, pd) for i in range(4)]; T_PT = [Tl() for _ in range(4)]
                sm = sb("sm", [128, 8, 32], F32, pd); T_sm = Tl()
                ee = sb("ee", [128, 8, 32], F32, pd); T_ee = Tl()
                pp = sb("pp", [128, 8, 32], F32, pd); T_pp = Tl()
                pbf = sb("pbf", [128, 8, 32], BF16, pd); T_pbf = Tl()
                pT = sb("pT", [32, 8, 128], BF16, pd); T_pT = Tl()
                imp = sb("imp", [128, 2, 32], F32, pd); T_imp = Tl()
                t8 = sb("t8", [128, 16], F32, pd); T_t8 = Tl()
                wk8 = sb("wk8", [128, 32], F32, pd); T_wk8 = Tl()
                penf = sb("penf", [128, 2, 32], F32, pd); T_penf = Tl()
                pen = sb("pen", [128, 2, 32], BF16, pd); T_pen = Tl()
                mx = sb("mx", [128, 8], F32, pd); T_mx = Tl()
                rs = sb("rs", [128, 8], F32, pd); T_rs = Tl()
                rs4 = sb("rs4", [128, 4], F32, pd); T_rs4 = Tl()
                tmpo = sb("tmpo", [128, 4, 64], F32, pd); T_tmpo = Tl()
                ybb = sb("ybb", [128, 512], BF16, pd); T_ybb = Tl()

                k.ld(msk[:], msk_d[:, :, :, :], [T_msk])
                k.ldc(Eb[:], E_d[:, :], [T_E])
                k.ldc(Cm[:], Cm_d[:, :, :], [T_C])
                if 'NOS' in STOP:
                    k.memset("pool", ybT[:, :, TP:NT], 0.0, [T_yb[4]])

                rot = [4]

                def rps():
                    i = rot[0]
                    rot[0] = 4 + (i - 3) % 4
                    return i

                def B3(ap, sh):
                    return ap.to_broadcast(sh)

                for qb in range(4):
                    for j in range(4):
                        qt = 4 * qb + j
                        if DCUT < 1:
                            continue
                        piA = rps(); piB = rps()
                        for h in range(8):
                            base = (h % 2) * 64; ch = h // 2; kvh = h // 4
                            bk = piA if h % 2 == 0 else piB
                            k.mm(PS[bk][:, (h // 2) * 32:(h // 2 + 1) * 32], qT[base:base + 64, ch, qt * 128:(qt + 1) * 128],
                                 kcT[base:base + 64, kvh, :, :].rearrange("p t q -> p (t q)"), True, True, [T_q[ch][qb], T_kc], [PT[bk]])
                        for par, bk in ((0, piA), (1, piB)):
                            k.tt("dve", sm[:, par::2, :], PS[bk][:, 0:128].rearrange("p (h n) -> p h n", h=4),
                                 B3(msk[:, 0, qt, :].unsqueeze(1), [128, 4, 32]), ALU.add, [PT[bk], T_msk], [T_sm])
                        S.op("dve", lambda e: e.tensor_reduce(out=mx[:, 0:8], in_=sm[:], axis=AX.X, op=ALU.max), [T_sm], [T_mx])
                        k.tt("dve", sm[:], sm[:], B3(mx[:, 0:8].unsqueeze(2), [128, 8, 32]), ALU.subtract, [T_sm, T_mx], [T_sm])
                        k.act(ee[:], sm[:], AF.Exp, [T_sm], [T_ee])
                        k.tt("dve", ee[:], ee[:], B3(msk[:, 1, qt, :].unsqueeze(1), [128, 8, 32]), ALU.mult, [T_ee, T_msk], [T_ee])
                        S.op("dve", lambda e: e.tensor_reduce(out=rs[:, 0:8], in_=ee[:], axis=AX.X, op=ALU.add), [T_ee], [T_rs])
                        k.ts("dve", rs[:], rs[:], 1e-30, None, ALU.max, None, [T_rs], [T_rs])
                        S.op("dve", lambda e: e.reciprocal(out=rs[:], in_=rs[:]), [T_rs], [T_rs])
                        k.tt("dve", pp[:], ee[:], B3(rs[:, 0:8].unsqueeze(2), [128, 8, 32]), ALU.mult, [T_ee, T_rs], [T_pp])
                        k.cp("pool", pbf[:], pp[:], [T_pp], [T_pbf])
                        if qb >= 2 and 'NOTOPK' not in STOP:
                            S.op("dve", lambda e: e.tensor_reduce(out=imp[:], in_=pp[:].rearrange("p (k g) n -> p k n g", k=2), axis=AX.X, op=ALU.add),
                                 [T_pp], [T_imp])
                            k.tt("dve", imp[:], imp[:], B3(msk[:, 2, qt, :].unsqueeze(1), [128, 2, 32]), ALU.mult, [T_imp, T_msk], [T_imp])
                            k.tt("dve", imp[:], imp[:], B3(msk[:, 3, qt, :].unsqueeze(1), [128, 2, 32]), ALU.add, [T_imp, T_msk], [T_imp])
                            for kvh in range(2):
                                S.op("dve", (lambda kvh: lambda e: e.max(out=t8[:, 0:8], in_=imp[:, kvh, :]))(kvh), [T_imp], [T_t8])
                                S.op("dve", (lambda kvh: lambda e: e.match_replace(out=wk8[:], in_to_replace=t8[:, 0:8], in_values=imp[:, kvh, :], imm_value=-1e30))(kvh),
                                     [T_imp, T_t8], [T_wk8])
                                S.op("dve", lambda e: e.max(out=t8[:, 8:16], in_=wk8[:]), [T_wk8], [T_t8])
                                k.ts("dve", penf[:, kvh, :], imp[:, kvh, :], t8[:, 15:16], -NEGB, ALU.is_ge, ALU.mult, [T_imp, T_t8], [T_penf])
                            k.ts("dve", pen[:], penf[:], NEGB, None, ALU.add, None, [T_penf], [T_pen])
                            pi2 = rps()
                            k.tr(PSb[pi2][0:64, 0:128], pen[:].rearrange("p k n -> p (k n)"), ident_b[:], [T_pen, T_identb], [PT[pi2]])
                            k.cp("act", penT[:, j * 128:(j + 1) * 128], PSb[pi2][0:64, 0:128], [PT[pi2]], [T_penT])
                        if DCUT < 2:
                            continue
                        pi3 = rps()
                        for h in range(8):
                            k.tr(PSb[pi3][0:32, h * 128:(h + 1) * 128], pbf[:, h, :], ident_b[:], [T_pbf, T_identb], [PT[pi3]])
                        k.cp("act", pT[:].rearrange("p h q -> p (h q)"), PSb[pi3][0:32, 0:1024], [PT[pi3]], [T_pT])
                        if DCUT < 3:
                            continue
                        pi4 = rps(); pso, Tpo = PS[pi4], PT[pi4]
                        for h in range(8):
                            k.mm(pso[:, h * 64:(h + 1) * 64], pT[:, h, :], vc[:, h // 4, :], True, True, [T_pT, T_vc], [Tpo])
                        k.tt("dve", ybacc[:, j, :].rearrange("p (h d) -> p h d", h=8), pso[:, :].rearrange("p (h d) -> p h d", h=8),
                             B3(gates[:, qt, 0:8].unsqueeze(2), [128, 8, 64]), ALU.mult, [Tpo, T_gates[qt]], [T_ya[j]])

                    for kvh in range(2 if 'NOSEL' not in STOP else 0):
                        for bri, (vaug, T_va) in ((1, (vsel, T_vsel)), (2, (vwin, T_vwin))):
                            first = [True] * 4
                            kt_lo = 0 if bri == 1 else max(0, 4 * qb - 4)
                            q0 = qb * 512
                            km = (0 if bri == 1 else 2) + kvh
                            units = [(kt, hh) for kt in range(kt_lo, 4 * qb + 4) for hh in range(4)]

                            def scores(ui, kt, hh):
                                jlo = max(0, kt - 4 * qb)
                                jhi = 3 if bri == 1 else min(3, kt + 4 - 4 * qb)
                                c0, c1 = jlo * 128, (jhi + 1) * 128
                                h = 4 * kvh + hh
                                base = (h % 2) * 64; ch = h // 2
                                pi_ = rps(); ps, Tp = PS[pi_], PT[pi_]
                                grp = [(ps[:, c0:c1], kT[base:base + 64, km, kt * 128:(kt + 1) * 128], qT[base:base + 64, ch, q0 + c0:q0 + c1],
                                        [T_k[km][kt // 4], T_q[ch][qb]])]
                                if bri == 1 and qb >= 2:
                                    grp.append((ps[:, c0:c1], Eb[kvh * 32:(kvh + 1) * 32, kt * 128:(kt + 1) * 128], penT[kvh * 32:(kvh + 1) * 32, c0:c1],
                                                [T_E, T_penT]))
                                if kt >= 4 * qb:
                                    jd = kt - 4 * qb
                                    grp.append((ps[:, jd * 128:(jd + 1) * 128], ident_b[:], Cm[:, 0, :], [T_identb, T_C]))
                                if bri == 2 and 0 <= kt + 4 - 4 * qb <= 3:
                                    j4 = kt + 4 - 4 * qb
                                    grp.append((ps[:, j4 * 128:(j4 + 1) * 128], ident_b[:], Cm[:, 1, :], [T_identb, T_C]))
                                for gi, (o_, l_, r_, R_) in enumerate(grp):
                                    k.mm(o_, l_, r_, gi == 0, gi == len(grp) - 1, R_, [Tp])
                                pb_i = ui % 4
                                k.act(PTb[pb_i][:, c0:c1], ps[:, c0:c1], AF.Exp, [Tp], [T_PT[pb_i]])
                                return pb_i, jlo, jhi

                            def pvs(kt, hh, pb_i, jlo, jhi):
                                for j in range(jlo, jhi + 1):
                                    k.mm(PS[j][:, hh * 65:(hh + 1) * 65], PTb[pb_i][:, j * 128:(j + 1) * 128], vaug[:, kt, kvh, :],
                                         first[j], kt == 4 * qb + j, [T_PT[pb_i], T_va[kt]], [PT[j]], sgc=True)
                                    first[j] = False

                            LAG = 3
                            pend = []
                            for ui in range(len(units) + LAG):
                                if ui < len(units):
                                    pend.append(scores(ui, *units[ui]))
                                if ui >= LAG:
                                    pvs(*units[ui - LAG], *pend[ui - LAG])
                            for j in range(4):
                                qt = 4 * qb + j
                                po3 = PS[j][:, 0:260].rearrange("p (h c) -> p h c", c=65)
                                S.op("dve", (lambda po3: lambda e: e.reciprocal(out=rs4[:], in_=po3[:, :, 64]))(po3), [PT[j]], [T_rs4])
                                k.tt("dve", rs4[:], rs4[:], gates[:, qt, bri * 8 + 4 * kvh:bri * 8 + 4 * kvh + 4], ALU.mult, [T_rs4, T_gates[qt]], [T_rs4])
                                k.tt("dve", tmpo[:], po3[:, :, 0:64], B3(rs4[:, 0:4].unsqueeze(2), [128, 4, 64]), ALU.mult, [PT[j], T_rs4], [T_tmpo])
                                ya = ybacc[:, j, kvh * 256:(kvh + 1) * 256].rearrange("p (h d) -> p h d", h=4)
                                k.tt("dve", ya, ya, tmpo[:], ALU.add, [T_ya[j], T_tmpo], [T_ya[j]])
                    for j in range(4 if DCUT >= 4 else 0):
                        qt = 4 * qb + j
                        k.cp("act", ybb[:], ybacc[:, j, :], [T_ya[j]], [T_ybb])
                        pi_ = rps()
                        for c in range(4):
                            k.tr(PSb[pi_][:, c * 128:(c + 1) * 128], ybb[:, c * 128:(c + 1) * 128], ident_b[:], [T_ybb, T_identb], [PT[pi_]])
                        k.cp("dve", ybT[:, :, qt * 128:(qt + 1) * 128], PSb[pi_][:, 0:512].rearrange("p (c q) -> p c q", c=4), [PT[pi_]], [T_yb[qb]])
                k.cp("dve", qTs[:], qT[:, :, TP:NT], [T_q[m][4] for m in range(4)], [T_qTs])
                k.cp("dve", kTs[:], kT[:, :, TP:NT], [T_k[m][4] for m in range(4)], [T_kTs])
                for s in range(NS):
                    k.cp("pool", vnew[0:4, s, 0, :], vsel[0:4, 16 + s, :, :].rearrange("p k c -> p (k c)"), [T_vsel[16 + s]], [T_vnew])
                    k.cp("pool", vnew[0:4, s, 1, :], vwin[0:4, 16 + s, :, :].rearrange("p k c -> p (k c)"), [T_vwin[16 + s]], [T_vnew])
                S.flush()
                if STOP.startswith('D'):
                    S.wait_all_outputs("sp"); S.flush(); S.close(); return nc
            pr.close()
            with ExitStack() as ps_:
                kselT = sb("kselT", [128, NPG * 128], BF16, ps_); T_kst = [Tl() for _ in range(32)]
                vsl = sb("vsl", [128, NPG, 130], BF16, ps_); T_vsl = [Tl() for _ in range(32)]
                NPB = 4
                pgbuf = sb("pgbuf", [128, NPB, 512], F32, ps_)
                pg = [pgbuf[:, i, :] for i in range(NPB)]; T_pg = [Tl() for _ in range(NPB)]
                XcsB = [xbv[:, 8256 + b * 2048:8256 + (b + 1) * 2048].rearrange("p (t s c) -> p t s c", t=8, s=2) for b in range(2)]
                T_XcsB = [Tl(), Tl()]
                Wd = xbv[:, 14336:14336 + 16384].rearrange("p (s d m) -> p s d m", s=2, d=64); T_Wd2 = Tl()
                ptb = sb("ptb", [128, 128], I32, ps_); T_ptb = Tl()
                idxf = sb("idxf", [128, 128], F32, ps_); T_idxf = Tl()
                idxi = ptb; T_idxi = T_ptb
                iot_i = sb("iot_i", [128, 1], I32, ps_); iot_f = sb("iot_f", [128, 1], F32, ps_); T_iot = Tl()
                hidTs = sb("hidTs", [128, 2, 256], BF16, ps_); T_hid = Tl()
                kcTs = sb("kcTs", [128, 256], BF16, ps_); T_kcs = Tl()
                vcs = sb("vcs", [128, 2, 128], BF16, ps_); T_vcs = Tl()
                W2p = sb("W2p", [128, 2, 2, 128], BF16, ps_); T_W2p = Tl()
                w2v = sb("w2v", [128, 64], BF16, ps_); T_w2v = Tl()
                b1s = sb("b1s", [128, 2], F32, ps_); b2k = sb("b2k", [128, 1], F32, ps_); b2v = sb("b2v", [128, 128], F32, ps_); T_bs2 = Tl()
                qbd = sb("qbd", [128, 64], BF16, ps_); T_qbd = Tl()
                knew = sb("knew", [128, 2, 4], BF16, ps_); T_knew = Tl()
                ssm = [pgbuf[0:64, 0, :]] * 2; T_ssm = [T_pg[0]] * 2
                pex = [sb("pex0", [64, 512], BF16, ps_)] * 2; T_pex = [Tl()] * 2
                PTs = [sb("PTs0", [128, 256], BF16, ps_)] * 2; T_PTs = [Tl()] * 2
                PTn = sb("PTn", [4, 64], BF16, ps_); T_PTn = Tl()
                pc = pgbuf[0:64, 1, 0:256]; T_pc = T_pg[1]
                ec = pc; T_ec = T_pc
                pcb = sb("pcb", [64, 256], BF16, ps_); T_pcb = Tl()
                impf = pgbuf[0:64, 2, 0:256]; T_impf = T_pg[2]
                t8s = sb("t8s", [64, 16], F32, ps_); T_t8s = Tl()
                wks = pgbuf[0:64, 2, 256:512]; T_wks = T_pg[2]
                pens = sb("pens", [64, 256], F32, ps_); T_pens = Tl()
                mxs = sb("mxs", [64, 2], F32, ps_); T_mxs = Tl()
                pTs = sb("pTs", [128, 2, 64], BF16, ps_); T_pTs = Tl()
                kwinT = sb("kwinT", [128, 512], BF16, ps_); T_kwT = Tl()
                vwn = sb("vwn", [128, 4, 130], BF16, ps_); T_vwn = Tl()
                swt = [pgbuf[:, 3, 0:256]] * 2; T_swt = [T_pg[3]] * 2
                Gm = sb("Gm", [64, 64], F32, ps_); SelT = sb("SelT", [4, 64], F32, ps_); Hsel = sb("Hsel", [64, 8], F32, ps_)
                CN = sb("CN", [64, 4], F32, ps_); CW = sb("CW", [64, 4], F32, ps_); T_cst = Tl()
                gr3 = sb("gr3", [64, 3, 8], F32, ps_); grow = sb("grow", [64, 3], F32, ps_); T_grow = Tl()
                ob = sb("ob", [64, 64], F32, ps_); T_ob = Tl()
                obb = sb("obb", [64, 64], BF16, ps_); T_obb = Tl()
                rs2 = sb("rs2", [64, 1], F32, ps_); T_rs2 = Tl()
                tmo = pgbuf[0:64, 1, 256:320]; T_tmo = T_pg[1]
                cache2d = cache_d

                for (t_, d_) in ((Gm, Gm_d), (SelT, SelT_d), (Hsel, Hsel_d), (CN, CN_d), (CW, CW_d)):
                    k.ld(t_[:], d_[:, :], [T_cst])
                k.ld(b1s[:], b1r_d[:, :], [T_bs2]); k.ld(b2k[:], b2k_d[:, :], [T_bs2]); k.ld(b2v[:], b2v_d[:, :], [T_bs2])
                k.memset("pool", W2p[:], 0.0, [T_W2p])
                for parity in range(2):
                    for kvh in range(2):
                        k.ldc(W2p[parity * 64:(parity + 1) * 64, parity, kvh, kvh * 64:(kvh + 1) * 64], w2_d[0, :, :], [T_W2p])
                    k.ldc(w2v[parity * 64:(parity + 1) * 64, :], w2_d[1, :, :], [T_w2v])
                k.memset("pool", vsl[:], 1.0, T_vsl)
                k.memset("pool", vwn[:], 1.0, [T_vwn])
                S.op("pool", lambda e: e.iota(iot_i[:], pattern=[[0, 1]], base=0, channel_multiplier=1), (), [T_iot])
                k.cp("dve", iot_f[:], iot_i[:], [T_iot], [T_iot])

                rot = [1]

                def rp():
                    i = rot[0]
                    rot[0] = 1 + (i % 7)
                    return i

                PO, T_PO = PS[0], PT[0]
                cvt = [0]
                for s in range(NS if 'NOS' not in STOP else 0):
                    tg = 16 + s
                    k.ld(ptb[:], pt_d[s, :, :], [T_ptb])
                    k.cp("dve", idxf[:], ptb[:], [T_ptb], [T_idxf])
                    k.ts("dve", idxf[:], idxf[:], 128.0, iot_f[:, 0:1], ALU.mult, ALU.add, [T_idxf, T_iot], [T_idxf])
                    k.cp("dve", idxi[:], idxf[:], [T_idxf], [T_idxi])
                    k.memset("pool", qbd[:], 0.0, [T_qbd])
                    for h in range(8):
                        kvh = h // 4; g = h % 4; sb_ = (h % 2) * 64
                        k.cp("dve", qbd[kvh * 64:(kvh + 1) * 64, kvh * 32 + g * 4:kvh * 32 + g * 4 + 4], qTs[sb_:sb_ + 64, h // 2, s * 4:(s + 1) * 4],
                             [T_qTs], [T_qbd])
                    for br in range(2):
                        for kvh in range(2):
                            k.cp("dve", knew[kvh * 64:(kvh + 1) * 64, br, :], kTs[kvh * 64:(kvh + 1) * 64, br * 2 + kvh, s * 4:(s + 1) * 4], [T_kTs], [T_knew])
                    pi_ = rp()
                    k.mm(PS[pi_][0:64, 0:24], SelT[:, :], gates[0:4, tg, :], True, True, [T_cst, T_gates[tg]], [PT[pi_]])
                    k.tt("dve", gr3[:], PS[pi_][0:64, 0:24].rearrange("p (b h) -> p b h", b=3), Hsel[:, :].unsqueeze(1).to_broadcast([64, 3, 8]), ALU.mult,
                         [PT[pi_], T_cst], [T_grow])
                    S.op("dve", lambda e: e.tensor_reduce(out=grow[:], in_=gr3[:], axis=AX.X, op=ALU.add), [T_grow], [T_grow])

                    for j in range(NPG):
                        p_ = pg[j % NPB]; Tp_ = T_pg[j % NPB]
                        S.dma("pool", (lambda p_, j: lambda e: e.indirect_dma_start(out=p_[:, :], out_offset=None, in_=cache2d[:, :],
                                                                                    in_offset=bass.IndirectOffsetOnAxis(ap=idxi[:, j:j + 1], axis=0)))(p_, j),
                              [T_idxi], [Tp_])
                        jj = j % 8
                        Xcs = XcsB[(j // 8) % 2]; T_Xcs = T_XcsB[(j // 8) % 2]
                        Xcs5 = Xcs.rearrange("p t s (k d) -> p t s k d", k=2)
                        k.tt("dve", Xcs[:, jj, :, :], p_[:, 0:256].rearrange("p (s c) -> p s c", s=2), pe_bc[:].rearrange("p s k d -> p s (k d)"), ALU.add,
                             [Tp_, T_pe], [T_Xcs])
                        k.cp("act", vsl[:, j, :].rearrange("p (k c) -> p k c", k=2)[:, :, 0:64], p_[:, 384:512].rearrange("p (k d) -> p k d", k=2), [Tp_], [T_vsl[j // 4]])
                        if j % 4 == 0:
                            pit = rp()
                        k.tr(PS[pit][:, (j % 4) * 128:(j % 4 + 1) * 128], p_[:, 256:384], ident_f[:], [Tp_, T_identf], [PT[pit]])
                        if j % 4 == 3:
                            k.evac(kselT[:, (j - 3) * 128:(j + 1) * 128], PS[pit][:, :], [PT[pit]], [T_kst[j // 4]])
                        if jj == 7:
                            ch = j // 8
                            for sl in range(2):
                                pic = rp()
                                for d in range(64):
                                    k.mm(PS[pic][:, 0:16].rearrange("p (t k) -> p t k", k=2), Wd[:, sl, d, :], Xcs5[:, :, sl, :, d], d == 0, d == 63,
                                         [T_Wd2, T_Xcs], [PT[pic]])
                                k.act(hidTs[:, sl, ch * 16:(ch + 1) * 16], PS[pic][:, 0:16], AF.Gelu_apprx_tanh, [PT[pic], T_bs2], [T_hid], bias=b1s[:, sl:sl + 1])
                    hv = hidTs[:, :, :].rearrange("p s (g k) -> p s g k", k=2)
                    for parity in range(2):
                        pi_ = rp()
                        for kvh in range(2):
                            k.mm(PS[pi_][:, 0:128], W2p[:, parity, kvh, :], hv[:, 0, :, kvh], kvh == 0, kvh == 1, [T_W2p, T_hid], [PT[pi_]])
                        k.act(kcTs[:, parity * 128:(parity + 1) * 128], PS[pi_][:, 0:128], AF.Identity, [PT[pi_], T_bs2], [T_kcs], bias=b2k[:, 0:1])
                    for parity in range(2):
                        pi_ = rp()
                        for kvh in range(2):
                            k.mm(PS[pi_][:, kvh * 64:(kvh + 1) * 64], hv[parity * 64:(parity + 1) * 64, 1, :, kvh], w2v[parity * 64:(parity + 1) * 64, :], True, True,
                                 [T_hid, T_w2v], [PT[pi_]])
                        k.tt("dve", vcs[:, parity, :], PS[pi_][:, 0:128], b2v[:, :], ALU.add, [PT[pi_], T_bs2], [T_vcs])
                    pi_ = rp()
                    k.mm(PS[pi_][0:64, 0:256], qbd[:, :], kcTs[:, :], True, True, [T_qbd, T_kcs], [PT[pi_]])
                    S.op("dve", (lambda pi_: lambda e: e.tensor_reduce(out=mxs[:, 0:1], in_=PS[pi_][0:64, 0:256], axis=AX.X, op=ALU.max))(pi_), [PT[pi_]], [T_mxs])
                    k.ts("dve", mxs[:, 0:1], mxs[:, 0:1], -1.0, None, ALU.mult, None, [T_mxs], [T_mxs])
                    k.act(ec[:], PS[pi_][0:64, 0:256], AF.Exp, [PT[pi_], T_mxs], [T_ec, T_mxs], bias=mxs[:, 0:1], accum_out=mxs[:, 1:2])
                    S.op("dve", lambda e: e.reciprocal(out=mxs[:, 1:2], in_=mxs[:, 1:2]), [T_mxs], [T_mxs])
                    k.ts("dve", pc[:], ec[:], mxs[:, 1:2], None, ALU.mult, None, [T_mxs, T_pc], [T_pc])
                    k.cp("pool", pcb[:], pc[:], [T_pc], [T_pcb])
                    pi2 = rp()
                    k.mm(PS[pi2][0:64, 0:256], Gm[:, :], pc[:, :], True, True, [T_cst, T_pc], [PT[pi2]])
                    k.cp("act", impf[:], PS[pi2][0:64, 0:256], [PT[pi2]], [T_impf])
                    k.memset("dve", impf[:, 0:1], 1e4, [T_impf])
                    k.memset("dve", impf[:, 255:256], 1e4, [T_impf])
                    S.op("dve", lambda e: e.max(out=t8s[:, 0:8], in_=impf[:]), [T_impf], [T_t8s])
                    S.op("dve", lambda e: e.match_replace(out=wks[:], in_to_replace=t8s[:, 0:8], in_values=impf[:], imm_value=-1e30), [T_impf, T_t8s], [T_wks])
                    S.op("dve", lambda e: e.max(out=t8s[:, 8:16], in_=wks[:]), [T_wks], [T_t8s])
                    k.ts("dve", pens[:], impf[:], t8s[:, 14:15], -NEGB, ALU.is_ge, ALU.mult, [T_impf, T_t8s], [T_pens])
                    k.ts("dve", pens[:], pens[:], NEGB, None, ALU.add, None, [T_pens], [T_pens])
                    pi3 = rp()
                    for parity in range(2):
                        k.tr(PSb[pi3][:, parity * 64:(parity + 1) * 64], pcb[:, parity * 128:(parity + 1) * 128], ident_b[0:64, 0:64], [T_pcb, T_identb], [PT[pi3]])
                    k.cp("act", pTs[:].rearrange("p a r -> p (a r)"), PSb[pi3][:, 0:128], [PT[pi3]], [T_pTs])
                    pi4 = rp()
                    for parity in range(2):
                        k.mm(PS[pi4][0:64, 0:128], pTs[:, parity, :], vcs[:, parity, :], parity == 0, parity == 1, [T_pTs, T_vcs], [PT[pi4]])
                    for half in range(2):
                        rsl = slice(32 * half, 32 * half + 32)
                        k.ts("dve", ob[rsl, :], PS[pi4][rsl, half * 64:(half + 1) * 64], grow[rsl, 0:1], None, ALU.mult, None, [PT[pi4], T_grow], [T_ob])

                    for t in range(4):
                        w_ = swt[t % 2]; Tw_ = T_swt[t % 2]
                        k.ld(w_[:], swin_d[s, t * 128:(t + 1) * 128, :], [Tw_])
                        if t == 0:
                            piw = rp()
                        k.tr(PS[piw][:, t * 128:(t + 1) * 128], w_[:, 0:128], ident_f[:], [Tw_, T_identf], [PT[piw]])
                        k.cp("pool", vwn[:, t, :].rearrange("p (k c) -> p k c", k=2)[:, :, 0:64], w_[:, 128:256].rearrange("p (k d) -> p k d", k=2), [Tw_], [T_vwn])
                    k.evac(kwinT[:], PS[piw][:, :], [PT[piw]], [T_kwT])

                    pen3 = pens[:].rearrange("r (a j) -> r j a", a=2)
                    for br in range(2):
                        ngrp = 32 if br == 0 else 1
                        first = True
                        for gq in range(ngrp):
                            b_ = cvt[0] % 2; cvt[0] += 1
                            pi_ = rp()
                            if br == 0:
                                k.mm(PS[pi_][0:64, :], qbd[:, :], kselT[:, gq * 512:(gq + 1) * 512], True, True, [T_qbd, T_kst[gq]], [PT[pi_]])
                                k.tt("dve", ssm[b_][:].rearrange("r (j a i) -> r j a i", j=4, a=2), PS[pi_][0:64, :].rearrange("r (j a i) -> r j a i", j=4, a=2),
                                     pen3[:, gq * 4:(gq + 1) * 4, :].unsqueeze(3).to_broadcast([64, 4, 2, 64]), ALU.add, [PT[pi_], T_pens], [T_ssm[b_]])
                            else:
                                k.mm(PS[pi_][0:64, :], qbd[:, :], kwinT[:, :], True, True, [T_qbd, T_kwT], [PT[pi_]])
                                k.tt("dve", ssm[b_][:, 0:4], PS[pi_][0:64, 0:4], CW[:, :], ALU.add, [PT[pi_], T_cst], [T_ssm[b_]])
                            if br == 0:
                                k.act(pex[b_][:], ssm[b_][:], AF.Exp, [T_ssm[b_]], [T_pex[b_]])
                            else:
                                k.act(pex[b_][:, 0:4], ssm[b_][:, 0:4], AF.Exp, [T_ssm[b_]], [T_pex[b_]])
                                k.act(pex[b_][:, 4:512], PS[pi_][0:64, 4:512], AF.Exp, [PT[pi_]], [T_pex[b_]])
                            pit2 = rp()
                            for jj in range(4):
                                k.tr(PSb[pit2][:, jj * 64:(jj + 1) * 64], pex[b_][:, jj * 128:(jj + 1) * 128], ident_b[0:64, 0:64], [T_pex[b_], T_identb], [PT[pit2]])
                            k.evac(PTs[b_][:], PSb[pit2][:, 0:256], [PT[pit2]], [T_PTs[b_]])
                            for jj in range(4):
                                if br == 0:
                                    rhs_ = vsl[:, gq * 4 + jj, :]; Tr_ = T_vsl[gq]
                                else:
                                    rhs_ = vwn[:, jj, :]; Tr_ = T_vwn
                                k.mm(PO[0:64, 0:130], PTs[b_][:, jj * 64:(jj + 1) * 64], rhs_, first, False, [T_PTs[b_], Tr_], [T_PO])
                                first = False
                        pi_ = rp()
                        k.mm(PS[pi_][0:64, 0:4], qbd[:, :], knew[:, br, :], True, True, [T_qbd, T_knew], [PT[pi_]])
                        b_ = cvt[0] % 2; cvt[0] += 1
                        k.tt("dve", ssm[b_][:, 0:4], PS[pi_][0:64, 0:4], CN[:, :], ALU.add, [PT[pi_], T_cst], [T_ssm[b_]])
                        k.act(pex[b_][:, 0:4], ssm[b_][:, 0:4], AF.Exp, [T_ssm[b_]], [T_pex[b_]])
                        pit2 = rp()
                        k.tr(PSb[pit2][0:4, 0:64], pex[b_][:, 0:4], ident_b[0:64, 0:64], [T_pex[b_], T_identb], [PT[pit2]])
                        k.cp("dve", PTn[:], PSb[pit2][0:4, 0:64], [PT[pit2]], [T_PTn])
                        k.mm(PO[0:64, 0:130], PTn[:, :], vnew[0:4, s, br, :], False, True, [T_PTn, T_vnew], [T_PO])
                        for half in range(2):
                            rsl = slice(32 * half, 32 * half + 32)
                            c0 = half * 65
                            S.op("dve", (lambda rsl, c0: lambda e: e.reciprocal(out=rs2[rsl, :], in_=PO[rsl, c0 + 64:c0 + 65]))(rsl, c0), [T_PO], [T_rs2])
                            k.tt("dve", rs2[rsl, :], rs2[rsl, :], grow[rsl, 1 + br:2 + br], ALU.mult, [T_rs2, T_grow], [T_rs2])
                            k.ts("dve", tmo[rsl, :], PO[rsl, c0:c0 + 64], rs2[rsl, 0:1], None, ALU.mult, None, [T_PO, T_rs2], [T_tmo])
                            k.tt("dve", ob[rsl, :], ob[rsl, :], tmo[rsl, :], ALU.add, [T_ob, T_tmo], [T_ob])
                    k.cp("act", obb[:], ob[:], [T_ob], [T_obb])
                    pi_ = rp()
                    k.tr(PSb[pi_][0:64, 0:64], obb[:, :], ident_b[0:64, 0:64], [T_obb, T_identb], [PT[pi_]])
                    for h in range(8):
                        kvh = h // 4; g = h % 4; db = (h % 2) * 64
                        k.cp("dve", ybT[db:db + 64, h // 2, TP + s * 4:TP + (s + 1) * 4], PSb[pi_][0:64, kvh * 32 + g * 4:kvh * 32 + g * 4 + 4], [PT[pi_]], [T_yb[4]])
                S.flush()

        with ExitStack() as pe_:
            mixT = sb("mixT", [128, KC, NT], BF16, pe_); T_mix = [[Tl() for _ in TBS] for _ in range(KC)]
            with ExitStack() as pe1:
                wba = sb("wba", [128, 4, D], BF16, pe1); T_wba = Tl()
                wbb = sb("wbb", [128, 4, D], BF16, pe1); T_wbb = Tl()
                wzm = [sb("wzm%d" % i, [128, KC, 2, 128], BF16, pe1) for i in range(2)]; T_wzm = [Tl(), Tl()]
                gsa = [sb("gsa%d" % i, [128, 512], F32, pe1) for i in range(2)]; T_gsa = [Tl(), Tl()]
                gsb = [sb("gsb%d" % i, [128, 512], F32, pe1) for i in range(2)]; T_gsb = [Tl(), Tl()]
                k.ldc(wba[:], wba_d.rearrange("(kc p) n -> p kc n", p=128), [T_wba])
                k.ldc(wbb[:], wbb_d.rearrange("(kc p) n -> p kc n", p=128), [T_wbb])
                wi = win_d.rearrange("(kc p) n -> p kc n", p=128)
                it = 0
                for c in range(KC):
                    wz = wzm[c % 2]; Twz = T_wzm[c % 2]
                    k.ldc(wz[:, :, 0, :], wi[:, :, 2328 + c * 128:2328 + (c + 1) * 128], [Twz])
                    k.ldc(wz[:, :, 1, :], wi[:, :, 3352 + c * 128:3352 + (c + 1) * 128], [Twz])
                    for bi, (t0, n) in enumerate(TBS):
                        ga = gsa[it % 2]; Tga = T_gsa[it % 2]; gb = gsb[it % 2]; Tgb = T_gsb[it % 2]; it += 1
                        psA, TA = nextps()
                        for k4 in range(4):
                            k.mm(psA[:, 0:n], wba[:, k4, c * 128:(c + 1) * 128], uT[:, k4, t0:t0 + n], k4 == 0, k4 == 3, [T_wba, T_u[k4][bi]], [TA])
                        psG, TG = nextps()
                        for kc in range(KC):
                            k.mm(psG[:, 0:n], wz[:, kc, 0, :], hT[:, kc, t0:t0 + n], kc == 0, kc == KC - 1, [Twz, T_h[kc][bi]], [TG])
                        k.act(ga[:, 0:n], psG[:, 0:n], AF.Sigmoid, [TG], [Tga])
                        k.tt("dve", ga[:, 0:n], ga[:, 0:n], psA[:, 0:n], ALU.mult, [Tga, TA], [Tga])
                        psB, TB = nextps()
                        for k4 in range(4):
                            k.mm(psB[:, 0:n], wbb[:, k4, c * 128:(c + 1) * 128], ybT[:, k4, t0:t0 + n], k4 == 0, k4 == 3, [T_wbb, T_yb[bi]], [TB])
                        psH, TH = nextps()
                        for kc in range(KC):
                            k.mm(psH[:, 0:n], wz[:, kc, 1, :], hT[:, kc, t0:t0 + n], kc == 0, kc == KC - 1, [Twz, T_h[kc][bi]], [TH])
                        k.act(gb[:, 0:n], psH[:, 0:n], AF.Sigmoid, [TH], [Tgb])
                        k.tt("dve", gb[:, 0:n], gb[:, 0:n], psB[:, 0:n], ALU.mult, [Tgb, TB], [Tgb])
                        k.tt("pool", mixT[:, c, t0:t0 + n], ga[:, 0:n], gb[:, 0:n], ALU.add, [Tga, Tgb], [T_mix[c][bi]])
                S.flush()
                if STOP.startswith('E1'):
                    S.wait_all_outputs("sp"); S.flush(); S.close(); return nc
            with ExitStack() as pe2:
                wout = sb("wout", [128, KC, D], BF16, pe2); T_wout = Tl()
                xr = [sb("xr%d" % i, [128, 512], F32, pe2) for i in range(2)]; T_xr = [Tl(), Tl()]
                k.ldc(wout[:], wout_d.rearrange("(kc p) n -> p kc n", p=128), [T_wout])
                it = 0
                for c in range(KC):
                    for bi, (t0, n) in enumerate(TBS):
                        x_ = xr[it % 2]; Tx = T_xr[it % 2]; it += 1
                        k.ld(x_[:, 0:n], xT_d[:, c, t0:t0 + n], [Tx])
                        ps, Tp = nextps()
                        for kc in range(KC):
                            k.mm(ps[:, 0:n], wout[:, kc, c * 128:(c + 1) * 128], mixT[:, kc, t0:t0 + n], kc == 0, kc == KC - 1, [T_wout, T_mix[kc][bi]], [Tp])
                        if bi < 4:
                            k.stt(x1T[:, c, t0:t0 + n], ps[:, 0:n], modT[:, 16 + c, 0:1], x_[:, 0:n], ALU.mult, ALU.add, [Tp, Tx, T_mod], [T_x1[c][bi]])
                        else:
                            for s in range(NS):
                                k.stt(x1T[:, c, t0 + s * TS:t0 + (s + 1) * TS], ps[:, s * TS:(s + 1) * TS], modT[:, 16 + c, 1 + s:2 + s],
                                      x_[:, s * TS:(s + 1) * TS], ALU.mult, ALU.add, [Tp, Tx, T_mod], [T_x1[c][bi]])
                S.flush()

        mid_scope.close()
        with ExitStack() as pf:
            sq = [sb("sqF%d" % i, [128, 1040], BF16, pf) for i in range(2)]; T_sq = [Tl(), Tl()]
            rstd = sb("rstdF", [128, NT], F32, pf); T_rstd = [Tl() for _ in TBS]
            tmp = [sb("tmpF0", [128, 1040], F32, pf)] * 2; T_tmp = [Tl()] * 2
            actT = sb("actT", [128, 22, 1040], BF16, pf); T_act = [[Tl() for _ in range(3)] for _ in range(22)]
            wup = [sb("wup%d" % i, [128, KC, 2, 128], BF16, pf) for i in range(2)]; T_wup = [Tl(), Tl()]
            wdn = [sb("wdn%d" % i, [128, 22, 128], BF16, pf) for i in range(2)]; T_wdn = [Tl(), Tl()]
            U = [sb("U%d" % i, [128, 514], F32, pf) for i in range(4)]; T_U = [Tl() for _ in range(4)]
            cv = [sb("cv%d" % i, [128, 512], F32, pf) for i in range(4)]; T_cv = [Tl() for _ in range(4)]
            gl = [sb("gl%d" % i, [128, 512], F32, pf) for i in range(2)]; T_gl = [Tl(), Tl()]
            halo = sb("halo", [128, 44, 2], F32, pf); T_halo = [Tl() for _ in range(44)]
            convo = sb("convo", [128, 44, 10], F32, pf); T_convo = Tl()
            wcv = sb("wcv", [128, 44, 3], F32, pf); bcv = sb("bcv", [128, 44], F32, pf); T_wcv = Tl()
            sprev = sb("sprev", [128, 44, 4, 2], F32, pf); T_sprev = Tl()
            ups = [sb("ups%d" % i, [128, 4, 6], F32, pf) for i in range(2)]; T_ups = [Tl(), Tl()]
            cvs = [sb("cvs%d" % i, [128, 4, 4], F32, pf) for i in range(2)]; T_cvs = [Tl(), Tl()]
            gf = sb("gf", [128, KC], F32, pf); T_gf = Tl()
            yo = [sb("yo0", [128, 1040], F32, pf)] * 2; T_yo = [Tl()] * 2
            k.ld(wcv[:], wcv_d[:, :, :], [T_wcv]); k.ld(bcv[:], bcv_d[:, :], [T_wcv])
            k.ld(sprev[:], sprev_d[:, :, :, :], [T_sprev])
            k.ld(gf[:], gf_d[:, :], [T_gf])
            wupv = wup_d.rearrange("(kc p) n -> p kc n", p=128)
            wdnv = wdn_d.rearrange("(c p) n -> p c n", p=128)
            HALVES = [(0, [(0, 512), (512, 512)]), (1024, [(1024, 512), (1536, 512), (2048, 16)])]

            def rms_stats(src, Tsrc, hs, blocks, nb0):
                ntk = sum(n for _, n in blocks)
                for kc in range(KC):
                    s_ = sq[kc % 2]; Ts = T_sq[kc % 2]
                    k.act(s_[:, 0:ntk], src[:, kc, hs:hs + ntk], AF.Square, Tsrc(kc), [Ts])
                    for bi, (t0, n) in enumerate(blocks):
                        k.mm(PS[bi][:, 0:n], ones_b[:], s_[:, t0 - hs:t0 - hs + n], kc == 0, kc == KC - 1, [T_ones, Ts], [PT[bi]])
                for bi, (t0, n) in enumerate(blocks):
                    k.act(rstd[:, t0:t0 + n], PS[bi][:, 0:n], AF.Sqrt, [PT[bi]], [T_rstd[nb0 + bi]], bias=EPS, scale=1.0 / D)
                    S.op("dve", (lambda o: (lambda e: e.reciprocal(out=o, in_=o)))(rstd[:, t0:t0 + n]), [T_rstd[nb0 + bi]], [T_rstd[nb0 + bi]])

            uidx = [0]
            for hi, (hs, blocks) in enumerate(HALVES):
                nb0 = 0 if hi == 0 else 2
                ntk = sum(n for _, n in blocks)
                rms_stats(x1T, lambda kc: [T_x1[kc][nb0 + b] for b in range(len(blocks))], hs, blocks, nb0)
                for kc in range(KC):
                    t_ = tmp[kc % 2]; Tt = T_tmp[kc % 2]
                    Tx = [T_x1[kc][nb0 + b] for b in range(len(blocks))]
                    Th = [T_h[kc][nb0 + b] for b in range(len(blocks))]
                    k.tt("dve", t_[:, 0:ntk], x1T[:, kc, hs:hs + ntk], rstd[:, hs:hs + ntk], ALU.mult, Tx + T_rstd[nb0:nb0 + len(blocks)], [Tt])
                    npr = ntk if hi == 0 else 1024
                    k.act(hT[:, kc, hs:hs + npr], t_[:, 0:npr], AF.Identity, [Tt, T_A2, T_mod], Th, bias=modT[:, 24 + kc, 0:1], scale=A2[:, kc, 0:1])
                    if hi == 1:
                        for s in range(NS):
                            c0 = 1024 + s * TS
                            k.act(hT[:, kc, TP + s * TS:TP + (s + 1) * TS], t_[:, c0:c0 + TS], AF.Identity, [Tt, T_A2, T_mod], Th,
                                  bias=modT[:, 24 + kc, 1 + s:2 + s], scale=A2[:, kc, 1 + s:2 + s])
                for cp_ in range(22):
                    wu_ = wup[cp_ % 2]; Twu = T_wup[cp_ % 2]
                    k.ldc(wu_[:, :, 0, :], wupv[:, :, cp_ * 128:(cp_ + 1) * 128], [Twu])
                    k.ldc(wu_[:, :, 1, :], wupv[:, :, DFF + cp_ * 128:DFF + (cp_ + 1) * 128], [Twu])
                    for cc in range(1):
                        c = cp_
                        for bi, (t0, n) in enumerate(blocks):
                            gbi = nb0 + bi
                            res = []
                            for ag in range(2):
                                idx = c + 22 * ag
                                ps, Tp = nextps()
                                for kc in range(KC):
                                    k.mm(ps[:, 0:n], wu_[:, kc, ag, cc * 128:(cc + 1) * 128], hT[:, kc, t0:t0 + n], kc == 0, kc == KC - 1,
                                         [Twu, T_h[kc][gbi]], [Tp])
                                w0 = wcv[:, idx, 0:1]; w1 = wcv[:, idx, 1:2]; w2 = wcv[:, idx, 2:3]; bb = bcv[:, idx:idx + 1]
                                if n == 512:
                                    ui = uidx[0] % 4; uidx[0] += 1
                                    U_ = U[ui]; TU = T_U[ui]; cv_ = cv[ui]; Tcv = T_cv[ui]
                                    if t0 == 0:
                                        k.memset("pool", U_[:, 0:2], 0.0, [TU])
                                    else:
                                        k.cp("pool", U_[:, 0:2], halo[:, idx, :], [T_halo[idx]], [TU])
                                    k.cp("act", U_[:, 2:514], ps[:, 0:512], [Tp], [TU])
                                    k.cp("pool", halo[:, idx, :], U_[:, 512:514], [TU], [T_halo[idx]])
                                    if t0 == 1536:
                                        k.cp("pool", convo[:, idx, 0:2], U_[:, 512:514], [TU], [T_convo])
                                    k.ts("dve", cv_[:], U_[:, 2:514], w2, bb, ALU.mult, ALU.add, [TU, T_wcv], [Tcv])
                                    k.stt(cv_[:], U_[:, 1:513], w1, cv_[:], ALU.mult, ALU.add, [TU, T_wcv, Tcv], [Tcv])
                                    k.stt(cv_[:], U_[:, 0:512], w0, cv_[:], ALU.mult, ALU.add, [TU, T_wcv, Tcv], [Tcv])
                                    res.append((cv_[:], Tcv))
                                else:
                                    u_ = ups[ag]; Tu_ = T_ups[ag]; c_ = cvs[ag]; Tc_ = T_cvs[ag]
                                    k.cp("pool", u_[:, :, 0:2], sprev[:, idx, :, :], [T_sprev], [Tu_])
                                    k.cp("act", u_[:, :, 2:6], ps[:, 0:16].rearrange("p (s t) -> p s t", s=4), [Tp], [Tu_])
                                    k.cp("pool", convo[:, idx, 2:10].rearrange("p (s r) -> p s r", s=4), u_[:, :, 4:6], [Tu_], [T_convo])
                                    k.ts("dve", c_[:], u_[:, :, 2:6], w2, bb, ALU.mult, ALU.add, [Tu_, T_wcv], [Tc_])
                                    k.stt(c_[:], u_[:, :, 1:5], w1, c_[:], ALU.mult, ALU.add, [Tu_, T_wcv, Tc_], [Tc_])
                                    k.stt(c_[:], u_[:, :, 0:4], w0, c_[:], ALU.mult, ALU.add, [Tu_, T_wcv, Tc_], [Tc_])
                                    res.append((c_[:].rearrange("p s t -> p (s t)"), Tc_))
                            (ca, Tca), (cg, Tcg) = res
                            g_ = gl[(c + bi) % 2]; Tg_ = T_gl[(c + bi) % 2]
                            k.act(g_[:, 0:n], ca, AF.Gelu_apprx_tanh, [Tca], [Tg_])
                            k.tt("dve", actT[:, c, t0 - hs:t0 - hs + n], g_[:, 0:n], cg, ALU.mult, [Tg_, Tcg], [T_act[c][bi]])
                for m in range(KC):
                    wd_ = wdn[m % 2]; Twd = T_wdn[m % 2]
                    k.ldc(wd_[:], wdnv[:, :, m * 128:(m + 1) * 128], [Twd])
                    for bi, (t0, n) in enumerate(blocks):
                        gbi = nb0 + bi
                        ps, Tp = nextps()
                        for c in range(22):
                            k.mm(ps[:, 0:n], wd_[:, c, :], actT[:, c, t0 - hs:t0 - hs + n], c == 0, c == 21, [Twd, T_act[c][bi]], [Tp])
                        if n == 512:
                            k.stt(x1T[:, m, t0:t0 + n], ps[:, 0:n], modT[:, 40 + m, 0:1], x1T[:, m, t0:t0 + n], ALU.mult, ALU.add,
                                  [Tp, T_mod, T_x1[m][gbi]], [T_x1[m][gbi]])
                        else:
                            for s in range(NS):
                                xs_ = x1T[:, m, t0 + s * TS:t0 + (s + 1) * TS]
                                k.stt(xs_, ps[:, s * TS:(s + 1) * TS], modT[:, 40 + m, 1 + s:2 + s], xs_, ALU.mult, ALU.add,
                                      [Tp, T_mod, T_x1[m][gbi]], [T_x1[m][gbi]])
                rms_stats(x1T, lambda kc: [T_x1[kc][nb0 + b] for b in range(len(blocks))], hs, blocks, nb0)
                for m in range(KC):
                    t_ = tmp[m % 2]; Tt = T_tmp[m % 2]
                    y_ = yo[m % 2]; Ty = T_yo[m % 2]
                    Tx = [T_x1[m][nb0 + b] for b in range(len(blocks))]
                    k.tt("dve", t_[:, 0:ntk], x1T[:, m, hs:hs + ntk], rstd[:, hs:hs + ntk], ALU.mult, Tx + T_rstd[nb0:nb0 + len(blocks)], [Tt])
                    k.act(y_[:, 0:ntk], t_[:, 0:ntk], AF.Copy, [Tt, T_gf], [Ty], scale=gf[:, m:m + 1])
                    k.st(yT_o[:, m, hs:hs + ntk], y_[:, 0:ntk], [Ty])
            k.st(conv_o[:, :], convo[:].rearrange("p i r -> p (i r)"), [T_convo])
            S.wait_all_outputs("sp")
            S.flush()
    S.close()
    return nc


_NC_CACHE = {}


def _prep_inputs(inp):
    f = lambda a: np.ascontiguousarray(a, dtype=np.float32)
    xp = np.asarray(inp["x_prompt"]); xs = np.asarray(inp["x_sample"])
    cp_ = np.asarray(inp["c_prompt"]); cs_ = np.asarray(inp["c_sample"])

    def fm(vec):
        return f(np.asarray(vec).reshape(KC, 128).T)

    shared = {
        "w_ada": f(np.asarray(inp["w_ada"])[0]),
        "b_adaT": f(np.asarray(inp["b_ada"])[0].reshape(48, 128).T),
        "g1T": fm(inp["g_norm1"][0]), "g2T": fm(inp["g_norm2"][0]), "gfT": fm(inp["g_final"]),
        "w_in": f(np.asarray(inp["w_in"])[0]),
        "ln_g_bc": f(np.broadcast_to(np.asarray(inp["ln_v_g"])[0][None, :], (128, 512))),
        "ln_b_bc": f(np.broadcast_to(np.asarray(inp["ln_v_b"])[0][None, :], (128, 512))),
        "ident": np.eye(128, dtype=np.float32),
    }
    ws = np.asarray(inp["w_spatial"])[0]
    shared["wsT"] = f(ws.transpose(2, 0, 1))
    shared["wssT"] = f(ws[:, :4, :4].transpose(2, 0, 1))
    shared["bs"] = f(np.asarray(inp["b_spatial"])[0][None])
    b1 = np.asarray(inp["cmp_b1"])[0]; b2 = np.asarray(inp["cmp_b2"])[0]
    shared["b1r"] = f(np.concatenate([b1.T, b1.T], axis=0))
    shared["b2r"] = f(np.concatenate([b2.T, b2.T], axis=0))
    shared["cmp_w1"] = f(np.asarray(inp["cmp_w1"])[0])
    shared["cmp_w2"] = f(np.asarray(inp["cmp_w2"])[0])
    shared["w_branch_a"] = f(np.asarray(inp["w_branch_a"])[0])
    shared["w_branch_b"] = f(np.asarray(inp["w_branch_b"])[0])
    shared["w_out"] = f(np.asarray(inp["w_out"])[0])
    shared["w_up"] = f(np.asarray(inp["w_up"])[0])
    shared["w_down"] = f(np.asarray(inp["w_down"])[0])
    shared["w_convT"] = f(np.asarray(inp["w_conv"])[0].reshape(3, 44, 128).transpose(2, 1, 0))
    shared["b_convT"] = f(np.asarray(inp["b_conv"])[0].reshape(44, 128).T)
    t = np.arange(2048)[:, None]; n = np.arange(32)[None, :]
    avail = (n + 1) * 64 <= t + 1
    cur = t // 64
    forced = (n == 0) | (n == cur) | (n == cur - 1)
    future = n > cur
    mk = np.stack([np.where(avail, 0.0, -1e30), avail.astype(np.float32), (~(forced | future)).astype(np.float32),
                   np.where(forced, 1e4, np.where(future, -1.0, 0.0))], axis=0)
    shared["msk"] = f(mk.reshape(4, 16, 128, 32).transpose(2, 0, 1, 3))
    key = np.arange(2048)[None, :]; r = np.arange(64)[:, None]
    shared["Emat"] = f((key // 64 == (r % 32)).astype(np.float32))
    b_ = np.arange(128)[:, None]; a_ = np.arange(128)[None, :]
    shared["Cm"] = f(np.stack([np.where(a_ >= b_, 0.0, NEGB), np.where(a_ < b_, 0.0, NEGB)], axis=1))
    sconv = np.asarray(inp["state_ffn_conv"])[0]
    if not STOP:
        shared["cache2d"] = np.asarray(inp["cache_kv"], dtype=np.float32).reshape(5120 * 128, 512)
    shared["b2k"] = f(np.concatenate([b2[0], b2[0]])[:, None])
    shared["b2v"] = f(np.broadcast_to(np.concatenate([b2[1], b2[1]])[None, :], (128, 128)))
    rr = np.arange(64); kvh_r = rr // 32; sl_r = rr % 32; g_r = sl_r // 4; tok_r = sl_r % 4; used = sl_r < 16
    shared["Gm"] = f(((kvh_r[:, None] == kvh_r[None, :]) & (tok_r[:, None] == tok_r[None, :]) & used[:, None]).astype(np.float32))
    shared["SelT"] = f((np.arange(4)[:, None] == tok_r[None, :]).astype(np.float32))
    shared["Hsel"] = f((np.arange(8)[None, :] == (4 * kvh_r + np.minimum(g_r, 3))[:, None]).astype(np.float32))
    shared["CN"] = f(np.where(np.arange(4)[None, :] <= tok_r[:, None], 0.0, NEGB))
    shared["CW"] = f(np.where(np.arange(4)[None, :] > tok_r[:, None], 0.0, NEGB))
    ptab = np.asarray(inp["page_table"]).astype(np.int32)
    pe = np.asarray(inp["cmp_pe"])[0]
    pe_bc = np.broadcast_to(pe.transpose(1, 0, 2)[None, :, :, None, :], (2, 64, 2, 2, 64)).reshape(128, 2, 2, 64)
    shared["pe_bc"] = f(pe_bc)
    maps = []
    swin = np.asarray(inp["state_kv_win"])[0].reshape(32, 512, 256)
    for c in range(NCORES):
        xall = np.concatenate([xp[c], xs[4 * c:4 * c + 4].reshape(16, D)], axis=0)
        xT = f(xall.T.reshape(KC, 128, NT).transpose(1, 0, 2))
        call = np.concatenate([cp_[c:c + 1], cs_[4 * c:4 * c + 4]], axis=0)
        cT = f(call.T.reshape(KC, 128, 5).transpose(1, 0, 2))
        m = dict(shared)
        m["xT"] = xT
        m["cT"] = cT
        m["state_win"] = f(swin[4 * c:4 * c + 4])
        m["pt_bc"] = np.ascontiguousarray(np.broadcast_to(ptab[4 * c:4 * c + 4][:, None, :], (4, 128, 128)), dtype=np.int32)
        m["sprevT"] = f(sconv[4 * c:4 * c + 4].reshape(4, 2, 44, 128).transpose(3, 2, 0, 1))
        maps.append(m)
    return maps


def kernel(**inp):
    if "nc" not in _NC_CACHE:
        _NC_CACHE["nc"] = build_program()
    nc = _NC_CACHE["nc"]
    maps = _prep_inputs(inp)
    res = run_bass_kernel_spmd(nc, maps, core_ids=list(range(NCORES)))
    R = res.results
    y_prompt = np.zeros((8, 2048, 1024), np.float32)
    y_sample = np.zeros((32, 4, 1024), np.float32)
    kv_prompt = np.zeros((1, 8, 2048, 4, 2, 64), np.float32)
    kv_sample = np.zeros((1, 32, 4, 4, 2, 64), np.float32)
    win_prompt = np.zeros((1, 8, 512, 2, 2, 64), np.float32)
    win_sample = np.zeros((1, 32, 512, 2, 2, 64), np.float32)
    v_chunk = np.zeros((1, 32, 4, 512), np.float32)
    conv_prompt = np.zeros((1, 8, 2, 5632), np.float32)
    conv_sample = np.zeros((1, 32, 2, 5632), np.float32)
    for c in range(NCORES):
        r = R[c]
        kv = r["kv_tok"]
        kv_prompt[0, c] = kv[:TP].reshape(2048, 4, 2, 64)
        kv_sample[0, 4 * c:4 * c + 4] = kv[TP:].reshape(4, 4, 4, 2, 64)
        win_prompt[0, c] = r["win_p"].reshape(512, 2, 2, 64)
        win_sample[0, 4 * c:4 * c + 4] = r["win_s"].reshape(4, 512, 2, 2, 64)
        v_chunk[0, 4 * c:4 * c + 4] = r["vchunk"].reshape(4, 4, 512)
        yT = r["yT"].transpose(2, 1, 0).reshape(NT, D)
        y_prompt[c] = yT[:TP]
        y_sample[4 * c:4 * c + 4] = yT[TP:].reshape(4, 4, D)
        cv = r["convT"].reshape(128, 44, 10)
        conv_prompt[0, c] = cv[:, :, 0:2].transpose(2, 1, 0).reshape(2, 5632)
        conv_sample[0, 4 * c:4 * c + 4] = cv[:, :, 2:10].reshape(128, 44, 4, 2).transpose(2, 3, 1, 0).reshape(4, 2, 5632)
    return (y_prompt, y_sample, kv_prompt, kv_sample, win_prompt, win_sample, v_chunk, conv_prompt, conv_sample)
```
